# Optimizing a Trainium2 kernel written in Bass

```python
import jax, jax.numpy as jnp
from jax import lax
import numpy as np

D_MODEL = 2048
BATCH = 8
SEQ = 2048
DEPTH = 2

HEAD_DIM = 128
BLOCK = 128
ROPE_THETA = 10000.0
RMS_EPS = 1e-6
NEG_INF = -1e30

LRU_WIDTH = D_MODEL // 2
LRU_BLOCKS = 8
LRU_BLOCK_DIM = LRU_WIDTH // LRU_BLOCKS
LRU_CONV = 4
LRU_C = 8.0

DIL_CONFIGS = ((128, 1), (512, 4), (2048, 16))
DIL_GROUPS = 3
DIL_HEADS = 4
DIL_WIDTH = DIL_GROUPS * DIL_HEADS * HEAD_DIM
DIL_OUT = DIL_HEADS * HEAD_DIM

SB_HEADS = (D_MODEL // 2) // HEAD_DIM
SB_WIDTH = SB_HEADS * HEAD_DIM

RWKV_HEAD = 64
RWKV_WIDTH = D_MODEL // 2
RWKV_HEADS = RWKV_WIDTH // RWKV_HEAD
RWKV_W_LORA = 64
RWKV_A_LORA = 64
RWKV_G_LORA = 160
RWKV_V_LORA = 32
RWKV_GN_EPS = 64e-5
RWKV_IN = 3 * RWKV_WIDTH + RWKV_W_LORA + RWKV_A_LORA + RWKV_G_LORA

A_IN = 2 * LRU_WIDTH
B_IN = 3 * DIL_WIDTH
C_IN = 3 * SB_WIDTH
OFF_B = A_IN
OFF_C = OFF_B + B_IN
OFF_D = OFF_C + C_IN
N_IN = OFF_D + RWKV_IN
N_BRANCH = 4

D_FF = (11 * D_MODEL) // 4
FFN_CONV = 3
PLE_DIM = 256

kernel_name = 'hybrid_gated_parallel_mixers'


def rms_norm(x, g):
    xf = x.astype(jnp.float32)
    y = xf * lax.rsqrt(jnp.mean(xf * xf, axis=-1, keepdims=True) + RMS_EPS)
    return (y * g.astype(jnp.float32)).astype(x.dtype)


def causal_dwconv(x, w, b):
    width, channels = w.shape
    y = lax.conv_general_dilated(x, w[:, None, :], window_strides=(1,), padding=[(width - 1, 0)],
                                 dimension_numbers=('NWC', 'WIO', 'NWC'), feature_group_count=channels)
    return y + b


def token_shift(x):
    return jnp.pad(x, ((0, 0), (1, 0), (0, 0)))[:, :-1]


def rope(x, pos):
    half = x.shape[-1] // 2
    inv_freq = ROPE_THETA ** (-jnp.arange(half, dtype=jnp.float32) / half)
    ang = pos.astype(jnp.float32)[:, None] * inv_freq[None, :]
    cos = jnp.cos(ang)[None, :, None, :]
    sin = jnp.sin(ang)[None, :, None, :]
    xf = x.astype(jnp.float32)
    x1, x2 = xf[..., :half], xf[..., half:]
    return jnp.concatenate([x1 * cos - x2 * sin, x2 * cos + x1 * sin], axis=-1).astype(x.dtype)


def rglru_mixer(x_in, gate_in, conv_w, conv_b, w_r, b_r, w_i, b_i, lam):
    B, S, W = x_in.shape
    u = causal_dwconv(x_in, conv_w, conv_b)
    ub = u.reshape(B, S, LRU_BLOCKS, LRU_BLOCK_DIM)
    r = jax.nn.sigmoid((jnp.einsum('bshi,hij->bshj', ub, w_r).reshape(B, S, W) + b_r).astype(jnp.float32))
    ig = jax.nn.sigmoid((jnp.einsum('bshi,hij->bshj', ub, w_i).reshape(B, S, W) + b_i).astype(jnp.float32))
    log_a = -LRU_C * r * jax.nn.softplus(-lam.astype(jnp.float32))
    a = jnp.exp(log_a)
    inp = jnp.sqrt(1.0 - jnp.exp(2.0 * log_a)) * ig * u.astype(jnp.float32)

    def combine(left, right):
        a1, h1 = left
        a2, h2 = right
        return a1 * a2, a2 * h1 + h2

    _, h = lax.associative_scan(combine, (a, inp), axis=1)
    return (h * jax.nn.gelu(gate_in.astype(jnp.float32))).astype(x_in.dtype)


def dilated_window_attention(q, k, v, dilation, span):
    B, S, H, Dh = q.shape
    L = S // dilation
    nb = -(-L // BLOCK)
    pad = nb * BLOCK - L

    def to_blocks(t):
        t = t.reshape(B, L, dilation, H, Dh).transpose(0, 2, 3, 1, 4)
        t = jnp.pad(t, ((0, 0), (0, 0), (0, 0), (0, pad), (0, 0)))
        return t.reshape(B, dilation, H, nb, BLOCK, Dh)

    def with_prev(t):
        prev = jnp.pad(t, ((0, 0), (0, 0), (0, 0), (1, 0), (0, 0), (0, 0)))[:, :, :, :-1]
        return jnp.concatenate([prev, t], axis=4)

    qb = to_blocks(q)
    kw = with_prev(to_blocks(k))
    vw = with_prev(to_blocks(v)).astype(jnp.float32)
    s = jnp.einsum('bdhnqe,bdhnke->bdhnqk', qb, kw, preferred_element_type=jnp.float32) * (Dh ** -0.5)
    qi = jnp.arange(BLOCK)[:, None]
    kj = jnp.arange(2 * BLOCK)[None, :]
    dist = qi + BLOCK - kj
    key_l = (jnp.arange(nb) * BLOCK)[:, None, None] - BLOCK + kj[None]
    mask = (dist >= 0)[None] & (dist <= span)[None] & (key_l >= 0)
    s = jnp.where(mask, s, NEG_INF)
    m = jnp.max(s, axis=-1)
    e = jnp.exp(s - m[..., None])
    den = jnp.sum(e, axis=-1)
    o = jnp.einsum('bdhnqk,bdhnke->bdhnqe', e, vw) / den[..., None]
    lse = m + jnp.log(den)

    def from_blocks(t):
        t = t.reshape((B, dilation, H, nb * BLOCK) + t.shape[5:])[:, :, :, :L]
        t = jnp.moveaxis(t, 3, 1)
        return t.reshape((B, S, H) + t.shape[4:])

    return from_blocks(o), from_blocks(lse)


def dilated_mixer(seg, pos):
    B, S, _ = seg.shape
    qkv = seg.reshape(B, S, 3, DIL_GROUPS * DIL_HEADS, HEAD_DIM)
    q = rope(qkv[:, :, 0], pos).reshape(B, S, DIL_GROUPS, DIL_HEADS, HEAD_DIM)
    k = rope(qkv[:, :, 1], pos).reshape(B, S, DIL_GROUPS, DIL_HEADS, HEAD_DIM)
    v = qkv[:, :, 2].reshape(B, S, DIL_GROUPS, DIL_HEADS, HEAD_DIM)
    outs, lses = [], []
    for g, (window, dilation) in enumerate(DIL_CONFIGS):
        o_g, lse_g = dilated_window_attention(q[:, :, g], k[:, :, g], v[:, :, g], dilation, window // dilation)
        outs.append(o_g)
        lses.append(lse_g)
    wts = jax.nn.softmax(jnp.stack(lses, axis=0), axis=0)
    o = jnp.einsum('gbsh,gbshe->bshe', wts, jnp.stack(outs, axis=0))
    return o.reshape(B, S, DIL_OUT).astype(seg.dtype)


def stick_breaking_mixer(seg):
    B, S, _ = seg.shape
    qkv = seg.reshape(B, S, 3, SB_HEADS, HEAD_DIM)
    q, k, v = (jnp.moveaxis(qkv[:, :, j], 2, 1) for j in range(3))
    vf = v.astype(jnp.float32)
    nb = S // BLOCK
    qb = jnp.moveaxis(q.reshape(B, SB_HEADS, nb, BLOCK, HEAD_DIM), 2, 0)
    key_pos = jnp.arange(S)
    scale = HEAD_DIM ** -0.5

    def one_block(args):
        q_blk, n = args
        z = jnp.einsum('bhqe,bhke->bhqk', q_blk, k, preferred_element_type=jnp.float32) * scale
        q_pos = n * BLOCK + jnp.arange(BLOCK)
        before = key_pos[None, :] < q_pos[:, None]
        log_1m = jnp.where(before, jax.nn.log_sigmoid(-z), 0.0)
        suffix = lax.cumsum(log_1m, axis=3, reverse=True) - log_1m
        att = jnp.where(before, jnp.exp(jax.nn.log_sigmoid(z) + suffix), 0.0)
        return jnp.einsum('bhqk,bhke->bhqe', att, vf)

    o = lax.map(one_block, (qb, jnp.arange(nb)))
    o = jnp.moveaxis(o, 0, 2).reshape(B, SB_HEADS, S, HEAD_DIM)
    return jnp.moveaxis(o, 1, 2).reshape(B, S, SB_WIDTH).astype(seg.dtype)


def rwkv7_mixer(seg, mu, w0, w_up, a0, a_up, g_up, k_k, k_a, r_k, gn_w, gn_b, v_first, v_res):
    B, S, _ = seg.shape
    W = RWKV_WIDTH
    seg = seg + (token_shift(seg) - seg) * mu
    r = seg[..., :W]
    k = seg[..., W:2 * W]
    v = seg[..., 2 * W:3 * W]
    o1 = 3 * W
    o2 = o1 + RWKV_W_LORA
    o3 = o2 + RWKV_A_LORA
    w_low, a_low, g_low = seg[..., o1:o2], seg[..., o2:o3], seg[..., o3:]
    w = -jax.nn.softplus(-(w0 + jnp.tanh(w_low) @ w_up).astype(jnp.float32)) - 0.5
    decay = jnp.exp(-jnp.exp(w))
    a = jax.nn.sigmoid((a0 + a_low @ a_up).astype(jnp.float32))
    g = jax.nn.sigmoid(g_low) @ g_up
    if v_res is None:
        v_first = v
    else:
        v0, v_down, v_up = v_res
        v = v + (v_first - v) * jax.nn.sigmoid(v0 + (v @ v_down) @ v_up)

    def heads(t):
        return t.astype(jnp.float32).reshape(B, S, RWKV_HEADS, RWKV_HEAD)

    r, k, vh, decay, a = (heads(t) for t in (r, k, v, decay, a))
    kk = k * k_k.astype(jnp.float32).reshape(RWKV_HEADS, RWKV_HEAD)
    kk = kk / jnp.maximum(jnp.linalg.norm(kk, axis=-1, keepdims=True), 1e-12)
    k = k * (1.0 + (a - 1.0) * k_a.astype(jnp.float32).reshape(RWKV_HEADS, RWKV_HEAD))
    xs = tuple(jnp.moveaxis(t, 1, 0) for t in (r, decay, k, vh, kk, kk * a))

    def step(state, inp):
        r_t, w_t, k_t, v_t, kk_t, b_t = inp
        sa = jnp.einsum('bhij,bhj->bhi', state, kk_t)
        state = state * w_t[:, :, None, :] - sa[..., None] * b_t[:, :, None, :] + v_t[..., None] * k_t[:, :, None, :]
        return state, jnp.einsum('bhij,bhj->bhi', state, r_t)

    state0 = jnp.zeros((B, RWKV_HEADS, RWKV_HEAD, RWKV_HEAD), jnp.float32)
    _, y = lax.scan(step, state0, xs)
    y = jnp.moveaxis(y, 0, 1)
    mean = jnp.mean(y, axis=-1, keepdims=True)
    var = jnp.mean(jnp.square(y - mean), axis=-1, keepdims=True)
    y = ((y - mean) * lax.rsqrt(var + RWKV_GN_EPS)).reshape(B, S, W) * gn_w + gn_b
    bonus = jnp.sum(r * k * r_k.astype(jnp.float32), axis=-1, keepdims=True) * vh
    y = (y + bonus.reshape(B, S, W)) * g.astype(jnp.float32)
    return y.astype(seg.dtype), v_first


def conv_ffn(h, w_up, conv_w, conv_b, w_down):
    u = causal_dwconv(h @ w_up, conv_w, conv_b)
    gate, up = jnp.split(u, 2, axis=-1)
    return (jax.nn.gelu(gate) * up) @ w_down


def setup_inputs(seed: int = 0) -> dict:
    key = jax.random.key(seed)
    ks = iter(jax.random.split(key, 48))

    def nrm(shape, scale):
        return scale * jax.random.normal(next(ks), shape, jnp.float32)

    def gain():
        return 1.0 + nrm((DEPTH, D_MODEL), 0.05)

    x = nrm((BATCH, SEQ, D_MODEL), 1.0)
    p = nrm((DEPTH, BATCH, SEQ, PLE_DIM), 1.0)
    norm_mix_pre = gain()
    norm_mix_post = gain()
    norm_ffn_pre = gain()
    norm_ffn_post = gain()
    norm_ple_pre = gain()
    norm_ple_post = gain()
    w_in = nrm((DEPTH, D_MODEL, N_IN), D_MODEL ** -0.5)
    w_merge_gate = nrm((DEPTH, D_MODEL, N_BRANCH * D_MODEL), D_MODEL ** -0.5)
    lru_conv_w = nrm((DEPTH, LRU_CONV, LRU_WIDTH), LRU_CONV ** -0.5)
    lru_conv_b = nrm((DEPTH, LRU_WIDTH), 0.02)
    lru_w_r = nrm((DEPTH, LRU_BLOCKS, LRU_BLOCK_DIM, LRU_BLOCK_DIM), LRU_BLOCK_DIM ** -0.5)
    lru_b_r = nrm((DEPTH, LRU_WIDTH), 0.02)
    lru_w_i = nrm((DEPTH, LRU_BLOCKS, LRU_BLOCK_DIM, LRU_BLOCK_DIM), LRU_BLOCK_DIM ** -0.5)
    lru_b_i = nrm((DEPTH, LRU_WIDTH), 0.02)
    a_init = jax.random.uniform(next(ks), (DEPTH, LRU_WIDTH), jnp.float32, minval=0.9, maxval=0.999)
    lru_lambda = jnp.log(a_init) - jnp.log1p(-a_init)
    rwkv_mu = jax.random.uniform(next(ks), (DEPTH, RWKV_IN), jnp.float32)
    rwkv_w0 = nrm((DEPTH, RWKV_WIDTH), 0.5)
    rwkv_w_up = nrm((DEPTH, RWKV_W_LORA, RWKV_WIDTH), 0.1)
    rwkv_a0 = nrm((DEPTH, RWKV_WIDTH), 0.1)
    rwkv_a_up = nrm((DEPTH, RWKV_A_LORA, RWKV_WIDTH), 0.1)
    rwkv_g_up = nrm((DEPTH, RWKV_G_LORA, RWKV_WIDTH), RWKV_G_LORA ** -0.5)
    rwkv_k_k = 0.85 + nrm((DEPTH, RWKV_WIDTH), 0.05)
    rwkv_k_a = 1.0 + nrm((DEPTH, RWKV_WIDTH), 0.05)
    rwkv_r_k = nrm((DEPTH, RWKV_HEADS, RWKV_HEAD), 0.1)
    rwkv_gn_w = 1.0 + nrm((DEPTH, RWKV_WIDTH), 0.05)
    rwkv_gn_b = nrm((DEPTH, RWKV_WIDTH), 0.02)
    rwkv_v0 = nrm((DEPTH - 1, RWKV_WIDTH), 0.1)
    rwkv_v_down = nrm((DEPTH - 1, RWKV_WIDTH, RWKV_V_LORA), RWKV_WIDTH ** -0.5)
    rwkv_v_up = nrm((DEPTH - 1, RWKV_V_LORA, RWKV_WIDTH), RWKV_V_LORA ** -0.5)
    w_branch_a = nrm((DEPTH, LRU_WIDTH, D_MODEL), LRU_WIDTH ** -0.5)
    w_branch_b = nrm((DEPTH, DIL_OUT, D_MODEL), DIL_OUT ** -0.5)
    w_branch_c = nrm((DEPTH, SB_WIDTH, D_MODEL), SB_WIDTH ** -0.5)
    w_branch_d = nrm((DEPTH, RWKV_WIDTH, D_MODEL), RWKV_WIDTH ** -0.5)
    w_out = nrm((DEPTH, D_MODEL, D_MODEL), D_MODEL ** -0.5)
    w_ffn_up = nrm((DEPTH, D_MODEL, 2 * D_FF), D_MODEL ** -0.5)
    ffn_conv_w = nrm((DEPTH, FFN_CONV, 2 * D_FF), FFN_CONV ** -0.5)
    ffn_conv_b = nrm((DEPTH, 2 * D_FF), 0.02)
    w_ffn_down = nrm((DEPTH, D_FF, D_MODEL), D_FF ** -0.5)
    w_ple = nrm((DEPTH, PLE_DIM, D_MODEL), PLE_DIM ** -0.5)
    w_ple_gate = nrm((DEPTH, D_MODEL, D_MODEL), D_MODEL ** -0.5)
    return {
        'x': x, 'p': p,
        'norm_mix_pre': norm_mix_pre, 'norm_mix_post': norm_mix_post,
        'norm_ffn_pre': norm_ffn_pre, 'norm_ffn_post': norm_ffn_post,
        'norm_ple_pre': norm_ple_pre, 'norm_ple_post': norm_ple_post,
        'w_in': w_in, 'w_merge_gate': w_merge_gate,
        'lru_conv_w': lru_conv_w, 'lru_conv_b': lru_conv_b,
        'lru_w_r': lru_w_r, 'lru_b_r': lru_b_r, 'lru_w_i': lru_w_i, 'lru_b_i': lru_b_i,
        'lru_lambda': lru_lambda,
        'rwkv_mu': rwkv_mu, 'rwkv_w0': rwkv_w0, 'rwkv_w_up': rwkv_w_up,
        'rwkv_a0': rwkv_a0, 'rwkv_a_up': rwkv_a_up, 'rwkv_g_up': rwkv_g_up,
        'rwkv_k_k': rwkv_k_k, 'rwkv_k_a': rwkv_k_a, 'rwkv_r_k': rwkv_r_k,
        'rwkv_gn_w': rwkv_gn_w, 'rwkv_gn_b': rwkv_gn_b,
        'rwkv_v0': rwkv_v0, 'rwkv_v_down': rwkv_v_down, 'rwkv_v_up': rwkv_v_up,
        'w_branch_a': w_branch_a, 'w_branch_b': w_branch_b,
        'w_branch_c': w_branch_c, 'w_branch_d': w_branch_d,
        'w_out': w_out,
        'w_ffn_up': w_ffn_up, 'ffn_conv_w': ffn_conv_w, 'ffn_conv_b': ffn_conv_b, 'w_ffn_down': w_ffn_down,
        'w_ple': w_ple, 'w_ple_gate': w_ple_gate,
    }


def reference(x, p, norm_mix_pre, norm_mix_post, norm_ffn_pre, norm_ffn_post, norm_ple_pre, norm_ple_post,
              w_in, w_merge_gate, lru_conv_w, lru_conv_b, lru_w_r, lru_b_r, lru_w_i, lru_b_i, lru_lambda,
              rwkv_mu, rwkv_w0, rwkv_w_up, rwkv_a0, rwkv_a_up, rwkv_g_up, rwkv_k_k, rwkv_k_a, rwkv_r_k,
              rwkv_gn_w, rwkv_gn_b, rwkv_v0, rwkv_v_down, rwkv_v_up,
              w_branch_a, w_branch_b, w_branch_c, w_branch_d, w_out,
              w_ffn_up, ffn_conv_w, ffn_conv_b, w_ffn_down, w_ple, w_ple_gate):
    B, S, _ = x.shape
    pos = jnp.arange(S)
    v_first = None
    for i in range(DEPTH):
        h = rms_norm(x, norm_mix_pre[i])
        proj = h @ w_in[i]
        y_a = rglru_mixer(proj[..., :LRU_WIDTH], proj[..., LRU_WIDTH:OFF_B], lru_conv_w[i], lru_conv_b[i],
                          lru_w_r[i], lru_b_r[i], lru_w_i[i], lru_b_i[i], lru_lambda[i])
        y_b = dilated_mixer(proj[..., OFF_B:OFF_C], pos)
        y_c = stick_breaking_mixer(proj[..., OFF_C:OFF_D])
        v_res = (rwkv_v0[i - 1], rwkv_v_down[i - 1], rwkv_v_up[i - 1]) if i > 0 else None
        y_d, v_first = rwkv7_mixer(proj[..., OFF_D:], rwkv_mu[i], rwkv_w0[i], rwkv_w_up[i], rwkv_a0[i],
                                   rwkv_a_up[i], rwkv_g_up[i], rwkv_k_k[i], rwkv_k_a[i], rwkv_r_k[i],
                                   rwkv_gn_w[i], rwkv_gn_b[i], v_first, v_res)
        gates = jax.nn.sigmoid(h @ w_merge_gate[i]).reshape(B, S, N_BRANCH, D_MODEL)
        merged = (gates[:, :, 0] * (y_a @ w_branch_a[i]) + gates[:, :, 1] * (y_b @ w_branch_b[i])
                  + gates[:, :, 2] * (y_c @ w_branch_c[i]) + gates[:, :, 3] * (y_d @ w_branch_d[i]))
        x = x + rms_norm(merged @ w_out[i], norm_mix_post[i])
        h = rms_norm(x, norm_ffn_pre[i])
        x = x + rms_norm(conv_ffn(h, w_ffn_up[i], ffn_conv_w[i], ffn_conv_b[i], w_ffn_down[i]), norm_ffn_post[i])
        gate = jax.nn.sigmoid(rms_norm(x, norm_ple_pre[i]) @ w_ple_gate[i])
        x = x + rms_norm((p[i] @ w_ple[i]) * gate, norm_ple_post[i])
    return x
```

```python
import numpy as np
import concourse.bass as bass
import concourse.mybir as mybir
from concourse.bass_utils import run_bass_kernel_spmd

F32 = mybir.dt.float32
BF16 = mybir.dt.bfloat16
AF = mybir.ActivationFunctionType
ALU = mybir.AluOpType
AX = mybir.AxisListType

D = 2048
T = 2048
DEPTH = 2
NCH = D // 128
LRU_W = 1024
DIL_W = 1536
SB_W = 1024
RW_W = 1024
RW_IN = 3360
OFF_B = 2048
OFF_C = OFF_B + 3 * DIL_W
OFF_D = OFF_C + 3 * SB_W
N_IN = OFF_D + RW_IN
D_FF = 5632
PLE = 256
EPS = 1e-6


class Tok:
    __slots__ = ("sem", "val", "known", "eng", "dma")

    def __init__(self, sem, val, known, eng, dma):
        self.sem, self.val, self.known, self.eng, self.dma = sem, val, known, eng, dma


class Prog:
    ENGS = ("pe", "act", "dve", "pool", "sp")
    NSLOT = {"sp": 24, "pool": 24, "act": 8}

    def __init__(self, nc):
        self.nc = nc
        self.ops = {e: [] for e in self.ENGS}
        self.cnt = {e: 0 for e in self.ENGS}
        self.known = {e: {} for e in self.ENGS}
        self.snap = {e: None for e in self.ENGS}
        self.res_w = {}
        self.res_r = {}
        self.slot_uses = {e: [0] * n for e, n in self.NSLOT.items()}
        self.slot_next = {e: 0 for e in self.NSLOT}
        self.slot_last = {e: [None] * n for e, n in self.NSLOT.items()}
        self.sb_mark = None
        self.n_ops = 0

    def sb(self, name, shape, dt):
        self.uid = getattr(self, "uid", 0) + 1
        return self.nc.alloc_sbuf_tensor("%s_u%d" % (name, self.uid), list(shape), dt)

    def ps(self, name, shape, dt=F32):
        self.uid = getattr(self, "uid", 0) + 1
        return self.nc.alloc_psum_tensor("%s_u%d" % (name, self.uid), list(shape), dt)

    def mark(self):
        return (self.nc.sbuf_base, self.nc.sbuf_top, self.nc.psum_base, self.nc.psum_top)

    def release(self, m):
        self.barrier()
        self.nc.sbuf_base, self.nc.sbuf_top, self.nc.psum_base, self.nc.psum_top = m

    def _snapshot(self, eng):
        if self.snap[eng] is None:
            self.snap[eng] = dict(self.known[eng])
        return self.snap[eng]

    def _learn(self, eng, tok):
        k = self.known[eng]
        changed = False
        if k.get(tok.sem, 0) < tok.val:
            k[tok.sem] = tok.val
            changed = True
        for s, v in tok.known.items():
            if k.get(s, 0) < v:
                k[s] = v
                changed = True
        if changed:
            self.snap[eng] = None

    def add(self, eng, fn, reads=(), writes=(), dma=False, extra=()):
        deps = list(extra)
        for r in reads:
            t = self.res_w.get(r)
            if t is not None:
                deps.append(t)
        for w in writes:
            t = self.res_w.get(w)
            if t is not None:
                deps.append(t)
            rr = self.res_r.get(w)
            if rr:
                deps.extend(rr.values())
        waits = {}
        for tok in deps:
            if tok.eng == eng and eng == "pe" and not tok.dma:
                continue
            if self.known[eng].get(tok.sem, 0) >= tok.val:
                continue
            if waits.get(tok.sem, 0) < tok.val:
                waits[tok.sem] = tok.val
            self._learn(eng, tok)
        if dma:
            i = self.slot_next[eng]
            self.slot_next[eng] = (i + 1) % self.NSLOT[eng]
            prev = self.slot_last[eng][i]
            if prev is not None and self.known[eng].get(prev.sem, 0) < prev.val:
                if waits.get(prev.sem, 0) < prev.val:
                    waits[prev.sem] = prev.val
                self._learn(eng, prev)
            self.slot_uses[eng][i] += 1
            sem = "d_%s_%d" % (eng, i)
            tok = Tok(sem, 16 * self.slot_uses[eng][i], self._snapshot(eng), eng, True)
            self.slot_last[eng][i] = tok
            inc = (sem, 16)
        else:
            self.cnt[eng] += 1
            tok = Tok("c_" + eng, self.cnt[eng], self._snapshot(eng), eng, False)
            inc = ("c_" + eng, 1)
        self.ops[eng].append((list(waits.items()), fn, inc))
        self.n_ops += 1
        for w in writes:
            self.res_w[w] = tok
            self.res_r[w] = {}
        for r in reads:
            d = self.res_r.setdefault(r, {})
            d[tok.sem if dma else eng] = tok
        return tok

    def barrier(self):
        toks = []
        for e in self.ENGS:
            if self.cnt[e] > 0:
                toks.append(Tok("c_" + e, self.cnt[e], {}, e, False))
        for e in self.NSLOT:
            for t in self.slot_last[e]:
                if t is not None:
                    toks.append(t)
        for e in self.ENGS:
            if not self.ops[e] and e not in ("sp",):
                pass
            ex = [t for t in toks if not (t.eng == e and not t.dma and False)]
            self.add(e, None, extra=ex)
        self.res_w.clear()
        self.res_r.clear()

    def finish(self, toks):
        self.add("sp", None, extra=list(toks))
        self.barrier()

    def emit(self):
        nc = self.nc
        names = ["c_" + e for e in self.ENGS]
        for e, n in self.NSLOT.items():
            names += ["d_%s_%d" % (e, i) for i in range(n)]
        sems = {n: nc.alloc_semaphore(n) for n in names}
        engobj = {"pe": "tensor", "act": "scalar", "dve": "vector", "pool": "gpsimd", "sp": "sync"}
        with nc.Block() as block:
            for e in self.ENGS:
                ops = self.ops[e]

                def body(eng, ops=ops, e=e):
                    for waits, fn, inc in ops:
                        for s, v in waits:
                            eng.wait_ge(sems[s], v)
                        if fn is None:
                            ins = eng.nop()
                        else:
                            ins = fn(eng)
                        ins.then_inc(sems[inc[0]], inc[1])

                getattr(block, engobj[e])(body)


def _cols(v):
    v = np.asarray(v, np.float32).reshape(-1)
    n = v.shape[0]
    c = (n + 127) // 128
    buf = np.zeros((c * 128,), np.float32)
    buf[:n] = v
    return buf.reshape(c, 128).T


class ParamPack:
    def __init__(self):
        self.off = {}
        self.n = 0
        self.parts = []

    def add(self, name, v):
        a = _cols(v)
        self.off[name] = (self.n, a.shape[1])
        self.n += a.shape[1]
        self.parts.append(a)

    def array(self):
        return np.ascontiguousarray(np.concatenate(self.parts, axis=1))


def pack_params(inp, L):
    pk = ParamPack()
    for nm in ("norm_mix_pre", "norm_mix_post", "norm_ffn_pre", "norm_ffn_post", "norm_ple_pre", "norm_ple_post"):
        pk.add(nm, inp[nm][L])
    for k in range(4):
        pk.add("lru_conv_w%d" % k, inp["lru_conv_w"][L, k])
    for nm in ("lru_conv_b", "lru_b_r", "lru_b_i", "lru_lambda"):
        pk.add(nm, inp[nm][L])
    mu = inp["rwkv_mu"][L]
    pk.add("mu_rkv", mu[:3072])
    pk.add("mu_wl", mu[3072:3136])
    pk.add("mu_al", mu[3136:3200])
    pk.add("mu_g1", mu[3200:3328])
    pk.add("mu_g2", mu[3328:3360])
    for nm in ("rwkv_w0", "rwkv_a0", "rwkv_k_k", "rwkv_k_a", "rwkv_r_k", "rwkv_gn_w", "rwkv_gn_b"):
        pk.add(nm, inp[nm][L])
    pk.add("rwkv_v0", inp["rwkv_v0"][L - 1] if L > 0 else np.zeros(1024, np.float32))
    for k in range(3):
        pk.add("ffn_conv_w%d" % k, inp["ffn_conv_w"][L, k])
    pk.add("ffn_conv_b", inp["ffn_conv_b"][L])
    return pk


class Ctx:
    pass


def act_fn(out, in_, func, bias=None, scale=None):
    kw = {}
    if bias is not None:
        kw["bias"] = bias
    if scale is not None:
        kw["scale"] = scale
    return lambda e: e.activation(out, in_, func, **kw)


def rmsnorm_hT(P, C, src, gcol, hT, tag):
    m = P.mark()
    xs = P.sb(tag + "_xs", [128, NCH, 512], F32)
    sq = P.sb(tag + "_sq", [128, NCH, 512], F32)
    rstd = P.sb(tag + "_rstd", [128, 512], F32)
    ss = P.ps(tag + "_ss", [128, 512], F32)
    srcv = src.rearrange("(c p) t -> p c t", p=128)
    for g in range(T // 512):
        tsl = slice(g * 512, (g + 1) * 512)
        P.add("sp", lambda e, tsl=tsl: e.dma_start(out=xs[:], in_=srcv[:, :, tsl]), writes=[xs.name], dma=True)
        P.add("act", act_fn(sq[:], xs[:], AF.Square), reads=[xs.name], writes=[sq.name])
        for c in range(NCH):
            P.add("pe", lambda e, c=c: e.matmul(ss[:], C.ones_invD[:], sq[:, c, :], start=(c == 0), stop=(c == NCH - 1)),
                  reads=[sq.name, "const"], writes=[ss.name])
        P.add("act", act_fn(rstd[:], ss[:], AF.Sqrt, bias=C.eps_col, scale=1.0), reads=[ss.name, "const"], writes=[rstd.name])
        P.add("dve", lambda e: e.reciprocal(rstd[:], rstd[:]), reads=[rstd.name], writes=[rstd.name])
        for c in range(NCH):
            P.add("dve", lambda e, c=c, tsl=tsl: e.scalar_tensor_tensor(hT[:, c, tsl], xs[:, c, :], gcol[:, c:c + 1], rstd[:],
                                                                         ALU.mult, ALU.mult),
                  reads=[xs.name, rstd.name, "params"], writes=[hT.name])
    P.release(m)


def phase_inproj(P, C, L):
    m = P.mark()
    hT = P.sb("hT", [128, NCH, T], BF16)
    rmsnorm_hT(P, C, C.xT, C.pcol("norm_mix_pre"), hT, "n1")
    hv = C.hT_d.rearrange("(c p) t -> p c t", p=128)
    P.add("sp", lambda e: e.dma_start(out=hv, in_=hT[:]), reads=[hT.name], writes=["hT_d"], dma=True)
    w = C.W[L]["w_in"]
    m2 = P.mark()
    proj_from_hT(P, C, hT, w[:, 0:2048], 2048, C.projA, "ipA", None)
    P.release(m2)
    proj_from_hT(P, C, hT, w[:, OFF_B:OFF_B + 3072], 3072, C.projB, "ipB", None)
    P.release(m2)
    proj_from_hT(P, C, hT, w[:, OFF_C:OFF_C + 2048], 2048, C.projC, "ipC", None)
    P.release(m2)
    proj_from_hT(P, C, hT, w[:, OFF_D:OFF_D + RW_IN], RW_IN, C.projD, "ipD", None)
    P.release(m2)
    proj_tokmajor(P, C, hT, w[:, OFF_B + 3072:OFF_B + 4608], 1536, C.vtokB, "ivB")
    P.release(m2)
    proj_tokmajor(P, C, hT, w[:, OFF_C + 2048:OFF_C + 3072], 1024, C.vtokC, "ivC")
    P.release(m)


def proj_tokmajor(P, C, hT, w, n_out, dst, tag):
    WB = 512
    nblk = n_out // WB
    wb = [P.sb("%s_wb%d" % (tag, i), [128, NCH, WB], BF16) for i in range(2)]
    ost = [P.sb("%s_os%d" % (tag, i), [128, WB], BF16) for i in range(3)]
    pss = [P.ps("%s_ps%d" % (tag, i), [128, 512], F32) for i in range(4)]
    wv = w.rearrange("(c p) n -> p c n", p=128)
    k = 0
    for b in range(nblk):
        wt = wb[b % 2]
        P.add("pool", lambda e, wt=wt, b=b: e.dma_start(out=wt[:], in_=wv[:, :, b * WB:(b + 1) * WB]), writes=[wt.name], dma=True)
        for tt in range(T // 128):
            pt = pss[k % 4]
            os_ = ost[k % 3]
            k += 1
            for c in range(NCH):
                P.add("pe", lambda e, pt=pt, wt=wt, c=c, tt=tt: e.matmul(pt[:], hT[:, c, tt * 128:(tt + 1) * 128], wt[:, c, :],
                                                                          start=(c == 0), stop=(c == NCH - 1)),
                      reads=[wt.name, hT.name], writes=[pt.name])
            if k % 2 == 0:
                P.add("act", lambda e, pt=pt, os_=os_: e.copy(os_[:], pt[:]), reads=[pt.name], writes=[os_.name])
            else:
                P.add("dve", lambda e, pt=pt, os_=os_: e.tensor_copy(os_[:], pt[:]), reads=[pt.name], writes=[os_.name])
            P.add("sp", lambda e, os_=os_, tt=tt, b=b: e.dma_start(out=dst[tt * 128:(tt + 1) * 128, b * WB:(b + 1) * WB], in_=os_[:]),
                  reads=[os_.name], writes=[(tag, tt, b)], dma=True)


def proj_from_hT(P, C, hT, w, n_out, dstT, tag, evac, odt=F32, func=None):
    WB = 512
    nblk = (n_out + WB - 1) // WB
    wb = [P.sb("%s_wb%d" % (tag, i), [128, NCH, WB], BF16) for i in range(2)]
    ost = [P.sb("%s_os%d" % (tag, i), [128, T], odt) for i in range(3)]
    pss = [P.ps("%s_ps%d" % (tag, i), [128, 512], F32) for i in range(6)]
    wv = w.rearrange("(c p) n -> p c n", p=128)
    k_ps = 0
    k_os = 0
    for b in range(nblk):
        n0b = b * WB
        wbs = min(WB, n_out - n0b)
        wt = wb[b % 2]
        P.add("pool", lambda e, wt=wt, n0b=n0b, wbs=wbs: e.dma_start(out=wt[:, :, 0:wbs], in_=wv[:, :, n0b:n0b + wbs]),
              writes=[wt.name], dma=True)
        for j in range((wbs + 127) // 128):
            msz = min(128, wbs - j * 128)
            n0 = n0b + j * 128
            os_ = ost[k_os % 3]
            k_os += 1
            for g in range(T // 512):
                tsl = slice(g * 512, (g + 1) * 512)
                pt = pss[k_ps % 6]
                k_ps += 1
                for c in range(NCH):
                    P.add("pe", lambda e, pt=pt, wt=wt, c=c, j=j, msz=msz, tsl=tsl:
                          e.matmul(pt[0:msz, :], wt[:, c, j * 128:j * 128 + msz], hT[:, c, tsl], start=(c == 0), stop=(c == NCH - 1)),
                          reads=[wt.name, hT.name], writes=[pt.name])
                if func is not None:
                    A(P, os_[0:msz, tsl], pt[0:msz, :], func, [pt.name], [os_.name])
                elif k_ps % 2 == 0:
                    P.add("act", lambda e, pt=pt, os_=os_, msz=msz, tsl=tsl: e.copy(os_[0:msz, tsl], pt[0:msz, :]),
                          reads=[pt.name], writes=[os_.name])
                else:
                    P.add("dve", lambda e, pt=pt, os_=os_, msz=msz, tsl=tsl: e.tensor_copy(os_[0:msz, tsl], pt[0:msz, :]),
                          reads=[pt.name], writes=[os_.name])
            P.add("sp", lambda e, os_=os_, n0=n0, msz=msz: e.dma_start(out=dstT[n0:n0 + msz, :], in_=os_[0:msz, :]),
                  reads=[os_.name], writes=[("dT", tag, n0)], dma=True)


def dma_in(P, eng, dst, src, key=None):
    return P.add(eng, lambda e: e.dma_start(out=dst, in_=src), writes=[key], dma=True)


def V(P, fn, reads, writes):
    return P.add("dve", fn, reads=reads, writes=writes)


def A(P, out, in_, func, reads, writes, bias=None, scale=None):
    return P.add("act", act_fn(out, in_, func, bias=bias, scale=scale), reads=reads, writes=writes)


GELU_K = 1.5957691216057308


def gelu_mul(P, C, g, other, out, tmp1, tmp2, n, reads_extra=()):
    A(P, tmp1, g, AF.Square, [n["g"]], [n["t1"]])
    V(P, lambda e: e.tensor_scalar(tmp1, tmp1, 0.044715, 1.0, ALU.mult, ALU.add), [n["t1"]], [n["t1"]])
    V(P, lambda e: e.tensor_tensor(tmp1, tmp1, g, ALU.mult), [n["t1"], n["g"]], [n["t1"]])
    A(P, tmp2, tmp1, AF.Sigmoid, [n["t1"]], [n["t2"]], scale=GELU_K)
    V(P, lambda e: e.tensor_tensor(tmp2, tmp2, g, ALU.mult), [n["t2"], n["g"]], [n["t2"]])
    V(P, lambda e: e.tensor_tensor(out, tmp2, other, ALU.mult), [n["t2"], n["o"]], [n["out"]])


def phase_rglru(P, C, li):
    m = P.mark()
    W = C.W[li]
    wr = P.sb("wr", [128, 8, 128], BF16)
    wi = P.sb("wi", [128, 8, 128], BF16)
    P.add("pool", lambda e: e.dma_start(out=wr[:], in_=W["lru_w_r"].rearrange("(h i) j -> i h j", i=128)), writes=[wr.name], dma=True)
    P.add("pool", lambda e: e.dma_start(out=wi[:], in_=W["lru_w_i"].rearrange("(h i) j -> i h j", i=128)), writes=[wi.name], dma=True)
    cc = P.sb("lru_c", [128, 16], F32)
    lam = C.pcol("lru_lambda")
    A(P, cc[:, 0:8], lam, AF.Exp, ["params"], [cc.name], scale=-1.0)
    A(P, cc[:, 0:8], cc[:, 0:8], AF.Ln, [cc.name, "const"], [cc.name], bias=C.one_col, scale=1.0)
    V(P, lambda e: e.tensor_scalar(cc[:, 8:16], cc[:, 0:8], -16.0, None, ALU.mult), [cc.name], [cc.name])
    V(P, lambda e: e.tensor_scalar(cc[:, 0:8], cc[:, 0:8], -8.0, None, ALU.mult), [cc.name], [cc.name])
    names = ["x", "g", "u", "ub", "r", "ig", "a", "t1", "t2", "hs"]
    tl = {}
    for nm in names:
        tl[nm] = P.sb("lru_" + nm, [128, T], BF16 if nm == "ub" else F32)
    yo = P.sb("lru_y", [128, T], BF16)
    pss = [P.ps("lru_ps%d" % i, [128, 512], F32) for i in range(4)]
    cw = [C.pcol("lru_conv_w%d" % k) for k in range(4)]
    cb = C.pcol("lru_conv_b")
    br = C.pcol("lru_b_r")
    bi = C.pcol("lru_b_i")
    x, g, u, ub, r, ig, a, t1, t2, hs = [tl[nm] for nm in names]
    for h in range(8):
        P.add("sp", lambda e, h=h: e.dma_start(out=x[:], in_=C.projA[128 * h:128 * h + 128, :]), writes=[x.name], dma=True)
        P.add("sp", lambda e, h=h: e.dma_start(out=g[:], in_=C.projA[1024 + 128 * h:1024 + 128 * h + 128, :]), writes=[g.name], dma=True)
        V(P, lambda e, h=h: e.tensor_scalar(u[:], x[:], cw[3][:, h:h + 1], cb[:, h:h + 1], ALU.mult, ALU.add), [x.name, "params"], [u.name])
        for k, sh in ((2, 1), (1, 2), (0, 3)):
            V(P, lambda e, h=h, k=k, sh=sh: e.scalar_tensor_tensor(u[:, sh:], x[:, :T - sh], cw[k][:, h:h + 1], u[:, sh:], ALU.mult, ALU.add),
              [x.name, u.name, "params"], [u.name])
        A(P, ub[:], u[:], AF.Copy, [u.name], [ub.name])
        for (wt, bcol, dst) in ((wr, br, r), (wi, bi, ig)):
            for gq in range(4):
                pt = pss[gq]
                tsl = slice(gq * 512, gq * 512 + 512)
                P.add("pe", lambda e, pt=pt, wt=wt, h=h, tsl=tsl: e.matmul(pt[:], wt[:, h, :], ub[:, tsl], start=True, stop=True),
                      reads=[wt.name, ub.name], writes=[pt.name])
                A(P, dst[:, tsl], pt[:], AF.Sigmoid, [pt.name, "params"], [dst.name], bias=bcol[:, h:h + 1], scale=1.0)
        A(P, a[:], r[:], AF.Exp, [r.name, cc.name], [a.name], scale=cc[:, h:h + 1])
        A(P, t1[:], r[:], AF.Exp, [r.name, cc.name], [t1.name], scale=cc[:, 8 + h:9 + h])
        A(P, t1[:], t1[:], AF.Sqrt, [t1.name, "const"], [t1.name], bias=C.one_col, scale=-1.0)
        V(P, lambda e: e.tensor_tensor(t1[:], t1[:], ig[:], ALU.mult), [t1.name, ig.name], [t1.name])
        V(P, lambda e: e.tensor_tensor(t1[:], t1[:], u[:], ALU.mult), [t1.name, u.name], [t1.name])
        V(P, lambda e: e.tensor_tensor_scan(hs[:], a[:], t1[:], 0.0, ALU.mult, ALU.add), [a.name, t1.name], [hs.name])
        gelu_mul(P, C, g[:], hs[:], yo[:], t1[:], t2[:], dict(g=g.name, t1=t1.name, t2=t2.name, o=hs.name, out=yo.name))
        P.add("sp", lambda e, h=h: e.dma_start(out=C.yaT[128 * h:128 * h + 128, :], in_=yo[:]), reads=[yo.name], writes=[("yaT", h)], dma=True)
    P.release(m)


def phase_dilated(P, C, li):
    m = P.mark()
    rope = P.sb("rope", [128, 2, T], F32)
    P.add("sp", lambda e: e.dma_start(out=rope[:], in_=C.rope_d[:, 0:2, :]), writes=[rope.name], dma=True)
    ones_bf = P.sb("ones_bf", [128, 128], BF16)
    A(P, ones_bf[:], C.ones1, AF.Copy, ["const"], [ones_bf.name])
    scale = 128.0 ** -0.5
    xq = [P.sb("dl_x%d" % i, [128, T], F32) for i in range(2)]
    cm = [P.sb("dl_cm%d" % i, [128, T], BF16) for i in range(2)]
    t1 = [P.sb("dl_t1_%d" % i, [128, 512], F32) for i in range(2)]
    t2 = [P.sb("dl_t2_%d" % i, [128, 512], F32) for i in range(2)]
    vblk = P.sb("dl_v", [128, 16, 128], BF16)
    num = P.sb("dl_num", [128, T], F32)
    den = P.sb("dl_den", [128, T], F32)
    ex = [P.sb("dl_ex%d" % i, [128, 256], F32) for i in range(2)]
    pT = [P.sb("dl_pT%d" % i, [128, 256], BF16) for i in range(2)]
    yo = P.sb("dl_yo", [128, T], BF16)
    psR = [P.ps("dl_psR%d" % i, [128, 512], F32) for i in range(2)]
    psS = [P.ps("dl_psS%d" % i, [128, 512], F32) for i in range(2)]
    psO = [P.ps("dl_psO%d" % i, [128, 512], F32) for i in range(2)]
    kR = 0
    kb = 0
    DIL = (1, 4, 16)
    for h in range(4):
        for g in range(3):
            d = DIL[g]
            Lc = T // d
            nbc = Lc // 128
            hq = g * 4 + h
            vsrc = C.vtokB[:, hq * 128:(hq + 1) * 128].rearrange("(n j r) c -> r j n c", j=128, r=d)
            for r in range(d):
                P.add("sp", lambda e, r=r, vsrc=vsrc, nbc=nbc: e.dma_start(out=vblk[:, r * nbc:(r + 1) * nbc, :], in_=vsrc[r]),
                      writes=[vblk.name], dma=True)
            for qi in range(2):
                row0 = qi * 1536 + hq * 128
                P.add("sp", lambda e, qi=qi, row0=row0: e.dma_start(out=xq[qi][:], in_=C.projB[row0:row0 + 128, :]),
                      writes=[xq[qi].name], dma=True)
                dstv = cm[qi][:].rearrange("p (r l) -> p r l", r=d)
                for gq in range(4):
                    tsl = slice(gq * 512, gq * 512 + 512)
                    pr = psR[kR % 2]
                    a1 = t1[kR % 2]
                    a2 = t2[kR % 2]
                    kR += 1
                    P.add("pe", lambda e, pr=pr, qi=qi, tsl=tsl: e.matmul(pr[:], C.rmat, xq[qi][:, tsl], start=True, stop=True),
                          reads=[xq[qi].name, "const"], writes=[pr.name])
                    V(P, lambda e, a1=a1, qi=qi, tsl=tsl: e.tensor_tensor(a1[:], xq[qi][:, tsl], rope[:, 0, tsl], ALU.mult),
                      [xq[qi].name, rope.name], [a1.name])
                    V(P, lambda e, a2=a2, pr=pr, tsl=tsl: e.tensor_tensor(a2[:], pr[:], rope[:, 1, tsl], ALU.mult),
                      [pr.name, rope.name], [a2.name])
                    l0 = gq * 512 // d
                    nl = 512 // d
                    V(P, lambda e, a1=a1, a2=a2, dstv=dstv, l0=l0, nl=nl, d=d:
                      e.tensor_tensor(dstv[:, :, l0:l0 + nl], a1[:].rearrange("p (l r) -> p r l", r=d),
                                      a2[:].rearrange("p (l r) -> p r l", r=d), ALU.add),
                      [a1.name, a2.name], [cm[qi].name])
            qc, kc = cm[0], cm[1]
            accv_n = num[:].rearrange("p (l r) -> p r l", r=d)
            accv_d = den[:].rearrange("p (l r) -> p r l", r=d)
            for b in range(16):
                r, n = b // nbc, b % nbc
                hasprev = n > 0
                sp_ = psS[kb % 2]
                op_ = psO[kb % 2]
                exb = ex[kb % 2]
                pb = pT[kb % 2]
                kb += 1
                c0 = 0 if hasprev else 128
                P.add("pe", lambda e, sp_=sp_, b=b: e.matmul(sp_[:, 128:256], kc[:, 128 * b:128 * b + 128], qc[:, 128 * b:128 * b + 128],
                                                             start=True, stop=True),
                      reads=[kc.name, qc.name], writes=[sp_.name])
                if hasprev:
                    P.add("pe", lambda e, sp_=sp_, b=b: e.matmul(sp_[:, 0:128], kc[:, 128 * (b - 1):128 * b], qc[:, 128 * b:128 * b + 128],
                                                                 start=True, stop=True),
                          reads=[kc.name, qc.name], writes=[sp_.name])
                A(P, exb[:, c0:256], sp_[:, c0:256], AF.Exp, [sp_.name], [exb.name], scale=scale)
                V(P, lambda e, exb=exb, pb=pb, c0=c0: e.tensor_tensor(pb[:, c0:256], exb[:, c0:256], C.dilmask[:, c0:256], ALU.mult),
                  [exb.name, "const"], [pb.name])
                P.add("pe", lambda e, op_=op_, pb=pb, b=b, h=h, hasprev=hasprev:
                      e.matmul(op_[:, 0:128], vblk[:, b, :], pb[:, 128:256], start=True, stop=not hasprev),
                      reads=[vblk.name, pb.name], writes=[op_.name])
                if hasprev:
                    P.add("pe", lambda e, op_=op_, pb=pb, b=b, h=h:
                          e.matmul(op_[:, 0:128], vblk[:, b - 1, :], pb[:, 0:128], start=False, stop=True),
                          reads=[vblk.name, pb.name], writes=[op_.name])
                P.add("pe", lambda e, op_=op_, pb=pb, hasprev=hasprev:
                      e.matmul(op_[:, 128:256], ones_bf[:], pb[:, 128:256], start=True, stop=not hasprev),
                      reads=[ones_bf.name, pb.name], writes=[op_.name])
                if hasprev:
                    P.add("pe", lambda e, op_=op_, pb=pb: e.matmul(op_[:, 128:256], ones_bf[:], pb[:, 0:128], start=False, stop=True),
                          reads=[ones_bf.name, pb.name], writes=[op_.name])
                dn = accv_n[:, r, 128 * n:128 * n + 128]
                dd = accv_d[:, r, 128 * n:128 * n + 128]
                if g == 0:
                    V(P, lambda e, dn=dn, op_=op_: e.tensor_copy(dn, op_[:, 0:128]), [op_.name], [num.name])
                    V(P, lambda e, dd=dd, op_=op_: e.tensor_copy(dd, op_[:, 128:256]), [op_.name], [den.name])
                else:
                    V(P, lambda e, dn=dn, op_=op_: e.tensor_tensor(dn, dn, op_[:, 0:128], ALU.add), [op_.name, num.name], [num.name])
                    V(P, lambda e, dd=dd, op_=op_: e.tensor_tensor(dd, dd, op_[:, 128:256], ALU.add), [op_.name, den.name], [den.name])
        V(P, lambda e: e.reciprocal(den[:], den[:]), [den.name], [den.name])
        V(P, lambda e: e.tensor_tensor(yo[:], num[:], den[:], ALU.mult), [num.name, den.name], [yo.name])
        P.add("sp", lambda e, h=h: e.dma_start(out=C.ybT[128 * h:128 * h + 128, :], in_=yo[:]), reads=[yo.name], writes=[("ybT", h)], dma=True)
    P.release(m)


def phase_stickbreak(P, C, li):
    m = P.mark()
    scale = 128.0 ** -0.5
    ident_bf = P.sb("sb_ident", [128, 128], BF16)
    A(P, ident_bf[:], C.ident, AF.Copy, ["const"], [ident_bf.name])
    ones_t = P.sb("sb_ones", [128, T], F32)
    P.add("pool", lambda e: e.memset(ones_t[:], 1.0), writes=[ones_t.name])
    vsb = P.sb("sb_v", [128, 16, 1024], BF16)
    P.add("sp", lambda e: e.dma_start(out=vsb[:], in_=C.vtokC.rearrange("(n j) c -> j n c", j=128)), writes=[vsb.name], dma=True)
    xq = [P.sb("sb_x%d" % i, [128, T], F32) for i in range(2)]
    qb = P.sb("sb_qb", [128, T], BF16)
    kb_ = P.sb("sb_kb", [128, T], BF16)
    ez = P.sb("sb_ez", [128, T], F32)
    nl = P.sb("sb_nl", [128, T], F32)
    cs = P.sb("sb_cs", [128, T], F32)
    att = P.sb("sb_att", [128, T], BF16)
    attT = [P.sb("sb_attT%d" % i, [128, 512], BF16) for i in range(2)]
    ntot = P.sb("sb_ntot", [128, 1], F32)
    yo = P.sb("sb_yo", [128, T], BF16)
    psZ = [P.ps("sb_psZ%d" % i, [128, 512], F32) for i in range(4)]
    psT = [P.ps("sb_psT%d" % i, [128, 512], BF16) for i in range(2)]
    psY = [P.ps("sb_psY%d" % i, [128, 128], F32) for i in range(2)]
    kT = 0
    for h in range(8):
        for qi, dst in ((0, qb), (1, kb_)):
            row0 = qi * 1024 + h * 128
            P.add("sp", lambda e, qi=qi, row0=row0: e.dma_start(out=xq[qi][:], in_=C.projC[row0:row0 + 128, :]),
                  writes=[xq[qi].name], dma=True)
            A(P, dst[:], xq[qi][:], AF.Copy, [xq[qi].name], [dst.name])
        for n in range(16):
            nk = 128 * (n + 1)
            nbank = (nk + 511) // 512
            for bk in range(nbank):
                c0 = bk * 512
                w = min(512, nk - c0)
                pz = psZ[bk]
                P.add("pe", lambda e, pz=pz, c0=c0, w=w, n=n: e.matmul(pz[:, 0:w], qb[:, 128 * n:128 * n + 128], kb_[:, c0:c0 + w],
                                                                      start=True, stop=True),
                      reads=[qb.name, kb_.name], writes=[pz.name])
                A(P, ez[:, c0:c0 + w], pz[:, 0:w], AF.Exp, [pz.name], [ez.name], scale=scale)
            d0 = 128 * n
            V(P, lambda e, d0=d0: e.tensor_tensor(ez[:, d0:d0 + 128], ez[:, d0:d0 + 128], C.sbmask, ALU.mult), [ez.name, "const"], [ez.name])
            A(P, nl[:, 0:nk], ez[:, 0:nk], AF.Ln, [ez.name, "const"], [nl.name], bias=C.one_col, scale=1.0)
            V(P, lambda e, nk=nk: e.tensor_tensor_scan(cs[:, 0:nk], ones_t[:, 0:nk], nl[:, 0:nk], 0.0, ALU.mult, ALU.add),
              [ones_t.name, nl.name], [cs.name])
            V(P, lambda e, nk=nk: e.tensor_scalar(ntot[:], cs[:, nk - 1:nk], -1.0, None, ALU.mult), [cs.name], [ntot.name])
            V(P, lambda e, nk=nk: e.tensor_tensor(cs[:, 0:nk], cs[:, 0:nk], nl[:, 0:nk], ALU.subtract), [cs.name, nl.name], [cs.name])
            A(P, cs[:, 0:nk], cs[:, 0:nk], AF.Exp, [cs.name, ntot.name], [cs.name], bias=ntot[:, 0:1], scale=1.0)
            V(P, lambda e, nk=nk: e.tensor_tensor(att[:, 0:nk], ez[:, 0:nk], cs[:, 0:nk], ALU.mult), [ez.name, cs.name], [att.name])
            py = psY[n % 2]
            for b0 in range(0, n + 1, 4):
                nb = min(4, n + 1 - b0)
                pt = psT[kT % 2]
                at = attT[kT % 2]
                kT += 1
                for j in range(nb):
                    b = b0 + j
                    P.add("pe", lambda e, pt=pt, j=j, b=b: e.transpose(pt[:, 128 * j:128 * j + 128], att[:, 128 * b:128 * b + 128], ident_bf[:]),
                          reads=[att.name, ident_bf.name], writes=[pt.name])
                if kT % 2 == 0:
                    V(P, lambda e, pt=pt, at=at, nb=nb: e.tensor_copy(at[:, 0:128 * nb], pt[:, 0:128 * nb]), [pt.name], [at.name])
                else:
                    P.add("act", lambda e, pt=pt, at=at, nb=nb: e.copy(at[:, 0:128 * nb], pt[:, 0:128 * nb]), reads=[pt.name], writes=[at.name])
                for j in range(nb):
                    b = b0 + j
                    P.add("pe", lambda e, py=py, at=at, j=j, b=b, h=h, n=n:
                          e.matmul(py[:], vsb[:, b, h * 128:(h + 1) * 128], at[:, 128 * j:128 * j + 128], start=(b == 0), stop=(b == n)),
                          reads=[vsb.name, at.name], writes=[py.name])
            P.add("act", lambda e, py=py, n=n: e.copy(yo[:, 128 * n:128 * n + 128], py[:]), reads=[py.name], writes=[yo.name])
        P.add("sp", lambda e, h=h: e.dma_start(out=C.ycT[128 * h:128 * h + 128, :], in_=yo[:]), reads=[yo.name], writes=[("ycT", h)], dma=True)
    P.release(m)


DECAY_K = -0.6065306597126334
GN_EPS = 64e-5


def lerp(P, out, x, mu, om, rd, wr):
    n = x.shape[-1]
    V(P, lambda e: e.tensor_scalar(out, x, om, None, ALU.mult), rd, wr)
    V(P, lambda e: e.scalar_tensor_tensor(out[:, 1:], x[:, :n - 1], mu, out[:, 1:], ALU.mult, ALU.add), rd + wr, wr)


def phase_rwkv_prep(P, C, li):
    m = P.mark()
    W = C.W[li]
    L = C.L
    o_rkv, n_rkv = C.poff["mu_rkv"]
    o_end = C.poff["mu_g2"][0] + C.poff["mu_g2"][1]
    nmu = o_end - o_rkv
    omu = P.sb("rw_omu", [128, nmu], F32)
    V(P, lambda e: e.tensor_scalar(omu[:], C.params[:, o_rkv:o_end], -1.0, 1.0, ALU.mult, ALU.add), ["params"], [omu.name])
    mucol = lambda nm, c=0: C.params[:, C.poff[nm][0] + c:C.poff[nm][0] + c + 1]
    omcol = lambda nm, c=0: omu[:, C.poff[nm][0] - o_rkv + c:C.poff[nm][0] - o_rkv + c + 1]
    wup = P.sb("rw_wup", [64, RW_W], BF16)
    aup = P.sb("rw_aup", [64, RW_W], BF16)
    gup1 = P.sb("rw_gup1", [128, RW_W], BF16)
    gup2 = P.sb("rw_gup2", [32, RW_W], BF16)
    P.add("pool", lambda e: e.dma_start(out=wup[:], in_=W["rwkv_w_up"]), writes=[wup.name], dma=True)
    P.add("pool", lambda e: e.dma_start(out=aup[:], in_=W["rwkv_a_up"]), writes=[aup.name], dma=True)
    P.add("pool", lambda e: e.dma_start(out=gup1[:], in_=W["rwkv_g_up"][0:128, :]), writes=[gup1.name], dma=True)
    P.add("pool", lambda e: e.dma_start(out=gup2[:], in_=W["rwkv_g_up"][128:160, :]), writes=[gup2.name], dma=True)
    if L > 0:
        vdn = P.sb("rw_vdn", [128, 8, 32], BF16)
        vup = P.sb("rw_vup", [32, RW_W], BF16)
        P.add("pool", lambda e: e.dma_start(out=vdn[:], in_=W["rwkv_v_down"].rearrange("(c p) n -> p c n", p=128)), writes=[vdn.name], dma=True)
        P.add("pool", lambda e: e.dma_start(out=vup[:], in_=W["rwkv_v_up"]), writes=[vup.name], dma=True)
    rmask = P.sb("rw_rmask", [128, T], F32)
    P.add("sp", lambda e: e.dma_start(out=rmask[:], in_=C.rope_d[:, 2, :]), writes=[rmask.name], dma=True)
    xin = P.sb("rw_xin", [128, T], F32)
    tmp = P.sb("rw_tmp", [128, T], F32)
    twl = P.sb("rw_twl", [64, T], BF16)
    tal = P.sb("rw_tal", [64, T], BF16)
    tg1 = P.sb("rw_tg1", [128, T], BF16)
    tg2 = P.sb("rw_tg2", [32, T], BF16)
    for (row0, nr, munm, dst, fn) in ((3072, 64, "mu_wl", twl, "tanh"), (3136, 64, "mu_al", tal, AF.Copy),
                                      (3200, 128, "mu_g1", tg1, AF.Sigmoid), (3328, 32, "mu_g2", tg2, AF.Sigmoid)):
        P.add("sp", lambda e, row0=row0, nr=nr: e.dma_start(out=xin[0:nr, :], in_=C.projD[row0:row0 + nr, :]), writes=[xin.name], dma=True)
        lerp(P, tmp[0:nr, :], xin[0:nr, :], mucol(munm)[0:nr], omcol(munm)[0:nr], [xin.name, "params", omu.name], [tmp.name])
        if fn == "tanh":
            A(P, tmp[0:nr, :], tmp[0:nr, :], AF.Sigmoid, [tmp.name], [tmp.name], scale=2.0)
            V(P, lambda e, dst=dst, nr=nr: e.tensor_scalar(dst[0:nr, :], tmp[0:nr, :], 2.0, -1.0, ALU.mult, ALU.add), [tmp.name], [dst.name])
        else:
            A(P, dst[0:nr, :], tmp[0:nr, :], fn, [tmp.name], [dst.name])
    pss = [P.ps("rw_ps%d" % i, [128, 512], F32) for i in range(6)]
    kps = [0]

    def nps():
        kps[0] += 1
        return pss[kps[0] % 6]

    vb = None
    vd = None
    if L > 0:
        vb = P.sb("rw_vb", [128, 8, T], BF16)
        vd = P.sb("rw_vd", [32, T], BF16)
        for c in range(8):
            P.add("sp", lambda e, c=c: e.dma_start(out=xin[:], in_=C.projD[2048 + 128 * c:2048 + 128 * c + 128, :]), writes=[xin.name], dma=True)
            lerp(P, tmp[:], xin[:], mucol("mu_rkv", 16 + c), omcol("mu_rkv", 16 + c), [xin.name, "params", omu.name], [tmp.name])
            A(P, vb[:, c, :], tmp[:], AF.Copy, [tmp.name], [vb.name])
        for gq in range(4):
            tsl = slice(gq * 512, gq * 512 + 512)
            pt = nps()
            for c in range(8):
                P.add("pe", lambda e, pt=pt, c=c, tsl=tsl: e.matmul(pt[0:32, :], vdn[:, c, :], vb[:, c, tsl], start=(c == 0), stop=(c == 7)),
                      reads=[vdn.name, vb.name], writes=[pt.name])
            A(P, vd[:, tsl], pt[0:32, :], AF.Copy, [pt.name], [vd.name])
    names = ["r", "k", "v", "a", "lw", "kk", "t1", "t2", "t3"]
    tl = {nm: P.sb("rw_" + nm, [128, T], F32) for nm in names}
    r_, k_, v_, a_, lw, kk, t1, t2, t3 = [tl[nm] for nm in names]
    ar = P.sb("rw_arsb", [128, 32, 128], BF16)
    kbt = P.sb("rw_kbsb", [128, 32, 128], BF16)
    vo = P.sb("rw_vo", [128, T], BF16)
    wc = P.sb("rw_wcsb", [128, 32], F32)
    c3 = lambda t: t[:].rearrange("p (q s) -> p q s", s=64)
    for c in range(8):
        for (dst, roff, mc) in ((r_, 0, c), (k_, 1024, 8 + c), (v_, 2048, 16 + c)):
            P.add("sp", lambda e, roff=roff, c=c: e.dma_start(out=xin[:], in_=C.projD[roff + 128 * c:roff + 128 * c + 128, :]),
                  writes=[xin.name], dma=True)
            lerp(P, dst[:], xin[:], mucol("mu_rkv", mc), omcol("mu_rkv", mc), [xin.name, "params", omu.name], [dst.name])
        csl = slice(128 * c, 128 * c + 128)
        for gq in range(4):
            tsl = slice(gq * 512, gq * 512 + 512)
            pt = nps()
            P.add("pe", lambda e, pt=pt, tsl=tsl, csl=csl: e.matmul(pt[:], wup[:, csl], twl[:, tsl], start=True, stop=True),
                  reads=[wup.name, twl.name], writes=[pt.name])
            A(P, lw[:, tsl], pt[:], AF.Sigmoid, [pt.name, "params"], [lw.name], bias=C.pcol("rwkv_w0")[:, c:c + 1], scale=1.0)
            pt = nps()
            P.add("pe", lambda e, pt=pt, tsl=tsl, csl=csl: e.matmul(pt[:], aup[:, csl], tal[:, tsl], start=True, stop=True),
                  reads=[aup.name, tal.name], writes=[pt.name])
            A(P, a_[:, tsl], pt[:], AF.Sigmoid, [pt.name, "params"], [a_.name], bias=C.pcol("rwkv_a0")[:, c:c + 1], scale=1.0)
            pt = nps()
            P.add("pe", lambda e, pt=pt, tsl=tsl, csl=csl: e.matmul(pt[:], gup1[:, csl], tg1[:, tsl], start=True, stop=False),
                  reads=[gup1.name, tg1.name], writes=[pt.name])
            P.add("pe", lambda e, pt=pt, tsl=tsl, csl=csl: e.matmul(pt[:], gup2[:, csl], tg2[:, tsl], start=False, stop=True),
                  reads=[gup2.name, tg2.name], writes=[pt.name])
            A(P, t3[:, tsl], pt[:], AF.Copy, [pt.name], [t3.name])
        P.add("sp", lambda e, csl=csl: e.dma_start(out=C.rw_g[csl, :], in_=t3[:]), reads=[t3.name], writes=[("rw_g", c)], dma=True)
        if L > 0:
            for gq in range(4):
                tsl = slice(gq * 512, gq * 512 + 512)
                pt = nps()
                P.add("pe", lambda e, pt=pt, tsl=tsl, csl=csl: e.matmul(pt[:], vup[:, csl], vd[:, tsl], start=True, stop=True),
                      reads=[vup.name, vd.name], writes=[pt.name])
                A(P, t1[:, tsl], pt[:], AF.Sigmoid, [pt.name, "params"], [t1.name], bias=C.pcol("rwkv_v0")[:, c:c + 1], scale=1.0)
            P.add("sp", lambda e, csl=csl: e.dma_start(out=t2[:], in_=C.vfirstT[csl, :]), writes=[t2.name], dma=True)
            V(P, lambda e: e.tensor_tensor(t2[:], t2[:], v_[:], ALU.subtract), [t2.name, v_.name], [t2.name])
            V(P, lambda e: e.tensor_tensor(t2[:], t2[:], t1[:], ALU.mult), [t2.name, t1.name], [t2.name])
            V(P, lambda e: e.tensor_tensor(v_[:], v_[:], t2[:], ALU.add), [t2.name, v_.name], [v_.name])
        else:
            P.add("sp", lambda e, csl=csl: e.dma_start(out=C.vfirstT[csl, :], in_=v_[:]), reads=[v_.name], writes=[("vfirst", c)], dma=True)
        A(P, vo[:], v_[:], AF.Copy, [v_.name], [vo.name])
        P.add("sp", lambda e, csl=csl: e.dma_start(out=C.rw_v[csl, :], in_=vo[:]), reads=[vo.name], writes=[("rw_v", c)], dma=True)
        V(P, lambda e: e.tensor_scalar(lw[:], lw[:], DECAY_K, None, ALU.mult), [lw.name], [lw.name])
        V(P, lambda e: e.tensor_tensor_scan(t1[:], rmask[:], lw[:], 0.0, ALU.mult, ALU.add), [rmask.name, lw.name], [t1.name])
        V(P, lambda e: e.tensor_copy(wc[:], c3(t1)[:, :, 63]), [t1.name], [wc.name])
        A(P, wc[:], wc[:], AF.Exp, [wc.name], [wc.name])
        P.add("sp", lambda e, csl=csl: e.dma_start(out=C.rw_wc[csl, :], in_=wc[:]), reads=[wc.name], writes=[("rw_wc", c)], dma=True)
        V(P, lambda e, c=c: e.tensor_scalar(kk[:], k_[:], C.pcol("rwkv_k_k")[:, c:c + 1], None, ALU.mult), [k_.name, "params"], [kk.name])
        A(P, t2[:], kk[:], AF.Square, [kk.name], [t2.name])
        for gq in range(4):
            tsl = slice(gq * 512, gq * 512 + 512)
            pt = nps()
            P.add("pe", lambda e, pt=pt, tsl=tsl: e.matmul(pt[:], C.blk64, t2[:, tsl], start=True, stop=True),
                  reads=[t2.name, "const"], writes=[pt.name])
            A(P, t3[:, tsl], pt[:], AF.Sqrt, [pt.name], [t3.name])
        V(P, lambda e: e.tensor_scalar(t3[:], t3[:], 1e-12, None, ALU.max), [t3.name], [t3.name])
        V(P, lambda e: e.reciprocal(t3[:], t3[:]), [t3.name], [t3.name])
        V(P, lambda e: e.tensor_tensor(kk[:], kk[:], t3[:], ALU.mult), [kk.name, t3.name], [kk.name])
        V(P, lambda e, c=c: e.tensor_scalar(t2[:], a_[:], 1.0, C.pcol("rwkv_k_a")[:, c:c + 1], ALU.subtract, ALU.mult), [a_.name, "params"], [t2.name])
        V(P, lambda e: e.scalar_tensor_tensor(k_[:], t2[:], 1.0, k_[:], ALU.add, ALU.mult), [t2.name, k_.name], [k_.name])
        V(P, lambda e, c=c: e.scalar_tensor_tensor(t2[:], r_[:], C.pcol("rwkv_r_k")[:, c:c + 1], k_[:], ALU.mult, ALU.mult),
          [r_.name, k_.name, "params"], [t2.name])
        for gq in range(4):
            tsl = slice(gq * 512, gq * 512 + 512)
            pt = nps()
            P.add("pe", lambda e, pt=pt, tsl=tsl: e.matmul(pt[:], C.blk64, t2[:, tsl], start=True, stop=True),
                  reads=[t2.name, "const"], writes=[pt.name])
            V(P, lambda e, pt=pt, tsl=tsl: e.tensor_tensor(t3[:, tsl], pt[:], v_[:, tsl], ALU.mult), [pt.name, v_.name], [t3.name])
        P.add("sp", lambda e, csl=csl: e.dma_start(out=C.rw_bonus[csl, :], in_=t3[:]), reads=[t3.name], writes=[("rw_bonus", c)], dma=True)
        A(P, t2[:], t1[:], AF.Exp, [t1.name], [t2.name])
        V(P, lambda e: e.tensor_tensor(c3(ar)[:, :, 64:128] if False else ar[:, :, 64:128], c3(r_), c3(t2), ALU.mult), [r_.name, t2.name], [ar.name])
        V(P, lambda e: e.tensor_tensor(t2[:], t1[:], lw[:], ALU.subtract), [t1.name, lw.name], [t2.name])
        A(P, t2[:], t2[:], AF.Exp, [t2.name], [t2.name])
        V(P, lambda e: e.scalar_tensor_tensor(ar[:, :, 0:64], c3(kk), -1.0, c3(t2), ALU.mult, ALU.mult), [kk.name, t2.name], [ar.name])
        A(P, t2[:], t1[:], AF.Exp, [t1.name], [t2.name], scale=-1.0)
        V(P, lambda e: e.tensor_tensor(kbt[:, :, 0:64], c3(k_), c3(t2), ALU.mult), [k_.name, t2.name], [kbt.name])
        V(P, lambda e: e.tensor_tensor(t3[:], kk[:], a_[:], ALU.mult), [kk.name, a_.name], [t3.name])
        V(P, lambda e: e.tensor_tensor(kbt[:, :, 64:128], c3(t3), c3(t2), ALU.mult), [t3.name, t2.name], [kbt.name])
        P.add("sp", lambda e, csl=csl: e.dma_start(out=C.rw_ar[csl, :], in_=ar[:].rearrange("p q x -> p (q x)")), reads=[ar.name],
              writes=[("rw_ar", c)], dma=True)
        P.add("sp", lambda e, csl=csl: e.dma_start(out=C.rw_kb[csl, :], in_=kbt[:].rearrange("p q x -> p (q x)")), reads=[kbt.name],
              writes=[("rw_kb", c)], dma=True)
    P.release(m)


def phase_rwkv_chunks(P, C, li):
    m = P.mark()
    NQ = T // 64
    ident_bf = P.sb("rc_ident", [128, 128], BF16)
    A(P, ident_bf[:], C.ident, AF.Copy, ["const"], [ident_bf.name])
    id64 = ident_bf[0:64, 0:64]
    mask1 = C.consts[0:64, 1280:1792]
    mask2 = C.consts[0:64, 1792:2048]
    gneps = C.consts[0:64, 258:259]
    wcs = P.sb("rc_wcs", [64, 16, NQ], F32)
    P.add("sp", lambda e: e.dma_start(out=wcs[:], in_=C.rw_wc.rearrange("(hd j) q -> j hd q", j=64)), writes=[wcs.name], dma=True)
    STf = P.sb("rc_STf", [64, 16, 64], F32)
    STb = P.sb("rc_STb", [64, 16, 64], BF16)
    V(P, lambda e: e.memset(STf[:], 0.0), [], [STf.name])
    V(P, lambda e: e.memset(STb[:], 0.0), [], [STb.name])
    ARw = P.sb("rc_AR", [64, 16, 1024], BF16)
    KBw = P.sb("rc_KB", [64, 16, 1024], BF16)
    Vw = P.sb("rc_V", [64, 16, 512], BF16)
    Gw = P.sb("rc_G", [128, 8, 512], F32)
    Bw = P.sb("rc_B", [128, 8, 512], F32)
    Yw = P.sb("rc_Y", [128, 8, 512], BF16)
    psF = [P.ps("rc_psF%d" % i, [128, 512], F32) for i in range(6)]
    psB = [P.ps("rc_psB%d" % i, [128, 1024], BF16) for i in range(2)]
    kf = [0]
    kbn = [0]

    def nF():
        kf[0] += 1
        return psF[kf[0] % 6]

    def nB():
        kbn[0] += 1
        return psB[kbn[0] % 2]

    NR = 3
    rot = {}

    def sbr(nm, shape, dt):
        if nm not in rot:
            rot[nm] = [[P.sb("rc_%s%d" % (nm, i), shape, dt) for i in range(NR)], 0]
        rot[nm][1] += 1
        return rot[nm][0][rot[nm][1] % NR]

    def mm(out, lhsT, rhs, rd, wr, start=True, stop=True):
        P.add("pe", lambda e: e.matmul(out, lhsT, rhs, start=start, stop=stop), reads=rd, writes=wr)

    ar_v = C.rw_ar.rearrange("(hd j) x -> j hd x", j=64)
    kb_v = C.rw_kb.rearrange("(hd j) x -> j hd x", j=64)
    v_v = C.rw_v.rearrange("(hd j) t -> j hd t", j=64)
    g_v = C.rw_g.rearrange("(c p) t -> p c t", p=128)
    bo_v = C.rw_bonus.rearrange("(c p) t -> p c t", p=128)
    yd_v = C.ydT.rearrange("(c p) t -> p c t", p=128)
    gnw = C.pcol("rwkv_gn_w")
    gnb = C.pcol("rwkv_gn_b")
    for w in range(4):
        P.add("sp", lambda e, w=w: e.dma_start(out=ARw[:], in_=ar_v[:, :, w * 1024:(w + 1) * 1024]), writes=[ARw.name], dma=True)
        P.add("sp", lambda e, w=w: e.dma_start(out=KBw[:], in_=kb_v[:, :, w * 1024:(w + 1) * 1024]), writes=[KBw.name], dma=True)
        P.add("sp", lambda e, w=w: e.dma_start(out=Vw[:], in_=v_v[:, :, w * 512:(w + 1) * 512]), writes=[Vw.name], dma=True)
        P.add("sp", lambda e, w=w: e.dma_start(out=Gw[:], in_=g_v[:, :, w * 512:(w + 1) * 512]), writes=[Gw.name], dma=True)
        P.add("sp", lambda e, w=w: e.dma_start(out=Bw[:], in_=bo_v[:, :, w * 512:(w + 1) * 512]), writes=[Bw.name], dma=True)
        for ql in range(8):
            q = w * 8 + ql
            for hg in range(4):
                hds = [4 * hg + k for k in range(4)]
                At = [ARw[:, hd, ql * 128:ql * 128 + 64] for hd in hds]
                Rt = [ARw[:, hd, ql * 128 + 64:ql * 128 + 128] for hd in hds]
                AtRt = [ARw[:, hd, ql * 128:ql * 128 + 128] for hd in hds]
                Kt = [KBw[:, hd, ql * 128:ql * 128 + 64] for hd in hds]
                Bt = [KBw[:, hd, ql * 128 + 64:ql * 128 + 128] for hd in hds]
                Vt = [Vw[:, hd, ql * 64:ql * 64 + 64] for hd in hds]
                p1, p2, p3 = nF(), nF(), nF()
                for k in range(4):
                    mm(p1[0:64, k * 128:(k + 1) * 128], Bt[k], AtRt[k], [KBw.name, ARw.name], [p1.name])
                    mm(p2[0:64, k * 128:(k + 1) * 128], Kt[k], AtRt[k], [KBw.name, ARw.name], [p2.name])
                    mm(p3[0:64, k * 64:(k + 1) * 64], At[k], Bt[k], [KBw.name, ARw.name], [p3.name])
                s1 = sbr("s1", [64, 512], BF16)
                s2 = sbr("s2", [64, 512], BF16)
                xt0 = sbr("xt0", [64, 256], BF16)
                V(P, lambda e, s1=s1, p1=p1: e.tensor_tensor(s1[:], p1[0:64, :], mask1, ALU.mult), [p1.name, "const"], [s1.name])
                V(P, lambda e, s2=s2, p2=p2: e.tensor_tensor(s2[:], p2[0:64, :], mask1, ALU.mult), [p2.name, "const"], [s2.name])
                V(P, lambda e, xt0=xt0, p3=p3: e.tensor_tensor(xt0[:], p3[0:64, 0:256], mask2, ALU.mult), [p3.name, "const"], [xt0.name])
                if C.cfg.get("rc_stage", 9) <= 1:
                    continue
                RB = [s1[:, k * 128 + 64:k * 128 + 128] for k in range(4)]
                AK = [s2[:, k * 128:k * 128 + 64] for k in range(4)]
                RK = [s2[:, k * 128 + 64:k * 128 + 128] for k in range(4)]
                Xp = [([s1[:, k * 128:k * 128 + 64] for k in range(4)], s1.name)]
                XTp = [([xt0[:, k * 64:k * 64 + 64] for k in range(4)], xt0.name)]
                for lev in range(1, 6):
                    (xs_, xn), (xts_, xtn) = Xp[-1], XTp[-1]
                    px = nF()
                    for k in range(4):
                        mm(px[0:64, k * 64:(k + 1) * 64], xts_[k], xs_[k], [xn, xtn], [px.name])
                    xnew = sbr("xp%d" % lev, [64, 256], BF16)
                    P.add("act", lambda e, xnew=xnew, px=px: e.copy(xnew[:], px[0:64, 0:256]), reads=[px.name], writes=[xnew.name])
                    Xp.append(([xnew[:, k * 64:(k + 1) * 64] for k in range(4)], xnew.name))
                    if lev < 5:
                        pxt = nF()
                        for k in range(4):
                            mm(pxt[0:64, k * 64:(k + 1) * 64], xs_[k], xts_[k], [xn, xtn], [pxt.name])
                        xtnew = sbr("xtp%d" % lev, [64, 256], BF16)
                        V(P, lambda e, xtnew=xtnew, pxt=pxt: e.tensor_copy(xtnew[:], pxt[0:64, 0:256]), [pxt.name], [xtnew.name])
                        XTp.append(([xtnew[:, k * 64:(k + 1) * 64] for k in range(4)], xtnew.name))
                if C.cfg.get("rc_stage", 9) <= 2:
                    continue
                pt = nB()
                for k in range(4):
                    P.add("pe", lambda e, pt=pt, k=k, Kt=Kt: e.transpose(pt[0:64, k * 64:(k + 1) * 64], Kt[k], id64), reads=[KBw.name, ident_bf.name], writes=[pt.name])
                    P.add("pe", lambda e, pt=pt, k=k, Bt=Bt: e.transpose(pt[0:64, 256 + k * 64:256 + (k + 1) * 64], Bt[k], id64), reads=[KBw.name, ident_bf.name], writes=[pt.name])
                    P.add("pe", lambda e, pt=pt, k=k, Vt=Vt: e.transpose(pt[0:64, 512 + k * 64:512 + (k + 1) * 64], Vt[k], id64), reads=[Vw.name, ident_bf.name], writes=[pt.name])
                tok = sbr("tok", [64, 768], BF16)
                P.add("act", lambda e, tok=tok, pt=pt: e.copy(tok[:], pt[0:64, 0:768]), reads=[pt.name], writes=[tok.name])
                Ktok = [tok[:, k * 64:(k + 1) * 64] for k in range(4)]
                Btok = [tok[:, 256 + k * 64:256 + (k + 1) * 64] for k in range(4)]
                Vtok = [tok[:, 512 + k * 64:512 + (k + 1) * 64] for k in range(4)]
                if C.cfg.get("rc_stage", 9) <= 3:
                    continue
                pu = nF()
                for k in range(4):
                    mm(pu[0:64, k * 64:(k + 1) * 64], At[k], STb[:, hds[k], :], [ARw.name, STb.name], [pu.name], True, False)
                    mm(pu[0:64, k * 64:(k + 1) * 64], AK[k], Vtok[k], [s2.name, tok.name], [pu.name], False, True)
                Uf = sbr("Uf", [64, 256], F32)
                Ub = sbr("Ub", [64, 256], BF16)
                V(P, lambda e, Uf=Uf, pu=pu: e.tensor_copy(Uf[:], pu[0:64, 0:256]), [pu.name], [Uf.name])
                P.add("act", lambda e, Ub=Ub, Uf=Uf: e.copy(Ub[:], Uf[:]), reads=[Uf.name], writes=[Ub.name])
                for lev in range(6):
                    xs_, xn = Xp[lev]
                    pu2 = nF()
                    for k in range(4):
                        mm(pu2[0:64, k * 64:(k + 1) * 64], xs_[k], Ub[:, k * 64:(k + 1) * 64], [xn, Ub.name], [pu2.name])
                    V(P, lambda e, Uf=Uf, pu2=pu2: e.tensor_tensor(Uf[:], Uf[:], pu2[0:64, 0:256], ALU.add), [Uf.name, pu2.name], [Uf.name])
                    P.add("act", lambda e, Ub=Ub, Uf=Uf: e.copy(Ub[:], Uf[:]), reads=[Uf.name], writes=[Ub.name])
                if C.cfg.get("rc_stage", 9) <= 4:
                    continue
                py = nF()
                for k in range(4):
                    o = py[0:64, k * 64:(k + 1) * 64]
                    mm(o, Rt[k], STb[:, hds[k], :], [ARw.name, STb.name], [py.name], True, False)
                    mm(o, RB[k], Ub[:, k * 64:(k + 1) * 64], [s1.name, Ub.name], [py.name], False, False)
                    mm(o, RK[k], Vtok[k], [s2.name, tok.name], [py.name], False, True)
                pst = nF()
                for k in range(4):
                    o = pst[0:64, k * 64:(k + 1) * 64]
                    mm(o, Btok[k], Ub[:, k * 64:(k + 1) * 64], [tok.name, Ub.name], [pst.name], True, False)
                    mm(o, Ktok[k], Vtok[k], [tok.name], [pst.name], False, True)
                stv = STf[:, 4 * hg:4 * hg + 4, :]
                V(P, lambda e, stv=stv, pst=pst: e.tensor_tensor(stv, stv, pst[0:64, 0:256].rearrange("p (h i) -> p h i", i=64), ALU.add),
                  [STf.name, pst.name], [STf.name])
                V(P, lambda e, stv=stv, hg=hg, q=q: e.tensor_tensor(stv, stv, wcs[:, 4 * hg:4 * hg + 4, q:q + 1].to_broadcast([64, 4, 64]), ALU.mult),
                  [STf.name, wcs.name], [STf.name])
                P.add("act", lambda e, stv=stv, hg=hg: e.copy(STb[:, 4 * hg:4 * hg + 4, :], stv), reads=[STf.name], writes=[STb.name])
                if C.cfg.get("rc_stage", 9) <= 5:
                    continue
                ysb = sbr("ysb", [64, 256], F32)
                ysq = sbr("ysq", [64, 256], F32)
                st = sbr("st", [64, 16], F32)
                yn = sbr("yn", [64, 256], BF16)
                P.add("act", lambda e, ysb=ysb, py=py: e.copy(ysb[:], py[0:64, 0:256]), reads=[py.name], writes=[ysb.name])
                y3 = ysb[:].rearrange("p (h i) -> p h i", i=64)
                A(P, ysq[:], ysb[:], AF.Square, [ysb.name], [ysq.name])
                V(P, lambda e, st=st, y3=y3: e.tensor_reduce(st[:, 0:4], y3, AX.X, ALU.add), [ysb.name], [st.name])
                V(P, lambda e, st=st, ysq=ysq: e.tensor_reduce(st[:, 4:8], ysq[:].rearrange("p (h i) -> p h i", i=64), AX.X, ALU.add), [ysq.name], [st.name])
                V(P, lambda e, st=st: e.tensor_scalar(st[:, 0:4], st[:, 0:4], 1.0 / 64, None, ALU.mult), [st.name], [st.name])
                V(P, lambda e, st=st: e.tensor_tensor(st[:, 8:12], st[:, 0:4], st[:, 0:4], ALU.mult), [st.name], [st.name])
                V(P, lambda e, st=st: e.scalar_tensor_tensor(st[:, 4:8], st[:, 4:8], 1.0 / 64, st[:, 8:12], ALU.mult, ALU.subtract), [st.name], [st.name])
                A(P, st[:, 4:8], st[:, 4:8], AF.Sqrt, [st.name, "const"], [st.name], bias=gneps, scale=1.0)
                V(P, lambda e, st=st: e.reciprocal(st[:, 4:8], st[:, 4:8]), [st.name], [st.name])
                V(P, lambda e, st=st, y3=y3: e.tensor_tensor(y3, y3, st[:, 0:4].unsqueeze(2).to_broadcast([64, 4, 64]), ALU.subtract), [st.name, ysb.name], [ysb.name])
                V(P, lambda e, st=st, y3=y3, yn=yn: e.tensor_tensor(yn[:].rearrange("p (h i) -> p h i", i=64), y3,
                                                                   st[:, 4:8].unsqueeze(2).to_broadcast([64, 4, 64]), ALU.mult), [st.name, ysb.name], [yn.name])
                if C.cfg.get("rc_stage", 9) <= 6:
                    continue
                pyt = nB()
                for pp in range(2):
                    P.add("pe", lambda e, pyt=pyt, pp=pp, yn=yn: e.transpose(pyt[:, pp * 64:(pp + 1) * 64], yn[:, pp * 128:(pp + 1) * 128], id64),
                          reads=[yn.name, ident_bf.name], writes=[pyt.name])
                yf = sbr("yf", [128, 128], F32)
                for pp in range(2):
                    c = 2 * hg + pp
                    tw = slice(ql * 64, ql * 64 + 64)
                    V(P, lambda e, yf=yf, pyt=pyt, pp=pp, c=c: e.tensor_scalar(yf[:, pp * 64:(pp + 1) * 64], pyt[:, pp * 64:(pp + 1) * 64],
                                                                               gnw[:, c:c + 1], gnb[:, c:c + 1], ALU.mult, ALU.add),
                      [pyt.name, "params"], [yf.name])
                    V(P, lambda e, yf=yf, pp=pp, c=c, tw=tw: e.tensor_tensor(yf[:, pp * 64:(pp + 1) * 64], yf[:, pp * 64:(pp + 1) * 64], Bw[:, c, tw], ALU.add),
                      [yf.name, Bw.name], [yf.name])
                    V(P, lambda e, yf=yf, pp=pp, c=c, tw=tw: e.tensor_tensor(Yw[:, c, tw], yf[:, pp * 64:(pp + 1) * 64], Gw[:, c, tw], ALU.mult),
                      [yf.name, Gw.name], [Yw.name])
        P.add("sp", lambda e, w=w: e.dma_start(out=yd_v[:, :, w * 512:(w + 1) * 512], in_=Yw[:]), reads=[Yw.name], writes=[("ydT", w)], dma=True)
    P.release(m)


def load_hT(P, C, tag):
    hT = P.sb(tag + "_hT", [128, NCH, T], BF16)
    P.add("sp", lambda e: e.dma_start(out=hT[:], in_=C.hT_d.rearrange("(c p) t -> p c t", p=128)), writes=[hT.name], dma=True)
    return hT


def phase_gates(P, C, li):
    m = P.mark()
    hT = load_hT(P, C, "gt")
    proj_from_hT(P, C, hT, C.W[li]["w_merge_gate"], 4 * D, C.gatesT, "gt", None, func=AF.Sigmoid)
    P.release(m)


def phase_merge(P, C, li):
    m = P.mark()
    W = C.W[li]
    br = (("w_branch_a", C.yaT, 8), ("w_branch_b", C.ybT, 4), ("w_branch_c", C.ycT, 8), ("w_branch_d", C.ydT, 8))
    ysb = [P.sb("mg_y%d" % k, [128, kc, 512], BF16) for k, (_, _, kc) in enumerate(br)]
    wsb = [[P.sb("mg_w%d_%d" % (k, i), [128, kc, 256], BF16) for k, (_, _, kc) in enumerate(br)] for i in range(2)]
    gts = [P.sb("mg_g%d" % i, [128, 512], F32) for i in range(4)]
    tmps = [P.sb("mg_t%d" % i, [128, 512], F32) for i in range(3)]
    macc = [P.sb("mg_acc%d" % i, [128, 512], F32) for i in range(2)]
    outs = [P.sb("mg_o%d" % i, [128, 512], BF16) for i in range(2)]
    pss = [P.ps("mg_ps%d" % i, [128, 512], F32) for i in range(6)]
    kp = 0
    kg = 0
    kt = 0
    for g in range(4):
        tsl = slice(g * 512, g * 512 + 512)
        for k, (_, ysrc, kc) in enumerate(br):
            P.add("sp", lambda e, k=k, ysrc=ysrc, tsl=tsl: e.dma_start(out=ysb[k][:], in_=ysrc.rearrange("(c p) t -> p c t", p=128)[:, :, tsl]),
                  writes=[ysb[k].name], dma=True)
        for nb in range(8):
            wset = wsb[nb % 2]
            for k, (wn, _, kc) in enumerate(br):
                P.add("pool", lambda e, k=k, wn=wn, wset=wset, nb=nb: e.dma_start(out=wset[k][:], in_=W[wn].rearrange("(c p) n -> p c n", p=128)[:, :, nb * 256:(nb + 1) * 256]),
                      writes=[wset[k].name], dma=True)
            for j in range(2):
                n = nb * 2 + j
                acc = macc[n % 2]
                for k, (_, _, kc) in enumerate(br):
                    pt = pss[kp % 6]
                    kp += 1
                    for c in range(kc):
                        P.add("pe", lambda e, pt=pt, k=k, c=c, j=j, wset=wset, kc=kc: e.matmul(pt[:], wset[k][:, c, j * 128:(j + 1) * 128], ysb[k][:, c, :],
                                                                                         start=(c == 0), stop=(c == kc - 1)),
                              reads=[wset[k].name, ysb[k].name], writes=[pt.name])
                    gt = gts[kg % 4]
                    kg += 1
                    P.add("sp", lambda e, gt=gt, k=k, n=n, tsl=tsl: e.dma_start(out=gt[:], in_=C.gatesT[k * D + n * 128:k * D + n * 128 + 128, tsl]),
                          writes=[gt.name], dma=True)
                    if k == 0:
                        V(P, lambda e, acc=acc, pt=pt, gt=gt: e.tensor_tensor(acc[:], pt[:], gt[:], ALU.mult), [pt.name, gt.name], [acc.name])
                    else:
                        tm = tmps[kt % 3]
                        kt += 1
                        V(P, lambda e, tm=tm, pt=pt, gt=gt: e.tensor_tensor(tm[:], pt[:], gt[:], ALU.mult), [pt.name, gt.name], [tm.name])
                        P.add("pool", lambda e, acc=acc, tm=tm: e.tensor_tensor(acc[:], acc[:], tm[:], ALU.add), reads=[acc.name, tm.name], writes=[acc.name])
                ob = outs[n % 2]
                A(P, ob[:], acc[:], AF.Copy, [acc.name], [ob.name])
                P.add("sp", lambda e, ob=ob, n=n, tsl=tsl: e.dma_start(out=C.mergedT[n * 128:(n + 1) * 128, tsl], in_=ob[:]), reads=[ob.name],
                      writes=[("mergedT", n, g)], dma=True)
    P.release(m)


def proj_norm_residual(P, C, aT, KC, w, gcol, x_in, x_out, tag, gate=None, krows=None):
    m = P.mark()
    a_sb = P.sb(tag + "_a", [128, KC, 512], BF16)
    wsb = [P.sb("%s_w%d" % (tag, i), [128, KC, 256], BF16) for i in range(2)]
    mo = P.sb(tag + "_mo", [128, NCH, 512], F32)
    sqs = [P.sb("%s_sq%d" % (tag, i), [128, 512], F32) for i in range(2)]
    xs = [P.sb("%s_x%d" % (tag, i), [128, 512], F32) for i in range(3)]
    gts = [P.sb("%s_g%d" % (tag, i), [128, 512], F32) for i in range(2)] if gate is not None else None
    rstd = P.sb(tag + "_rstd", [128, 512], F32)
    pss = [P.ps("%s_ps%d" % (tag, i), [128, 512], F32) for i in range(4)]
    ss = P.ps(tag + "_ss", [128, 512], F32)
    av = aT.rearrange("(c p) t -> p c t", p=128)
    wv = w.rearrange("(c p) n -> p c n", p=128)
    kp = 0
    for g in range(4):
        tsl = slice(g * 512, g * 512 + 512)
        P.add("pool", lambda e, tsl=tsl: e.dma_start(out=a_sb[:], in_=av[:, :, tsl]), writes=[a_sb.name], dma=True)
        for nb in range(8):
            wt = wsb[nb % 2]
            P.add("pool", lambda e, wt=wt, nb=nb: e.dma_start(out=wt[:], in_=wv[:, :, nb * 256:(nb + 1) * 256]), writes=[wt.name], dma=True)
            for j in range(2):
                n = nb * 2 + j
                pt = pss[kp % 4]
                kp += 1
                for c in range(KC):
                    P.add("pe", lambda e, pt=pt, wt=wt, c=c, j=j: e.matmul(pt[:], wt[:, c, j * 128:(j + 1) * 128], a_sb[:, c, :], start=(c == 0), stop=(c == KC - 1)),
                          reads=[wt.name, a_sb.name], writes=[pt.name])
                if gate is not None:
                    gt = gts[n % 2]
                    P.add("sp", lambda e, gt=gt, n=n, tsl=tsl: e.dma_start(out=gt[:], in_=gate[n * 128:(n + 1) * 128, tsl]), writes=[gt.name], dma=True)
                    V(P, lambda e, pt=pt, gt=gt, n=n: e.tensor_tensor(mo[:, n, :], pt[:], gt[:], ALU.mult), [pt.name, gt.name], [mo.name])
                elif n % 2 == 0:
                    V(P, lambda e, pt=pt, n=n: e.tensor_copy(mo[:, n, :], pt[:]), [pt.name], [mo.name])
                else:
                    P.add("act", lambda e, pt=pt, n=n: e.copy(mo[:, n, :], pt[:]), reads=[pt.name], writes=[mo.name])
                sq = sqs[n % 2]
                A(P, sq[:], mo[:, n, :], AF.Square, [mo.name], [sq.name])
                P.add("pe", lambda e, sq=sq, n=n: e.matmul(ss[:], C.ones_invD, sq[:], start=(n == 0), stop=(n == NCH - 1)),
                      reads=[sq.name, "const"], writes=[ss.name])
        A(P, rstd[:], ss[:], AF.Sqrt, [ss.name, "const"], [rstd.name], bias=C.eps_col, scale=1.0)
        V(P, lambda e: e.reciprocal(rstd[:], rstd[:]), [rstd.name], [rstd.name])
        for n in range(NCH):
            xt = xs[n % 3]
            P.add("sp", lambda e, xt=xt, n=n, tsl=tsl: e.dma_start(out=xt[:], in_=x_in[n * 128:(n + 1) * 128, tsl]), writes=[xt.name], dma=True)
            V(P, lambda e, n=n: e.scalar_tensor_tensor(mo[:, n, :], mo[:, n, :], gcol[:, n:n + 1], rstd[:], ALU.mult, ALU.mult),
              [mo.name, rstd.name, "params"], [mo.name])
            P.add("pool", lambda e, xt=xt, n=n: e.tensor_tensor(xt[:], xt[:], mo[:, n, :], ALU.add), reads=[xt.name, mo.name], writes=[xt.name])
            P.add("sp", lambda e, xt=xt, n=n, tsl=tsl: e.dma_start(out=x_out[n * 128:(n + 1) * 128, tsl], in_=xt[:]), reads=[xt.name],
                  writes=[(tag, "xo", n, g)], dma=True)
    P.release(m)


def phase_mixout(P, C, li):
    proj_norm_residual(P, C, C.mergedT, NCH, C.W[li]["w_out"], C.pcol("norm_mix_post"), C.x0, C.x1, "mo")


def phase_ffn_up(P, C, li):
    m = P.mark()
    hT = P.sb("fu_hT", [128, NCH, T], BF16)
    rmsnorm_hT(P, C, C.x1, C.pcol("norm_ffn_pre"), hT, "n2")
    wv = C.W[li]["w_ffn_up"].rearrange("(c p) n -> p c n", p=128)
    wsb = [[P.sb("fu_w%d_%d" % (i, s_), [128, NCH, 256], BF16) for s_ in range(2)] for i in range(2)]
    raw = [[P.sb("fu_raw%d_%d" % (i, s_), [128, T], F32) for s_ in range(2)] for i in range(2)]
    cv = [P.sb("fu_cv%d" % s_, [128, T], F32) for s_ in range(2)]
    t1 = P.sb("fu_t1", [128, T], F32)
    t2 = P.sb("fu_t2", [128, T], F32)
    ao = [P.sb("fu_ao%d" % i, [128, T], BF16) for i in range(2)]
    pss = [P.ps("fu_ps%d" % i, [128, 512], F32) for i in range(6)]
    cw = [C.pcol("ffn_conv_w%d" % k) for k in range(3)]
    cb = C.pcol("ffn_conv_b")
    kp = 0
    NP = D_FF // 128
    for nb in range(NP // 2):
        wt = wsb[nb % 2]
        for s_ in range(2):
            col0 = s_ * D_FF + nb * 256
            P.add("pool", lambda e, wt=wt, s_=s_, col0=col0: e.dma_start(out=wt[s_][:], in_=wv[:, :, col0:col0 + 256]), writes=[wt[s_].name], dma=True)
        for j in range(2):
            n = nb * 2 + j
            rw = raw[n % 2]
            for s_ in range(2):
                for g in range(4):
                    tsl = slice(g * 512, g * 512 + 512)
                    pt = pss[kp % 6]
                    kp += 1
                    for c in range(NCH):
                        P.add("pe", lambda e, pt=pt, wt=wt, s_=s_, c=c, j=j, tsl=tsl: e.matmul(pt[:], wt[s_][:, c, j * 128:(j + 1) * 128], hT[:, c, tsl],
                                                                                          start=(c == 0), stop=(c == NCH - 1)),
                              reads=[wt[s_].name, hT.name], writes=[pt.name])
                    if kp % 2 == 0:
                        V(P, lambda e, pt=pt, rw=rw, s_=s_, tsl=tsl: e.tensor_copy(rw[s_][:, tsl], pt[:]), [pt.name], [rw[s_].name])
                    else:
                        P.add("act", lambda e, pt=pt, rw=rw, s_=s_, tsl=tsl: e.copy(rw[s_][:, tsl], pt[:]), reads=[pt.name], writes=[rw[s_].name])
                ch = s_ * NP + n
                x_ = rw[s_]
                o_ = cv[s_]
                eng = "dve"
                P.add(eng, lambda e, x_=x_, o_=o_, ch=ch: e.tensor_scalar(o_[:], x_[:], cw[2][:, ch:ch + 1], cb[:, ch:ch + 1], ALU.mult, ALU.add),
                      reads=[x_.name, "params"], writes=[o_.name])
                for kk_, sh in ((1, 1), (0, 2)):
                    P.add(eng, lambda e, x_=x_, o_=o_, ch=ch, kk_=kk_, sh=sh: e.scalar_tensor_tensor(o_[:, sh:], x_[:, :T - sh], cw[kk_][:, ch:ch + 1], o_[:, sh:],
                                                                                                   ALU.mult, ALU.add),
                          reads=[x_.name, o_.name, "params"], writes=[o_.name])
            ob = ao[n % 2]
            gelu_mul(P, C, cv[0][:], cv[1][:], ob[:], t1[:], t2[:], dict(g=cv[0].name, t1=t1.name, t2=t2.name, o=cv[1].name, out=ob.name))
            P.add("sp", lambda e, ob=ob, n=n: e.dma_start(out=C.actT[n * 128:(n + 1) * 128, :], in_=ob[:]), reads=[ob.name], writes=[("actT", n)], dma=True)
    P.release(m)


def phase_ffn_down(P, C, li):
    proj_norm_residual(P, C, C.actT, D_FF // 128, C.W[li]["w_ffn_down"], C.pcol("norm_ffn_post"), C.x1, C.x2, "fd")


def phase_ple(P, C, li):
    m = P.mark()
    hT = P.sb("pl_hT", [128, NCH, T], BF16)
    rmsnorm_hT(P, C, C.x2, C.pcol("norm_ple_pre"), hT, "n3")
    proj_from_hT(P, C, hT, C.W[li]["w_ple_gate"], D, C.gatesT[0:D, :], "pg", None, func=AF.Sigmoid)
    P.release(m)
    proj_norm_residual(P, C, C.pT_in[li], 2, C.W[li]["w_ple"], C.pcol("norm_ple_post"), C.x2, C.x3, "pl", gate=C.gatesT[0:D, :])


SCRATCH = {
    "projA": ([2048, T], F32),
    "projB": ([3072, T], F32),
    "projC": ([2048, T], F32),
    "projD": ([RW_IN, T], F32),
    "vtokB": ([T, 1536], BF16),
    "vtokC": ([T, 1024], BF16),
    "hT_d": ([D, T], BF16),
    "yaT": ([LRU_W, T], BF16),
    "ybT": ([512, T], BF16),
    "ycT": ([SB_W, T], BF16),
    "ydT": ([RW_W, T], BF16),
    "vfirstT": ([RW_W, T], F32),
    "gatesT": ([4 * D, T], F32),
    "mergedT": ([D, T], BF16),
    "rw_ar": ([RW_W, 32 * 128], BF16),
    "rw_kb": ([RW_W, 32 * 128], BF16),
    "rw_v": ([RW_W, T], BF16),
    "rw_wc": ([RW_W, 32], F32),
    "rw_g": ([RW_W, T], F32),
    "rw_bonus": ([RW_W, T], F32),
    "xT_a": ([D, T], F32),
    "xT_b": ([D, T], F32),
    "actT": ([D_FF, T], BF16),
}

WEIGHTS = {
    "w_in": [D, N_IN], "w_merge_gate": [D, 4 * D], "lru_w_r": [LRU_W, 128], "lru_w_i": [LRU_W, 128],
    "rwkv_w_up": [64, RW_W], "rwkv_a_up": [64, RW_W], "rwkv_g_up": [160, RW_W],
    "rwkv_v_down": [RW_W, 32], "rwkv_v_up": [32, RW_W],
    "w_branch_a": [LRU_W, D], "w_branch_b": [512, D], "w_branch_c": [SB_W, D], "w_branch_d": [RW_W, D],
    "w_out": [D, D], "w_ffn_up": [D, 2 * D_FF], "w_ffn_down": [D_FF, D], "w_ple": [PLE, D], "w_ple_gate": [D, D],
}


def build_program(cfg):
    nc = bass.Bass("TRN2", target_bir_lowering=False)
    P = Prog(nc)
    C = Ctx()
    C.cfg = cfg
    dbg = cfg.get("debug_out", ())
    ext_in = cfg.get("ext_in", ())
    layers = cfg["layers"]
    phases = cfg.get("phases", None)

    def din(name, shape, dt=F32):
        return nc.dram_tensor(name, list(shape), dt, kind="ExternalInput").ap()

    NBc = cfg.get("nb", 1)
    sfx = lambda s_: "" if NBc == 1 else "_b%d" % s_
    xT_ins = [din("xT_in" + sfx(s_), [D, T]) for s_ in range(NBc)]
    pT_ins = [[din("pT%d%s" % (L, sfx(s_)), [PLE, T]) for L in layers] for s_ in range(NBc)]
    out_Ts = [nc.dram_tensor("out_T" + sfx(s_), [D, T], F32, kind="ExternalOutput").ap() for s_ in range(NBc)]
    C.xT_in, C.pT_in, C.out_T = xT_ins[0], pT_ins[0], out_Ts[0]
    need_w = cfg.get("weights", None)
    C.W = []
    for L in layers:
        d = {}
        for nm, shp in WEIGHTS.items():
            if need_w is None or nm in need_w:
                d[nm] = din("%s_%d" % (nm, L), shp)
        C.W.append(d)
    C.npar = cfg["npar"]
    C.params_d = [din("params%d" % L, [128, C.npar]) for L in layers]
    C.consts_d = din("consts", [128, NCONST])
    for nm, (shp, dt) in SCRATCH.items():
        kind = "ExternalOutput" if nm in dbg else ("ExternalInput" if nm in ext_in else "Internal")
        setattr(C, nm, nc.dram_tensor(nm, list(shp), dt, kind=kind).ap())

    C.params = P.sb("params", [128, C.npar], F32)
    C.consts = P.sb("consts_sb", [128, NCONST], F32)
    C.ones_invD = C.consts[:, 0:128]
    C.ident = C.consts[:, 128:256]
    C.eps_col = C.consts[:, 256:257]
    C.one_col = C.consts[:, 257:258]
    C.rmat = C.consts[:, 512:640]
    C.dilmask = C.consts[:, 640:896]
    C.sbmask = C.consts[:, 896:1024]
    C.ones1 = C.consts[:, 1024:1152]
    C.blk64 = C.consts[:, 1152:1280]
    C.rope_d = din("rope", [128, 3, T])
    C.poff = cfg["poff"]
    C.pcol = lambda nm: C.params[:, C.poff[nm][0]:C.poff[nm][0] + C.poff[nm][1]]
    P.add("sp", lambda e: e.dma_start(out=C.consts[:], in_=C.consts_d), writes=["const"], dma=True)

    bufs = [C.xT_a, C.xT_b]
    for s_ in range(NBc):
      C.xT_in, C.pT_in, C.out_T = xT_ins[s_], pT_ins[s_], out_Ts[s_]
      xcur = C.xT_in
      for li, L in enumerate(layers):
          P.add("sp", lambda e, li=li: e.dma_start(out=C.params[:], in_=C.params_d[li]), writes=["params"], dma=True)
          last = (li == len(layers) - 1)
          C.xT = xcur
          C.x0 = xcur
          C.x1 = bufs[li % 2]
          C.x2 = bufs[(li + 1) % 2]
          C.x3 = C.out_T if last else bufs[li % 2]
          if "x_override" in cfg:
              for k_, v_ in cfg["x_override"].items():
                  setattr(C, k_, getattr(C, v_))
          C.L = L
          C.li = li
          for ph in PHASES:
              if phases is None or ph.__name__ in phases:
                  ph(P, C, li)
          P.barrier()
          xcur = C.x3
    P.finish([])
    P.emit()
    return nc, P


NCONST = 2048


def layer_weights(inp, L):
    f = lambda a: np.ascontiguousarray(np.asarray(a, np.float32))
    d = {}
    for nm in ("w_in", "w_merge_gate", "rwkv_w_up", "rwkv_a_up", "rwkv_g_up", "w_branch_a", "w_branch_b", "w_branch_c",
               "w_branch_d", "w_out", "w_ffn_up", "w_ffn_down", "w_ple", "w_ple_gate"):
        d[nm] = f(inp[nm][L])
    d["lru_w_r"] = f(inp["lru_w_r"][L].reshape(LRU_W, 128))
    d["lru_w_i"] = f(inp["lru_w_i"][L].reshape(LRU_W, 128))
    if L > 0:
        d["rwkv_v_down"] = f(inp["rwkv_v_down"][L - 1])
        d["rwkv_v_up"] = f(inp["rwkv_v_up"][L - 1])
    else:
        d["rwkv_v_down"] = np.zeros((RW_W, 32), np.float32)
        d["rwkv_v_up"] = np.zeros((32, RW_W), np.float32)
    return d


def make_consts():
    c = np.zeros((128, NCONST), np.float32)
    c[:, 0:128] = 1.0 / D
    c[:, 128:256] = np.eye(128, dtype=np.float32)
    c[:, 256] = EPS
    c[:, 257] = 1.0
    j = np.arange(128)
    rm = np.zeros((128, 128), np.float32)
    rm[j[:64] + 64, j[:64]] = -1.0
    rm[j[64:] - 64, j[64:]] = 1.0
    c[:, 512:640] = rm
    c[:, 640:768] = (j[:, None] >= j[None, :])
    c[:, 768:896] = (j[None, :] >= j[:, None])
    c[:, 896:1024] = (j[None, :] < j[:, None])
    c[:, 1024:1152] = 1.0
    c[:, 258] = GN_EPS
    c[0:64, 1152:1216] = 1.0
    c[64:128, 1216:1280] = 1.0
    j64 = np.arange(64)
    strict = (j64[:, None] < j64[None, :]).astype(np.float32)
    incl = (j64[:, None] <= j64[None, :]).astype(np.float32)
    for k in range(4):
        c[0:64, 1280 + k * 128:1280 + k * 128 + 64] = strict
        c[0:64, 1280 + k * 128 + 64:1280 + k * 128 + 128] = incl
        c[0:64, 1792 + k * 64:1792 + (k + 1) * 64] = strict.T
    return c


def make_rope():
    half = 64
    inv = (10000.0 ** (-np.arange(half, dtype=np.float32) / half)).astype(np.float32)
    ang = np.arange(T, dtype=np.float32)[None, :] * inv[:, None]
    r = np.zeros((128, 3, T), np.float32)
    r[:, 2, :] = 1.0
    r[:, 2, 0::64] = 0.0
    r[:64, 0] = np.cos(ang); r[64:, 0] = np.cos(ang)
    r[:64, 1] = np.sin(ang); r[64:, 1] = np.sin(ang)
    return r


PHASES = [phase_inproj, phase_rglru, phase_dilated, phase_stickbreak, phase_rwkv_prep, phase_rwkv_chunks,
          phase_gates, phase_merge, phase_mixout, phase_ffn_up, phase_ffn_down, phase_ple]


NCORES = 2
NB = 4


def kernel(**inputs):
    inp = {k: np.asarray(v) for k, v in inputs.items()}
    B = inp["x"].shape[0]
    assert B == NCORES * NB
    pks = [pack_params(inp, L) for L in range(DEPTH)]
    cfg = dict(layers=list(range(DEPTH)), npar=pks[0].n, poff=pks[0].off, nb=NB)
    nc, _ = build_program(cfg)
    consts = make_consts()
    rope = make_rope()
    lws = [layer_weights(inp, L) for L in range(DEPTH)]
    sfx = lambda s_: "" if NB == 1 else "_b%d" % s_
    in_maps = []
    for c in range(NCORES):
        im = {"consts": consts, "rope": rope}
        for s_ in range(NB):
            b = c * NB + s_
            im["xT_in" + sfx(s_)] = np.ascontiguousarray(inp["x"][b].T)
            for L in range(DEPTH):
                im["pT%d%s" % (L, sfx(s_))] = np.ascontiguousarray(inp["p"][L, b].T)
        for L in range(DEPTH):
            im["params%d" % L] = pks[L].array()
            for nm, a in lws[L].items():
                im["%s_%d" % (nm, L)] = a
        in_maps.append(im)
    res = run_bass_kernel_spmd(nc, in_maps, core_ids=list(range(NCORES)))
    outs = []
    for c in range(NCORES):
        for s_ in range(NB):
            outs.append(np.ascontiguousarray(np.asarray(res.results[c]["out_T" + sfx(s_)]).T))
    return np.stack(outs, axis=0).astype(np.float32)
```

```python
import numpy as np
import concourse.bass as bass
import concourse.mybir as mybir
from concourse.bass_utils import run_bass_kernel_spmd

F32 = mybir.dt.float32
BF16 = mybir.dt.bfloat16
AF = mybir.ActivationFunctionType
ALU = mybir.AluOpType
AX = mybir.AxisListType

D = 2048
T = 2048
DEPTH = 2
NCH = D // 128
LRU_W = 1024
DIL_W = 1536
SB_W = 1024
RW_W = 1024
RW_IN = 3360
OFF_B = 2048
OFF_C = OFF_B + 3 * DIL_W
OFF_D = OFF_C + 3 * SB_W
N_IN = OFF_D + RW_IN
D_FF = 5632
PLE = 256
EPS = 1e-6


class Tok:
    __slots__ = ("sem", "val", "known", "eng", "dma")

    def __init__(self, sem, val, known, eng, dma):
        self.sem, self.val, self.known, self.eng, self.dma = sem, val, known, eng, dma


class Prog:
    ENGS = ("pe", "act", "dve", "pool", "sp")
    NSLOT = {"sp": 24, "pool": 24, "act": 8}

    def __init__(self, nc):
        self.nc = nc
        self.ops = {e: [] for e in self.ENGS}
        self.cnt = {e: 0 for e in self.ENGS}
        self.known = {e: {} for e in self.ENGS}
        self.snap = {e: None for e in self.ENGS}
        self.res_w = {}
        self.res_r = {}
        self.slot_uses = {e: [0] * n for e, n in self.NSLOT.items()}
        self.slot_next = {e: 0 for e in self.NSLOT}
        self.slot_last = {e: [None] * n for e, n in self.NSLOT.items()}
        self.sb_mark = None
        self.n_ops = 0

    def sb(self, name, shape, dt):
        self.uid = getattr(self, "uid", 0) + 1
        return self.nc.alloc_sbuf_tensor("%s_u%d" % (name, self.uid), list(shape), dt)

    def ps(self, name, shape, dt=F32):
        self.uid = getattr(self, "uid", 0) + 1
        return self.nc.alloc_psum_tensor("%s_u%d" % (name, self.uid), list(shape), dt)

    def mark(self):
        return (self.nc.sbuf_base, self.nc.sbuf_top, self.nc.psum_base, self.nc.psum_top)

    def release(self, m):
        self.barrier()
        self.nc.sbuf_base, self.nc.sbuf_top, self.nc.psum_base, self.nc.psum_top = m

    def _snapshot(self, eng):
        if self.snap[eng] is None:
            self.snap[eng] = dict(self.known[eng])
        return self.snap[eng]

    def _learn(self, eng, tok):
        k = self.known[eng]
        changed = False
        if k.get(tok.sem, 0) < tok.val:
            k[tok.sem] = tok.val
            changed = True
        for s, v in tok.known.items():
            if k.get(s, 0) < v:
                k[s] = v
                changed = True
        if changed:
            self.snap[eng] = None

    def add(self, eng, fn, reads=(), writes=(), dma=False, extra=()):
        deps = list(extra)
        for r in reads:
            t = self.res_w.get(r)
            if t is not None:
                deps.append(t)
        for w in writes:
            t = self.res_w.get(w)
            if t is not None:
                deps.append(t)
            rr = self.res_r.get(w)
            if rr:
                deps.extend(rr.values())
        waits = {}
        for tok in deps:
            if tok.eng == eng and eng == "pe" and not tok.dma:
                continue
            if self.known[eng].get(tok.sem, 0) >= tok.val:
                continue
            if waits.get(tok.sem, 0) < tok.val:
                waits[tok.sem] = tok.val
            self._learn(eng, tok)
        if dma:
            i = self.slot_next[eng]
            self.slot_next[eng] = (i + 1) % self.NSLOT[eng]
            prev = self.slot_last[eng][i]
            if prev is not None and self.known[eng].get(prev.sem, 0) < prev.val:
                if waits.get(prev.sem, 0) < prev.val:
                    waits[prev.sem] = prev.val
                self._learn(eng, prev)
            self.slot_uses[eng][i] += 1
            sem = "d_%s_%d" % (eng, i)
            tok = Tok(sem, 16 * self.slot_uses[eng][i], self._snapshot(eng), eng, True)
            self.slot_last[eng][i] = tok
            inc = (sem, 16)
        else:
            self.cnt[eng] += 1
            tok = Tok("c_" + eng, self.cnt[eng], self._snapshot(eng), eng, False)
            inc = ("c_" + eng, 1)
        self.ops[eng].append((list(waits.items()), fn, inc))
        self.n_ops += 1
        for w in writes:
            self.res_w[w] = tok
            self.res_r[w] = {}
        for r in reads:
            d = self.res_r.setdefault(r, {})
            d[tok.sem if dma else eng] = tok
        return tok

    def barrier(self):
        toks = []
        for e in self.ENGS:
            if self.cnt[e] > 0:
                toks.append(Tok("c_" + e, self.cnt[e], {}, e, False))
        for e in self.NSLOT:
            for t in self.slot_last[e]:
                if t is not None:
                    toks.append(t)
        for e in self.ENGS:
            if not self.ops[e] and e not in ("sp",):
                pass
            ex = [t for t in toks if not (t.eng == e and not t.dma and False)]
            self.add(e, None, extra=ex)
        self.res_w.clear()
        self.res_r.clear()

    def finish(self, toks):
        self.add("sp", None, extra=list(toks))
        self.barrier()

    def emit(self):
        nc = self.nc
        names = ["c_" + e for e in self.ENGS]
        for e, n in self.NSLOT.items():
            names += ["d_%s_%d" % (e, i) for i in range(n)]
        sems = {n: nc.alloc_semaphore(n) for n in names}
        engobj = {"pe": "tensor", "act": "scalar", "dve": "vector", "pool": "gpsimd", "sp": "sync"}
        with nc.Block() as block:
            for e in self.ENGS:
                ops = self.ops[e]

                def body(eng, ops=ops, e=e):
                    for waits, fn, inc in ops:
                        for s, v in waits:
                            eng.wait_ge(sems[s], v)
                        if fn is None:
                            ins = eng.nop()
                        else:
                            ins = fn(eng)
                        ins.then_inc(sems[inc[0]], inc[1])

                getattr(block, engobj[e])(body)


def _cols(v):
    v = np.asarray(v, np.float32).reshape(-1)
    n = v.shape[0]
    c = (n + 127) // 128
    buf = np.zeros((c * 128,), np.float32)
    buf[:n] = v
    return buf.reshape(c, 128).T


class ParamPack:
    def __init__(self):
        self.off = {}
        self.n = 0
        self.parts = []

    def add(self, name, v):
        a = _cols(v)
        self.off[name] = (self.n, a.shape[1])
        self.n += a.shape[1]
        self.parts.append(a)

    def array(self):
        return np.ascontiguousarray(np.concatenate(self.parts, axis=1))


def pack_params(inp, L):
    pk = ParamPack()
    for nm in ("norm_mix_pre", "norm_mix_post", "norm_ffn_pre", "norm_ffn_post", "norm_ple_pre", "norm_ple_post"):
        pk.add(nm, inp[nm][L])
    for k in range(4):
        pk.add("lru_conv_w%d" % k, inp["lru_conv_w"][L, k])
    for nm in ("lru_conv_b", "lru_b_r", "lru_b_i", "lru_lambda"):
        pk.add(nm, inp[nm][L])
    mu = inp["rwkv_mu"][L]
    pk.add("mu_rkv", mu[:3072])
    pk.add("mu_wl", mu[3072:3136])
    pk.add("mu_al", mu[3136:3200])
    pk.add("mu_g1", mu[3200:3328])
    pk.add("mu_g2", mu[3328:3360])
    for nm in ("rwkv_w0", "rwkv_a0", "rwkv_k_k", "rwkv_k_a", "rwkv_r_k", "rwkv_gn_w", "rwkv_gn_b"):
        pk.add(nm, inp[nm][L])
    pk.add("rwkv_v0", inp["rwkv_v0"][L - 1] if L > 0 else np.zeros(1024, np.float32))
    for k in range(3):
        pk.add("ffn_conv_w%d" % k, inp["ffn_conv_w"][L, k])
    pk.add("ffn_conv_b", inp["ffn_conv_b"][L])
    return pk


class Ctx:
    pass


def act_fn(out, in_, func, bias=None, scale=None):
    kw = {}
    if bias is not None:
        kw["bias"] = bias
    if scale is not None:
        kw["scale"] = scale
    return lambda e: e.activation(out, in_, func, **kw)


def rmsnorm_hT(P, C, src, gcol, hT, tag):
    m = P.mark()
    xs = P.sb(tag + "_xs", [128, NCH, 512], F32)
    sq = P.sb(tag + "_sq", [128, NCH, 512], F32)
    rstd = P.sb(tag + "_rstd", [128, 512], F32)
    ss = P.ps(tag + "_ss", [128, 512], F32)
    srcv = src.rearrange("(c p) t -> p c t", p=128)
    for g in range(T // 512):
        tsl = slice(g * 512, (g + 1) * 512)
        P.add("sp", lambda e, tsl=tsl: e.dma_start(out=xs[:], in_=srcv[:, :, tsl]), writes=[xs.name], dma=True)
        P.add("act", act_fn(sq[:], xs[:], AF.Square), reads=[xs.name], writes=[sq.name])
        for c in range(NCH):
            P.add("pe", lambda e, c=c: e.matmul(ss[:], C.ones_invD[:], sq[:, c, :], start=(c == 0), stop=(c == NCH - 1)),
                  reads=[sq.name, "const"], writes=[ss.name])
        P.add("act", act_fn(rstd[:], ss[:], AF.Sqrt, bias=C.eps_col, scale=1.0), reads=[ss.name, "const"], writes=[rstd.name])
        P.add("dve", lambda e: e.reciprocal(rstd[:], rstd[:]), reads=[rstd.name], writes=[rstd.name])
        for c in range(NCH):
            P.add("dve", lambda e, c=c, tsl=tsl: e.scalar_tensor_tensor(hT[:, c, tsl], xs[:, c, :], gcol[:, c:c + 1], rstd[:],
                                                                         ALU.mult, ALU.mult),
                  reads=[xs.name, rstd.name, "params"], writes=[hT.name])
    P.release(m)


def phase_inproj(P, C, L):
    m = P.mark()
    hT = P.sb("hT", [128, NCH, T], BF16)
    rmsnorm_hT(P, C, C.xT, C.pcol("norm_mix_pre"), hT, "n1")
    hv = C.hT_d.rearrange("(c p) t -> p c t", p=128)
    P.add("sp", lambda e: e.dma_start(out=hv, in_=hT[:]), reads=[hT.name], writes=["hT_d"], dma=True)
    w = C.W[L]["w_in"]
    m2 = P.mark()
    proj_from_hT(P, C, hT, w[:, 0:2048], 2048, C.projA, "ipA", None)
    P.release(m2)
    proj_from_hT(P, C, hT, w[:, OFF_B:OFF_B + 3072], 3072, C.projB, "ipB", None)
    P.release(m2)
    proj_from_hT(P, C, hT, w[:, OFF_C:OFF_C + 2048], 2048, C.projC, "ipC", None)
    P.release(m2)
    proj_from_hT(P, C, hT, w[:, OFF_D:OFF_D + RW_IN], RW_IN, C.projD, "ipD", None)
    P.release(m2)
    proj_tokmajor(P, C, hT, w[:, OFF_B + 3072:OFF_B + 4608], 1536, C.vtokB, "ivB")
    P.release(m2)
    proj_tokmajor(P, C, hT, w[:, OFF_C + 2048:OFF_C + 3072], 1024, C.vtokC, "ivC")
    P.release(m)


def proj_tokmajor(P, C, hT, w, n_out, dst, tag):
    WB = 512
    nblk = n_out // WB
    wb = [P.sb("%s_wb%d" % (tag, i), [128, NCH, WB], BF16) for i in range(2)]
    ost = [P.sb("%s_os%d" % (tag, i), [128, WB], BF16) for i in range(3)]
    pss = [P.ps("%s_ps%d" % (tag, i), [128, 512], F32) for i in range(4)]
    wv = w.rearrange("(c p) n -> p c n", p=128)
    k = 0
    for b in range(nblk):
        wt = wb[b % 2]
        P.add("pool", lambda e, wt=wt, b=b: e.dma_start(out=wt[:], in_=wv[:, :, b * WB:(b + 1) * WB]), writes=[wt.name], dma=True)
        for tt in range(T // 128):
            pt = pss[k % 4]
            os_ = ost[k % 3]
            k += 1
            for c in range(NCH):
                P.add("pe", lambda e, pt=pt, wt=wt, c=c, tt=tt: e.matmul(pt[:], hT[:, c, tt * 128:(tt + 1) * 128], wt[:, c, :],
                                                                          start=(c == 0), stop=(c == NCH - 1)),
                      reads=[wt.name, hT.name], writes=[pt.name])
            if k % 2 == 0:
                P.add("act", lambda e, pt=pt, os_=os_: e.copy(os_[:], pt[:]), reads=[pt.name], writes=[os_.name])
            else:
                P.add("dve", lambda e, pt=pt, os_=os_: e.tensor_copy(os_[:], pt[:]), reads=[pt.name], writes=[os_.name])
            P.add("sp", lambda e, os_=os_, tt=tt, b=b: e.dma_start(out=dst[tt * 128:(tt + 1) * 128, b * WB:(b + 1) * WB], in_=os_[:]),
                  reads=[os_.name], writes=[(tag, tt, b)], dma=True)


def proj_from_hT(P, C, hT, w, n_out, dstT, tag, evac, odt=F32, func=None):
    WB = 512
    nblk = (n_out + WB - 1) // WB
    wb = [P.sb("%s_wb%d" % (tag, i), [128, NCH, WB], BF16) for i in range(2)]
    ost = [P.sb("%s_os%d" % (tag, i), [128, T], odt) for i in range(3)]
    pss = [P.ps("%s_ps%d" % (tag, i), [128, 512], F32) for i in range(6)]
    wv = w.rearrange("(c p) n -> p c n", p=128)
    k_ps = 0
    k_os = 0
    for b in range(nblk):
        n0b = b * WB
        wbs = min(WB, n_out - n0b)
        wt = wb[b % 2]
        P.add("pool", lambda e, wt=wt, n0b=n0b, wbs=wbs: e.dma_start(out=wt[:, :, 0:wbs], in_=wv[:, :, n0b:n0b + wbs]),
              writes=[wt.name], dma=True)
        for j in range((wbs + 127) // 128):
            msz = min(128, wbs - j * 128)
            n0 = n0b + j * 128
            os_ = ost[k_os % 3]
            k_os += 1
            for g in range(T // 512):
                tsl = slice(g * 512, (g + 1) * 512)
                pt = pss[k_ps % 6]
                k_ps += 1
                for c in range(NCH):
                    P.add("pe", lambda e, pt=pt, wt=wt, c=c, j=j, msz=msz, tsl=tsl:
                          e.matmul(pt[0:msz, :], wt[:, c, j * 128:j * 128 + msz], hT[:, c, tsl], start=(c == 0), stop=(c == NCH - 1)),
                          reads=[wt.name, hT.name], writes=[pt.name])
                if func is not None:
                    A(P, os_[0:msz, tsl], pt[0:msz, :], func, [pt.name], [os_.name])
                elif k_ps % 2 == 0:
                    P.add("act", lambda e, pt=pt, os_=os_, msz=msz, tsl=tsl: e.copy(os_[0:msz, tsl], pt[0:msz, :]),
                          reads=[pt.name], writes=[os_.name])
                else:
                    P.add("dve", lambda e, pt=pt, os_=os_, msz=msz, tsl=tsl: e.tensor_copy(os_[0:msz, tsl], pt[0:msz, :]),
                          reads=[pt.name], writes=[os_.name])
            P.add("sp", lambda e, os_=os_, n0=n0, msz=msz: e.dma_start(out=dstT[n0:n0 + msz, :], in_=os_[0:msz, :]),
                  reads=[os_.name], writes=[("dT", tag, n0)], dma=True)


def dma_in(P, eng, dst, src, key=None):
    return P.add(eng, lambda e: e.dma_start(out=dst, in_=src), writes=[key], dma=True)


def V(P, fn, reads, writes):
    return P.add("dve", fn, reads=reads, writes=writes)


def A(P, out, in_, func, reads, writes, bias=None, scale=None):
    return P.add("act", act_fn(out, in_, func, bias=bias, scale=scale), reads=reads, writes=writes)


GELU_K = 1.5957691216057308


def gelu_mul(P, C, g, other, out, tmp1, tmp2, n, reads_extra=()):
    A(P, tmp1, g, AF.Square, [n["g"]], [n["t1"]])
    V(P, lambda e: e.tensor_scalar(tmp1, tmp1, 0.044715, 1.0, ALU.mult, ALU.add), [n["t1"]], [n["t1"]])
    V(P, lambda e: e.tensor_tensor(tmp1, tmp1, g, ALU.mult), [n["t1"], n["g"]], [n["t1"]])
    A(P, tmp2, tmp1, AF.Sigmoid, [n["t1"]], [n["t2"]], scale=GELU_K)
    V(P, lambda e: e.tensor_tensor(tmp2, tmp2, g, ALU.mult), [n["t2"], n["g"]], [n["t2"]])
    V(P, lambda e: e.tensor_tensor(out, tmp2, other, ALU.mult), [n["t2"], n["o"]], [n["out"]])


def phase_rglru(P, C, li):
    m = P.mark()
    W = C.W[li]
    wr = P.sb("wr", [128, 8, 128], BF16)
    wi = P.sb("wi", [128, 8, 128], BF16)
    P.add("pool", lambda e: e.dma_start(out=wr[:], in_=W["lru_w_r"].rearrange("(h i) j -> i h j", i=128)), writes=[wr.name], dma=True)
    P.add("pool", lambda e: e.dma_start(out=wi[:], in_=W["lru_w_i"].rearrange("(h i) j -> i h j", i=128)), writes=[wi.name], dma=True)
    cc = P.sb("lru_c", [128, 16], F32)
    lam = C.pcol("lru_lambda")
    A(P, cc[:, 0:8], lam, AF.Exp, ["params"], [cc.name], scale=-1.0)
    A(P, cc[:, 0:8], cc[:, 0:8], AF.Ln, [cc.name, "const"], [cc.name], bias=C.one_col, scale=1.0)
    V(P, lambda e: e.tensor_scalar(cc[:, 8:16], cc[:, 0:8], -16.0, None, ALU.mult), [cc.name], [cc.name])
    V(P, lambda e: e.tensor_scalar(cc[:, 0:8], cc[:, 0:8], -8.0, None, ALU.mult), [cc.name], [cc.name])
    names = ["x", "g", "u", "ub", "r", "ig", "a", "t1", "t2", "hs"]
    tl = {}
    for nm in names:
        tl[nm] = P.sb("lru_" + nm, [128, T], BF16 if nm == "ub" else F32)
    yo = P.sb("lru_y", [128, T], BF16)
    pss = [P.ps("lru_ps%d" % i, [128, 512], F32) for i in range(4)]
    cw = [C.pcol("lru_conv_w%d" % k) for k in range(4)]
    cb = C.pcol("lru_conv_b")
    br = C.pcol("lru_b_r")
    bi = C.pcol("lru_b_i")
    x, g, u, ub, r, ig, a, t1, t2, hs = [tl[nm] for nm in names]
    for h in range(8):
        P.add("sp", lambda e, h=h: e.dma_start(out=x[:], in_=C.projA[128 * h:128 * h + 128, :]), writes=[x.name], dma=True)
        P.add("sp", lambda e, h=h: e.dma_start(out=g[:], in_=C.projA[1024 + 128 * h:1024 + 128 * h + 128, :]), writes=[g.name], dma=True)
        V(P, lambda e, h=h: e.tensor_scalar(u[:], x[:], cw[3][:, h:h + 1], cb[:, h:h + 1], ALU.mult, ALU.add), [x.name, "params"], [u.name])
        for k, sh in ((2, 1), (1, 2), (0, 3)):
            V(P, lambda e, h=h, k=k, sh=sh: e.scalar_tensor_tensor(u[:, sh:], x[:, :T - sh], cw[k][:, h:h + 1], u[:, sh:], ALU.mult, ALU.add),
              [x.name, u.name, "params"], [u.name])
        A(P, ub[:], u[:], AF.Copy, [u.name], [ub.name])
        for (wt, bcol, dst) in ((wr, br, r), (wi, bi, ig)):
            for gq in range(4):
                pt = pss[gq]
                tsl = slice(gq * 512, gq * 512 + 512)
                P.add("pe", lambda e, pt=pt, wt=wt, h=h, tsl=tsl: e.matmul(pt[:], wt[:, h, :], ub[:, tsl], start=True, stop=True),
                      reads=[wt.name, ub.name], writes=[pt.name])
                A(P, dst[:, tsl], pt[:], AF.Sigmoid, [pt.name, "params"], [dst.name], bias=bcol[:, h:h + 1], scale=1.0)
        A(P, a[:], r[:], AF.Exp, [r.name, cc.name], [a.name], scale=cc[:, h:h + 1])
        A(P, t1[:], r[:], AF.Exp, [r.name, cc.name], [t1.name], scale=cc[:, 8 + h:9 + h])
        A(P, t1[:], t1[:], AF.Sqrt, [t1.name, "const"], [t1.name], bias=C.one_col, scale=-1.0)
        V(P, lambda e: e.tensor_tensor(t1[:], t1[:], ig[:], ALU.mult), [t1.name, ig.name], [t1.name])
        V(P, lambda e: e.tensor_tensor(t1[:], t1[:], u[:], ALU.mult), [t1.name, u.name], [t1.name])
        V(P, lambda e: e.tensor_tensor_scan(hs[:], a[:], t1[:], 0.0, ALU.mult, ALU.add), [a.name, t1.name], [hs.name])
        gelu_mul(P, C, g[:], hs[:], yo[:], t1[:], t2[:], dict(g=g.name, t1=t1.name, t2=t2.name, o=hs.name, out=yo.name))
        P.add("sp", lambda e, h=h: e.dma_start(out=C.yaT[128 * h:128 * h + 128, :], in_=yo[:]), reads=[yo.name], writes=[("yaT", h)], dma=True)
    P.release(m)


def phase_dilated(P, C, li):
    m = P.mark()
    rope = P.sb("rope", [128, 2, T], F32)
    P.add("sp", lambda e: e.dma_start(out=rope[:], in_=C.rope_d[:, 0:2, :]), writes=[rope.name], dma=True)
    ones_bf = P.sb("ones_bf", [128, 128], BF16)
    A(P, ones_bf[:], C.ones1, AF.Copy, ["const"], [ones_bf.name])
    scale = 128.0 ** -0.5
    xq = [P.sb("dl_x%d" % i, [128, T], F32) for i in range(2)]
    cm = [P.sb("dl_cm%d" % i, [128, T], BF16) for i in range(2)]
    t1 = [P.sb("dl_t1_%d" % i, [128, 512], F32) for i in range(2)]
    t2 = [P.sb("dl_t2_%d" % i, [128, 512], F32) for i in range(2)]
    vblk = P.sb("dl_v", [128, 16, 128], BF16)
    num = P.sb("dl_num", [128, T], F32)
    den = P.sb("dl_den", [128, T], F32)
    ex = [P.sb("dl_ex%d" % i, [128, 256], F32) for i in range(2)]
    pT = [P.sb("dl_pT%d" % i, [128, 256], BF16) for i in range(2)]
    yo = P.sb("dl_yo", [128, T], BF16)
    psR = [P.ps("dl_psR%d" % i, [128, 512], F32) for i in range(2)]
    psS = [P.ps("dl_psS%d" % i, [128, 512], F32) for i in range(2)]
    psO = [P.ps("dl_psO%d" % i, [128, 512], F32) for i in range(2)]
    kR = 0
    kb = 0
    DIL = (1, 4, 16)
    for h in range(4):
        for g in range(3):
            d = DIL[g]
            Lc = T // d
            nbc = Lc // 128
            hq = g * 4 + h
            vsrc = C.vtokB[:, hq * 128:(hq + 1) * 128].rearrange("(n j r) c -> r j n c", j=128, r=d)
            for r in range(d):
                P.add("sp", lambda e, r=r, vsrc=vsrc, nbc=nbc: e.dma_start(out=vblk[:, r * nbc:(r + 1) * nbc, :], in_=vsrc[r]),
                      writes=[vblk.name], dma=True)
            for qi in range(2):
                row0 = qi * 1536 + hq * 128
                P.add("sp", lambda e, qi=qi, row0=row0: e.dma_start(out=xq[qi][:], in_=C.projB[row0:row0 + 128, :]),
                      writes=[xq[qi].name], dma=True)
                dstv = cm[qi][:].rearrange("p (r l) -> p r l", r=d)
                for gq in range(4):
                    tsl = slice(gq * 512, gq * 512 + 512)
                    pr = psR[kR % 2]
                    a1 = t1[kR % 2]
                    a2 = t2[kR % 2]
                    kR += 1
                    P.add("pe", lambda e, pr=pr, qi=qi, tsl=tsl: e.matmul(pr[:], C.rmat, xq[qi][:, tsl], start=True, stop=True),
                          reads=[xq[qi].name, "const"], writes=[pr.name])
                    V(P, lambda e, a1=a1, qi=qi, tsl=tsl: e.tensor_tensor(a1[:], xq[qi][:, tsl], rope[:, 0, tsl], ALU.mult),
                      [xq[qi].name, rope.name], [a1.name])
                    V(P, lambda e, a2=a2, pr=pr, tsl=tsl: e.tensor_tensor(a2[:], pr[:], rope[:, 1, tsl], ALU.mult),
                      [pr.name, rope.name], [a2.name])
                    l0 = gq * 512 // d
                    nl = 512 // d
                    V(P, lambda e, a1=a1, a2=a2, dstv=dstv, l0=l0, nl=nl, d=d:
                      e.tensor_tensor(dstv[:, :, l0:l0 + nl], a1[:].rearrange("p (l r) -> p r l", r=d),
                                      a2[:].rearrange("p (l r) -> p r l", r=d), ALU.add),
                      [a1.name, a2.name], [cm[qi].name])
            qc, kc = cm[0], cm[1]
            accv_n = num[:].rearrange("p (l r) -> p r l", r=d)
            accv_d = den[:].rearrange("p (l r) -> p r l", r=d)
            for b in range(16):
                r, n = b // nbc, b % nbc
                hasprev = n > 0
                sp_ = psS[kb % 2]
                op_ = psO[kb % 2]
                exb = ex[kb % 2]
                pb = pT[kb % 2]
                kb += 1
                c0 = 0 if hasprev else 128
                P.add("pe", lambda e, sp_=sp_, b=b: e.matmul(sp_[:, 128:256], kc[:, 128 * b:128 * b + 128], qc[:, 128 * b:128 * b + 128],
                                                             start=True, stop=True),
                      reads=[kc.name, qc.name], writes=[sp_.name])
                if hasprev:
                    P.add("pe", lambda e, sp_=sp_, b=b: e.matmul(sp_[:, 0:128], kc[:, 128 * (b - 1):128 * b], qc[:, 128 * b:128 * b + 128],
                                                                 start=True, stop=True),
                          reads=[kc.name, qc.name], writes=[sp_.name])
                A(P, exb[:, c0:256], sp_[:, c0:256], AF.Exp, [sp_.name], [exb.name], scale=scale)
                V(P, lambda e, exb=exb, pb=pb, c0=c0: e.tensor_tensor(pb[:, c0:256], exb[:, c0:256], C.dilmask[:, c0:256], ALU.mult),
                  [exb.name, "const"], [pb.name])
                P.add("pe", lambda e, op_=op_, pb=pb, b=b, h=h, hasprev=hasprev:
                      e.matmul(op_[:, 0:128], vblk[:, b, :], pb[:, 128:256], start=True, stop=not hasprev),
                      reads=[vblk.name, pb.name], writes=[op_.name])
                if hasprev:
                    P.add("pe", lambda e, op_=op_, pb=pb, b=b, h=h:
                          e.matmul(op_[:, 0:128], vblk[:, b - 1, :], pb[:, 0:128], start=False, stop=True),
                          reads=[vblk.name, pb.name], writes=[op_.name])
                P.add("pe", lambda e, op_=op_, pb=pb, hasprev=hasprev:
                      e.matmul(op_[:, 128:256], ones_bf[:], pb[:, 128:256], start=True, stop=not hasprev),
                      reads=[ones_bf.name, pb.name], writes=[op_.name])
                if hasprev:
                    P.add("pe", lambda e, op_=op_, pb=pb: e.matmul(op_[:, 128:256], ones_bf[:], pb[:, 0:128], start=False, stop=True),
                          reads=[ones_bf.name, pb.name], writes=[op_.name])
                dn = accv_n[:, r, 128 * n:128 * n + 128]
                dd = accv_d[:, r, 128 * n:128 * n + 128]
                if g == 0:
                    V(P, lambda e, dn=dn, op_=op_: e.tensor_copy(dn, op_[:, 0:128]), [op_.name], [num.name])
                    V(P, lambda e, dd=dd, op_=op_: e.tensor_copy(dd, op_[:, 128:256]), [op_.name], [den.name])
                else:
                    V(P, lambda e, dn=dn, op_=op_: e.tensor_tensor(dn, dn, op_[:, 0:128], ALU.add), [op_.name, num.name], [num.name])
                    V(P, lambda e, dd=dd, op_=op_: e.tensor_tensor(dd, dd, op_[:, 128:256], ALU.add), [op_.name, den.name], [den.name])
        V(P, lambda e: e.reciprocal(den[:], den[:]), [den.name], [den.name])
        V(P, lambda e: e.tensor_tensor(yo[:], num[:], den[:], ALU.mult), [num.name, den.name], [yo.name])
        P.add("sp", lambda e, h=h: e.dma_start(out=C.ybT[128 * h:128 * h + 128, :], in_=yo[:]), reads=[yo.name], writes=[("ybT", h)], dma=True)
    P.release(m)


def phase_stickbreak(P, C, li):
    m = P.mark()
    scale = 128.0 ** -0.5
    ident_bf = P.sb("sb_ident", [128, 128], BF16)
    A(P, ident_bf[:], C.ident, AF.Copy, ["const"], [ident_bf.name])
    ones_t = P.sb("sb_ones", [128, T], F32)
    P.add("pool", lambda e: e.memset(ones_t[:], 1.0), writes=[ones_t.name])
    vsb = P.sb("sb_v", [128, 16, 1024], BF16)
    P.add("sp", lambda e: e.dma_start(out=vsb[:], in_=C.vtokC.rearrange("(n j) c -> j n c", j=128)), writes=[vsb.name], dma=True)
    xq = [P.sb("sb_x%d" % i, [128, T], F32) for i in range(2)]
    qb = P.sb("sb_qb", [128, T], BF16)
    kb_ = P.sb("sb_kb", [128, T], BF16)
    ez = P.sb("sb_ez", [128, T], F32)
    nl = P.sb("sb_nl", [128, T], F32)
    cs = P.sb("sb_cs", [128, T], F32)
    att = P.sb("sb_att", [128, T], BF16)
    attT = [P.sb("sb_attT%d" % i, [128, 512], BF16) for i in range(2)]
    ntot = P.sb("sb_ntot", [128, 1], F32)
    yo = P.sb("sb_yo", [128, T], BF16)
    psZ = [P.ps("sb_psZ%d" % i, [128, 512], F32) for i in range(4)]
    psT = [P.ps("sb_psT%d" % i, [128, 512], BF16) for i in range(2)]
    psY = [P.ps("sb_psY%d" % i, [128, 128], F32) for i in range(2)]
    kT = 0
    for h in range(8):
        for qi, dst in ((0, qb), (1, kb_)):
            row0 = qi * 1024 + h * 128
            P.add("sp", lambda e, qi=qi, row0=row0: e.dma_start(out=xq[qi][:], in_=C.projC[row0:row0 + 128, :]),
                  writes=[xq[qi].name], dma=True)
            A(P, dst[:], xq[qi][:], AF.Copy, [xq[qi].name], [dst.name])
        for n in range(16):
            nk = 128 * (n + 1)
            nbank = (nk + 511) // 512
            for bk in range(nbank):
                c0 = bk * 512
                w = min(512, nk - c0)
                pz = psZ[bk]
                P.add("pe", lambda e, pz=pz, c0=c0, w=w, n=n: e.matmul(pz[:, 0:w], qb[:, 128 * n:128 * n + 128], kb_[:, c0:c0 + w],
                                                                      start=True, stop=True),
                      reads=[qb.name, kb_.name], writes=[pz.name])
                A(P, ez[:, c0:c0 + w], pz[:, 0:w], AF.Exp, [pz.name], [ez.name], scale=scale)
            d0 = 128 * n
            V(P, lambda e, d0=d0: e.tensor_tensor(ez[:, d0:d0 + 128], ez[:, d0:d0 + 128], C.sbmask, ALU.mult), [ez.name, "const"], [ez.name])
            A(P, nl[:, 0:nk], ez[:, 0:nk], AF.Ln, [ez.name, "const"], [nl.name], bias=C.one_col, scale=1.0)
            V(P, lambda e, nk=nk: e.tensor_tensor_scan(cs[:, 0:nk], ones_t[:, 0:nk], nl[:, 0:nk], 0.0, ALU.mult, ALU.add),
              [ones_t.name, nl.name], [cs.name])
            V(P, lambda e, nk=nk: e.tensor_scalar(ntot[:], cs[:, nk - 1:nk], -1.0, None, ALU.mult), [cs.name], [ntot.name])
            V(P, lambda e, nk=nk: e.tensor_tensor(cs[:, 0:nk], cs[:, 0:nk], nl[:, 0:nk], ALU.subtract), [cs.name, nl.name], [cs.name])
            A(P, cs[:, 0:nk], cs[:, 0:nk], AF.Exp, [cs.name, ntot.name], [cs.name], bias=ntot[:, 0:1], scale=1.0)
            V(P, lambda e, nk=nk: e.tensor_tensor(att[:, 0:nk], ez[:, 0:nk], cs[:, 0:nk], ALU.mult), [ez.name, cs.name], [att.name])
            py = psY[n % 2]
            for b0 in range(0, n + 1, 4):
                nb = min(4, n + 1 - b0)
                pt = psT[kT % 2]
                at = attT[kT % 2]
                kT += 1
                for j in range(nb):
                    b = b0 + j
                    P.add("pe", lambda e, pt=pt, j=j, b=b: e.transpose(pt[:, 128 * j:128 * j + 128], att[:, 128 * b:128 * b + 128], ident_bf[:]),
                          reads=[att.name, ident_bf.name], writes=[pt.name])
                if kT % 2 == 0:
                    V(P, lambda e, pt=pt, at=at, nb=nb: e.tensor_copy(at[:, 0:128 * nb], pt[:, 0:128 * nb]), [pt.name], [at.name])
                else:
                    P.add("act", lambda e, pt=pt, at=at, nb=nb: e.copy(at[:, 0:128 * nb], pt[:, 0:128 * nb]), reads=[pt.name], writes=[at.name])
                for j in range(nb):
                    b = b0 + j
                    P.add("pe", lambda e, py=py, at=at, j=j, b=b, h=h, n=n:
                          e.matmul(py[:], vsb[:, b, h * 128:(h + 1) * 128], at[:, 128 * j:128 * j + 128], start=(b == 0), stop=(b == n)),
                          reads=[vsb.name, at.name], writes=[py.name])
            P.add("act", lambda e, py=py, n=n: e.copy(yo[:, 128 * n:128 * n + 128], py[:]), reads=[py.name], writes=[yo.name])
        P.add("sp", lambda e, h=h: e.dma_start(out=C.ycT[128 * h:128 * h + 128, :], in_=yo[:]), reads=[yo.name], writes=[("ycT", h)], dma=True)
    P.release(m)


DECAY_K = -0.6065306597126334
GN_EPS = 64e-5


def lerp(P, out, x, mu, om, rd, wr):
    n = x.shape[-1]
    V(P, lambda e: e.tensor_scalar(out, x, om, None, ALU.mult), rd, wr)
    V(P, lambda e: e.scalar_tensor_tensor(out[:, 1:], x[:, :n - 1], mu, out[:, 1:], ALU.mult, ALU.add), rd + wr, wr)


def phase_rwkv_prep(P, C, li):
    m = P.mark()
    W = C.W[li]
    L = C.L
    o_rkv, n_rkv = C.poff["mu_rkv"]
    o_end = C.poff["mu_g2"][0] + C.poff["mu_g2"][1]
    nmu = o_end - o_rkv
    omu = P.sb("rw_omu", [128, nmu], F32)
    V(P, lambda e: e.tensor_scalar(omu[:], C.params[:, o_rkv:o_end], -1.0, 1.0, ALU.mult, ALU.add), ["params"], [omu.name])
    mucol = lambda nm, c=0: C.params[:, C.poff[nm][0] + c:C.poff[nm][0] + c + 1]
    omcol = lambda nm, c=0: omu[:, C.poff[nm][0] - o_rkv + c:C.poff[nm][0] - o_rkv + c + 1]
    wup = P.sb("rw_wup", [64, RW_W], BF16)
    aup = P.sb("rw_aup", [64, RW_W], BF16)
    gup1 = P.sb("rw_gup1", [128, RW_W], BF16)
    gup2 = P.sb("rw_gup2", [32, RW_W], BF16)
    P.add("pool", lambda e: e.dma_start(out=wup[:], in_=W["rwkv_w_up"]), writes=[wup.name], dma=True)
    P.add("pool", lambda e: e.dma_start(out=aup[:], in_=W["rwkv_a_up"]), writes=[aup.name], dma=True)
    P.add("pool", lambda e: e.dma_start(out=gup1[:], in_=W["rwkv_g_up"][0:128, :]), writes=[gup1.name], dma=True)
    P.add("pool", lambda e: e.dma_start(out=gup2[:], in_=W["rwkv_g_up"][128:160, :]), writes=[gup2.name], dma=True)
    if L > 0:
        vdn = P.sb("rw_vdn", [128, 8, 32], BF16)
        vup = P.sb("rw_vup", [32, RW_W], BF16)
        P.add("pool", lambda e: e.dma_start(out=vdn[:], in_=W["rwkv_v_down"].rearrange("(c p) n -> p c n", p=128)), writes=[vdn.name], dma=True)
        P.add("pool", lambda e: e.dma_start(out=vup[:], in_=W["rwkv_v_up"]), writes=[vup.name], dma=True)
    rmask = P.sb("rw_rmask", [128, T], F32)
    P.add("sp", lambda e: e.dma_start(out=rmask[:], in_=C.rope_d[:, 2, :]), writes=[rmask.name], dma=True)
    xin = P.sb("rw_xin", [128, T], F32)
    tmp = P.sb("rw_tmp", [128, T], F32)
    twl = P.sb("rw_twl", [64, T], BF16)
    tal = P.sb("rw_tal", [64, T], BF16)
    tg1 = P.sb("rw_tg1", [128, T], BF16)
    tg2 = P.sb("rw_tg2", [32, T], BF16)
    for (row0, nr, munm, dst, fn) in ((3072, 64, "mu_wl", twl, "tanh"), (3136, 64, "mu_al", tal, AF.Copy),
                                      (3200, 128, "mu_g1", tg1, AF.Sigmoid), (3328, 32, "mu_g2", tg2, AF.Sigmoid)):
        P.add("sp", lambda e, row0=row0, nr=nr: e.dma_start(out=xin[0:nr, :], in_=C.projD[row0:row0 + nr, :]), writes=[xin.name], dma=True)
        lerp(P, tmp[0:nr, :], xin[0:nr, :], mucol(munm)[0:nr], omcol(munm)[0:nr], [xin.name, "params", omu.name], [tmp.name])
        if fn == "tanh":
            A(P, tmp[0:nr, :], tmp[0:nr, :], AF.Sigmoid, [tmp.name], [tmp.name], scale=2.0)
            V(P, lambda e, dst=dst, nr=nr: e.tensor_scalar(dst[0:nr, :], tmp[0:nr, :], 2.0, -1.0, ALU.mult, ALU.add), [tmp.name], [dst.name])
        else:
            A(P, dst[0:nr, :], tmp[0:nr, :], fn, [tmp.name], [dst.name])
    pss = [P.ps("rw_ps%d" % i, [128, 512], F32) for i in range(6)]
    kps = [0]

    def nps():
        kps[0] += 1
        return pss[kps[0] % 6]

    vb = None
    vd = None
    if L > 0:
        vb = P.sb("rw_vb", [128, 8, T], BF16)
        vd = P.sb("rw_vd", [32, T], BF16)
        for c in range(8):
            P.add("sp", lambda e, c=c: e.dma_start(out=xin[:], in_=C.projD[2048 + 128 * c:2048 + 128 * c + 128, :]), writes=[xin.name], dma=True)
            lerp(P, tmp[:], xin[:], mucol("mu_rkv", 16 + c), omcol("mu_rkv", 16 + c), [xin.name, "params", omu.name], [tmp.name])
            A(P, vb[:, c, :], tmp[:], AF.Copy, [tmp.name], [vb.name])
        for gq in range(4):
            tsl = slice(gq * 512, gq * 512 + 512)
            pt = nps()
            for c in range(8):
                P.add("pe", lambda e, pt=pt, c=c, tsl=tsl: e.matmul(pt[0:32, :], vdn[:, c, :], vb[:, c, tsl], start=(c == 0), stop=(c == 7)),
                      reads=[vdn.name, vb.name], writes=[pt.name])
            A(P, vd[:, tsl], pt[0:32, :], AF.Copy, [pt.name], [vd.name])
    names = ["r", "k", "v", "a", "lw", "kk", "t1", "t2", "t3"]
    tl = {nm: P.sb("rw_" + nm, [128, T], F32) for nm in names}
    r_, k_, v_, a_, lw, kk, t1, t2, t3 = [tl[nm] for nm in names]
    ar = P.sb("rw_arsb", [128, 32, 128], BF16)
    kbt = P.sb("rw_kbsb", [128, 32, 128], BF16)
    vo = P.sb("rw_vo", [128, T], BF16)
    wc = P.sb("rw_wcsb", [128, 32], F32)
    c3 = lambda t: t[:].rearrange("p (q s) -> p q s", s=64)
    for c in range(8):
        for (dst, roff, mc) in ((r_, 0, c), (k_, 1024, 8 + c), (v_, 2048, 16 + c)):
            P.add("sp", lambda e, roff=roff, c=c: e.dma_start(out=xin[:], in_=C.projD[roff + 128 * c:roff + 128 * c + 128, :]),
                  writes=[xin.name], dma=True)
            lerp(P, dst[:], xin[:], mucol("mu_rkv", mc), omcol("mu_rkv", mc), [xin.name, "params", omu.name], [dst.name])
        csl = slice(128 * c, 128 * c + 128)
        for gq in range(4):
            tsl = slice(gq * 512, gq * 512 + 512)
            pt = nps()
            P.add("pe", lambda e, pt=pt, tsl=tsl, csl=csl: e.matmul(pt[:], wup[:, csl], twl[:, tsl], start=True, stop=True),
                  reads=[wup.name, twl.name], writes=[pt.name])
            A(P, lw[:, tsl], pt[:], AF.Sigmoid, [pt.name, "params"], [lw.name], bias=C.pcol("rwkv_w0")[:, c:c + 1], scale=1.0)
            pt = nps()
            P.add("pe", lambda e, pt=pt, tsl=tsl, csl=csl: e.matmul(pt[:], aup[:, csl], tal[:, tsl], start=True, stop=True),
                  reads=[aup.name, tal.name], writes=[pt.name])
            A(P, a_[:, tsl], pt[:], AF.Sigmoid, [pt.name, "params"], [a_.name], bias=C.pcol("rwkv_a0")[:, c:c + 1], scale=1.0)
            pt = nps()
            P.add("pe", lambda e, pt=pt, tsl=tsl, csl=csl: e.matmul(pt[:], gup1[:, csl], tg1[:, tsl], start=True, stop=False),
                  reads=[gup1.name, tg1.name], writes=[pt.name])
            P.add("pe", lambda e, pt=pt, tsl=tsl, csl=csl: e.matmul(pt[:], gup2[:, csl], tg2[:, tsl], start=False, stop=True),
                  reads=[gup2.name, tg2.name], writes=[pt.name])
            A(P, t3[:, tsl], pt[:], AF.Copy, [pt.name], [t3.name])
        P.add("sp", lambda e, csl=csl: e.dma_start(out=C.rw_g[csl, :], in_=t3[:]), reads=[t3.name], writes=[("rw_g", c)], dma=True)
        if L > 0:
            for gq in range(4):
                tsl = slice(gq * 512, gq * 512 + 512)
                pt = nps()
                P.add("pe", lambda e, pt=pt, tsl=tsl, csl=csl: e.matmul(pt[:], vup[:, csl], vd[:, tsl], start=True, stop=True),
                      reads=[vup.name, vd.name], writes=[pt.name])
                A(P, t1[:, tsl], pt[:], AF.Sigmoid, [pt.name, "params"], [t1.name], bias=C.pcol("rwkv_v0")[:, c:c + 1], scale=1.0)
            P.add("sp", lambda e, csl=csl: e.dma_start(out=t2[:], in_=C.vfirstT[csl, :]), writes=[t2.name], dma=True)
            V(P, lambda e: e.tensor_tensor(t2[:], t2[:], v_[:], ALU.subtract), [t2.name, v_.name], [t2.name])
            V(P, lambda e: e.tensor_tensor(t2[:], t2[:], t1[:], ALU.mult), [t2.name, t1.name], [t2.name])
            V(P, lambda e: e.tensor_tensor(v_[:], v_[:], t2[:], ALU.add), [t2.name, v_.name], [v_.name])
        else:
            P.add("sp", lambda e, csl=csl: e.dma_start(out=C.vfirstT[csl, :], in_=v_[:]), reads=[v_.name], writes=[("vfirst", c)], dma=True)
        A(P, vo[:], v_[:], AF.Copy, [v_.name], [vo.name])
        P.add("sp", lambda e, csl=csl: e.dma_start(out=C.rw_v[csl, :], in_=vo[:]), reads=[vo.name], writes=[("rw_v", c)], dma=True)
        V(P, lambda e: e.tensor_scalar(lw[:], lw[:], DECAY_K, None, ALU.mult), [lw.name], [lw.name])
        V(P, lambda e: e.tensor_tensor_scan(t1[:], rmask[:], lw[:], 0.0, ALU.mult, ALU.add), [rmask.name, lw.name], [t1.name])
        V(P, lambda e: e.tensor_copy(wc[:], c3(t1)[:, :, 63]), [t1.name], [wc.name])
        A(P, wc[:], wc[:], AF.Exp, [wc.name], [wc.name])
        P.add("sp", lambda e, csl=csl: e.dma_start(out=C.rw_wc[csl, :], in_=wc[:]), reads=[wc.name], writes=[("rw_wc", c)], dma=True)
        V(P, lambda e, c=c: e.tensor_scalar(kk[:], k_[:], C.pcol("rwkv_k_k")[:, c:c + 1], None, ALU.mult), [k_.name, "params"], [kk.name])
        A(P, t2[:], kk[:], AF.Square, [kk.name], [t2.name])
        for gq in range(4):
            tsl = slice(gq * 512, gq * 512 + 512)
            pt = nps()
            P.add("pe", lambda e, pt=pt, tsl=tsl: e.matmul(pt[:], C.blk64, t2[:, tsl], start=True, stop=True),
                  reads=[t2.name, "const"], writes=[pt.name])
            A(P, t3[:, tsl], pt[:], AF.Sqrt, [pt.name], [t3.name])
        V(P, lambda e: e.tensor_scalar(t3[:], t3[:], 1e-12, None, ALU.max), [t3.name], [t3.name])
        V(P, lambda e: e.reciprocal(t3[:], t3[:]), [t3.name], [t3.name])
        V(P, lambda e: e.tensor_tensor(kk[:], kk[:], t3[:], ALU.mult), [kk.name, t3.name], [kk.name])
        V(P, lambda e, c=c: e.tensor_scalar(t2[:], a_[:], 1.0, C.pcol("rwkv_k_a")[:, c:c + 1], ALU.subtract, ALU.mult), [a_.name, "params"], [t2.name])
        V(P, lambda e: e.scalar_tensor_tensor(k_[:], t2[:], 1.0, k_[:], ALU.add, ALU.mult), [t2.name, k_.name], [k_.name])
        V(P, lambda e, c=c: e.scalar_tensor_tensor(t2[:], r_[:], C.pcol("rwkv_r_k")[:, c:c + 1], k_[:], ALU.mult, ALU.mult),
          [r_.name, k_.name, "params"], [t2.name])
        for gq in range(4):
            tsl = slice(gq * 512, gq * 512 + 512)
            pt = nps()
            P.add("pe", lambda e, pt=pt, tsl=tsl: e.matmul(pt[:], C.blk64, t2[:, tsl], start=True, stop=True),
                  reads=[t2.name, "const"], writes=[pt.name])
            V(P, lambda e, pt=pt, tsl=tsl: e.tensor_tensor(t3[:, tsl], pt[:], v_[:, tsl], ALU.mult), [pt.name, v_.name], [t3.name])
        P.add("sp", lambda e, csl=csl: e.dma_start(out=C.rw_bonus[csl, :], in_=t3[:]), reads=[t3.name], writes=[("rw_bonus", c)], dma=True)
        A(P, t2[:], t1[:], AF.Exp, [t1.name], [t2.name])
        V(P, lambda e: e.tensor_tensor(c3(ar)[:, :, 64:128] if False else ar[:, :, 64:128], c3(r_), c3(t2), ALU.mult), [r_.name, t2.name], [ar.name])
        V(P, lambda e: e.tensor_tensor(t2[:], t1[:], lw[:], ALU.subtract), [t1.name, lw.name], [t2.name])
        A(P, t2[:], t2[:], AF.Exp, [t2.name], [t2.name])
        V(P, lambda e: e.scalar_tensor_tensor(ar[:, :, 0:64], c3(kk), -1.0, c3(t2), ALU.mult, ALU.mult), [kk.name, t2.name], [ar.name])
        A(P, t2[:], t1[:], AF.Exp, [t1.name], [t2.name], scale=-1.0)
        V(P, lambda e: e.tensor_tensor(kbt[:, :, 0:64], c3(k_), c3(t2), ALU.mult), [k_.name, t2.name], [kbt.name])
        V(P, lambda e: e.tensor_tensor(t3[:], kk[:], a_[:], ALU.mult), [kk.name, a_.name], [t3.name])
        V(P, lambda e: e.tensor_tensor(kbt[:, :, 64:128], c3(t3), c3(t2), ALU.mult), [t3.name, t2.name], [kbt.name])
        P.add("sp", lambda e, csl=csl: e.dma_start(out=C.rw_ar[csl, :], in_=ar[:].rearrange("p q x -> p (q x)")), reads=[ar.name],
              writes=[("rw_ar", c)], dma=True)
        P.add("sp", lambda e, csl=csl: e.dma_start(out=C.rw_kb[csl, :], in_=kbt[:].rearrange("p q x -> p (q x)")), reads=[kbt.name],
              writes=[("rw_kb", c)], dma=True)
    P.release(m)


def phase_rwkv_chunks(P, C, li):
    m = P.mark()
    NQ = T // 64
    ident_bf = P.sb("rc_ident", [128, 128], BF16)
    A(P, ident_bf[:], C.ident, AF.Copy, ["const"], [ident_bf.name])
    id64 = ident_bf[0:64, 0:64]
    mask1 = C.consts[0:64, 1280:1792]
    mask2 = C.consts[0:64, 1792:2048]
    gneps = C.consts[0:64, 258:259]
    wcs = P.sb("rc_wcs", [64, 16, NQ], F32)
    P.add("sp", lambda e: e.dma_start(out=wcs[:], in_=C.rw_wc.rearrange("(hd j) q -> j hd q", j=64)), writes=[wcs.name], dma=True)
    STf = P.sb("rc_STf", [64, 16, 64], F32)
    STb = P.sb("rc_STb", [64, 16, 64], BF16)
    V(P, lambda e: e.memset(STf[:], 0.0), [], [STf.name])
    V(P, lambda e: e.memset(STb[:], 0.0), [], [STb.name])
    ARw = P.sb("rc_AR", [64, 16, 1024], BF16)
    KBw = P.sb("rc_KB", [64, 16, 1024], BF16)
    Vw = P.sb("rc_V", [64, 16, 512], BF16)
    Gw = P.sb("rc_G", [128, 8, 512], F32)
    Bw = P.sb("rc_B", [128, 8, 512], F32)
    Yw = P.sb("rc_Y", [128, 8, 512], BF16)
    psF = [P.ps("rc_psF%d" % i, [128, 512], F32) for i in range(6)]
    psB = [P.ps("rc_psB%d" % i, [128, 1024], BF16) for i in range(2)]
    kf = [0]
    kbn = [0]

    def nF():
        kf[0] += 1
        return psF[kf[0] % 6]

    def nB():
        kbn[0] += 1
        return psB[kbn[0] % 2]

    NR = 3
    rot = {}

    def sbr(nm, shape, dt):
        if nm not in rot:
            rot[nm] = [[P.sb("rc_%s%d" % (nm, i), shape, dt) for i in range(NR)], 0]
        rot[nm][1] += 1
        return rot[nm][0][rot[nm][1] % NR]

    def mm(out, lhsT, rhs, rd, wr, start=True, stop=True):
        P.add("pe", lambda e: e.matmul(out, lhsT, rhs, start=start, stop=stop), reads=rd, writes=wr)

    ar_v = C.rw_ar.rearrange("(hd j) x -> j hd x", j=64)
    kb_v = C.rw_kb.rearrange("(hd j) x -> j hd x", j=64)
    v_v = C.rw_v.rearrange("(hd j) t -> j hd t", j=64)
    g_v = C.rw_g.rearrange("(c p) t -> p c t", p=128)
    bo_v = C.rw_bonus.rearrange("(c p) t -> p c t", p=128)
    yd_v = C.ydT.rearrange("(c p) t -> p c t", p=128)
    gnw = C.pcol("rwkv_gn_w")
    gnb = C.pcol("rwkv_gn_b")
    for w in range(4):
        P.add("sp", lambda e, w=w: e.dma_start(out=ARw[:], in_=ar_v[:, :, w * 1024:(w + 1) * 1024]), writes=[ARw.name], dma=True)
        P.add("sp", lambda e, w=w: e.dma_start(out=KBw[:], in_=kb_v[:, :, w * 1024:(w + 1) * 1024]), writes=[KBw.name], dma=True)
        P.add("sp", lambda e, w=w: e.dma_start(out=Vw[:], in_=v_v[:, :, w * 512:(w + 1) * 512]), writes=[Vw.name], dma=True)
        P.add("sp", lambda e, w=w: e.dma_start(out=Gw[:], in_=g_v[:, :, w * 512:(w + 1) * 512]), writes=[Gw.name], dma=True)
        P.add("sp", lambda e, w=w: e.dma_start(out=Bw[:], in_=bo_v[:, :, w * 512:(w + 1) * 512]), writes=[Bw.name], dma=True)
        for ql in range(8):
            q = w * 8 + ql
            for hg in range(4):
                hds = [4 * hg + k for k in range(4)]
                At = [ARw[:, hd, ql * 128:ql * 128 + 64] for hd in hds]
                Rt = [ARw[:, hd, ql * 128 + 64:ql * 128 + 128] for hd in hds]
                AtRt = [ARw[:, hd, ql * 128:ql * 128 + 128] for hd in hds]
                Kt = [KBw[:, hd, ql * 128:ql * 128 + 64] for hd in hds]
                Bt = [KBw[:, hd, ql * 128 + 64:ql * 128 + 128] for hd in hds]
                Vt = [Vw[:, hd, ql * 64:ql * 64 + 64] for hd in hds]
                p1, p2, p3 = nF(), nF(), nF()
                for k in range(4):
                    mm(p1[0:64, k * 128:(k + 1) * 128], Bt[k], AtRt[k], [KBw.name, ARw.name], [p1.name])
                    mm(p2[0:64, k * 128:(k + 1) * 128], Kt[k], AtRt[k], [KBw.name, ARw.name], [p2.name])
                    mm(p3[0:64, k * 64:(k + 1) * 64], At[k], Bt[k], [KBw.name, ARw.name], [p3.name])
                s1 = sbr("s1", [64, 512], BF16)
                s2 = sbr("s2", [64, 512], BF16)
                xt0 = sbr("xt0", [64, 256], BF16)
                V(P, lambda e, s1=s1, p1=p1: e.tensor_tensor(s1[:], p1[0:64, :], mask1, ALU.mult), [p1.name, "const"], [s1.name])
                V(P, lambda e, s2=s2, p2=p2: e.tensor_tensor(s2[:], p2[0:64, :], mask1, ALU.mult), [p2.name, "const"], [s2.name])
                V(P, lambda e, xt0=xt0, p3=p3: e.tensor_tensor(xt0[:], p3[0:64, 0:256], mask2, ALU.mult), [p3.name, "const"], [xt0.name])
                if C.cfg.get("rc_stage", 9) <= 1:
                    continue
                RB = [s1[:, k * 128 + 64:k * 128 + 128] for k in range(4)]
                AK = [s2[:, k * 128:k * 128 + 64] for k in range(4)]
                RK = [s2[:, k * 128 + 64:k * 128 + 128] for k in range(4)]
                Xp = [([s1[:, k * 128:k * 128 + 64] for k in range(4)], s1.name)]
                XTp = [([xt0[:, k * 64:k * 64 + 64] for k in range(4)], xt0.name)]
                for lev in range(1, 6):
                    (xs_, xn), (xts_, xtn) = Xp[-1], XTp[-1]
                    px = nF()
                    for k in range(4):
                        mm(px[0:64, k * 64:(k + 1) * 64], xts_[k], xs_[k], [xn, xtn], [px.name])
                    xnew = sbr("xp%d" % lev, [64, 256], BF16)
                    P.add("act", lambda e, xnew=xnew, px=px: e.copy(xnew[:], px[0:64, 0:256]), reads=[px.name], writes=[xnew.name])
                    Xp.append(([xnew[:, k * 64:(k + 1) * 64] for k in range(4)], xnew.name))
                    if lev < 5:
                        pxt = nF()
                        for k in range(4):
                            mm(pxt[0:64, k * 64:(k + 1) * 64], xs_[k], xts_[k], [xn, xtn], [pxt.name])
                        xtnew = sbr("xtp%d" % lev, [64, 256], BF16)
                        V(P, lambda e, xtnew=xtnew, pxt=pxt: e.tensor_copy(xtnew[:], pxt[0:64, 0:256]), [pxt.name], [xtnew.name])
                        XTp.append(([xtnew[:, k * 64:(k + 1) * 64] for k in range(4)], xtnew.name))
                if C.cfg.get("rc_stage", 9) <= 2:
                    continue
                pt = nB()
                for k in range(4):
                    P.add("pe", lambda e, pt=pt, k=k, Kt=Kt: e.transpose(pt[0:64, k * 64:(k + 1) * 64], Kt[k], id64), reads=[KBw.name, ident_bf.name], writes=[pt.name])
                    P.add("pe", lambda e, pt=pt, k=k, Bt=Bt: e.transpose(pt[0:64, 256 + k * 64:256 + (k + 1) * 64], Bt[k], id64), reads=[KBw.name, ident_bf.name], writes=[pt.name])
                    P.add("pe", lambda e, pt=pt, k=k, Vt=Vt: e.transpose(pt[0:64, 512 + k * 64:512 + (k + 1) * 64], Vt[k], id64), reads=[Vw.name, ident_bf.name], writes=[pt.name])
                tok = sbr("tok", [64, 768], BF16)
                P.add("act", lambda e, tok=tok, pt=pt: e.copy(tok[:], pt[0:64, 0:768]), reads=[pt.name], writes=[tok.name])
                Ktok = [tok[:, k * 64:(k + 1) * 64] for k in range(4)]
                Btok = [tok[:, 256 + k * 64:256 + (k + 1) * 64] for k in range(4)]
                Vtok = [tok[:, 512 + k * 64:512 + (k + 1) * 64] for k in range(4)]
                if C.cfg.get("rc_stage", 9) <= 3:
                    continue
                pu = nF()
                for k in range(4):
                    mm(pu[0:64, k * 64:(k + 1) * 64], At[k], STb[:, hds[k], :], [ARw.name, STb.name], [pu.name], True, False)
                    mm(pu[0:64, k * 64:(k + 1) * 64], AK[k], Vtok[k], [s2.name, tok.name], [pu.name], False, True)
                Uf = sbr("Uf", [64, 256], F32)
                Ub = sbr("Ub", [64, 256], BF16)
                V(P, lambda e, Uf=Uf, pu=pu: e.tensor_copy(Uf[:], pu[0:64, 0:256]), [pu.name], [Uf.name])
                P.add("act", lambda e, Ub=Ub, Uf=Uf: e.copy(Ub[:], Uf[:]), reads=[Uf.name], writes=[Ub.name])
                for lev in range(6):
                    xs_, xn = Xp[lev]
                    pu2 = nF()
                    for k in range(4):
                        mm(pu2[0:64, k * 64:(k + 1) * 64], xs_[k], Ub[:, k * 64:(k + 1) * 64], [xn, Ub.name], [pu2.name])
                    V(P, lambda e, Uf=Uf, pu2=pu2: e.tensor_tensor(Uf[:], Uf[:], pu2[0:64, 0:256], ALU.add), [Uf.name, pu2.name], [Uf.name])
                    P.add("act", lambda e, Ub=Ub, Uf=Uf: e.copy(Ub[:], Uf[:]), reads=[Uf.name], writes=[Ub.name])
                if C.cfg.get("rc_stage", 9) <= 4:
                    continue
                py = nF()
                for k in range(4):
                    o = py[0:64, k * 64:(k + 1) * 64]
                    mm(o, Rt[k], STb[:, hds[k], :], [ARw.name, STb.name], [py.name], True, False)
                    mm(o, RB[k], Ub[:, k * 64:(k + 1) * 64], [s1.name, Ub.name], [py.name], False, False)
                    mm(o, RK[k], Vtok[k], [s2.name, tok.name], [py.name], False, True)
                pst = nF()
                for k in range(4):
                    o = pst[0:64, k * 64:(k + 1) * 64]
                    mm(o, Btok[k], Ub[:, k * 64:(k + 1) * 64], [tok.name, Ub.name], [pst.name], True, False)
                    mm(o, Ktok[k], Vtok[k], [tok.name], [pst.name], False, True)
                stv = STf[:, 4 * hg:4 * hg + 4, :]
                V(P, lambda e, stv=stv, pst=pst: e.tensor_tensor(stv, stv, pst[0:64, 0:256].rearrange("p (h i) -> p h i", i=64), ALU.add),
                  [STf.name, pst.name], [STf.name])
                V(P, lambda e, stv=stv, hg=hg, q=q: e.tensor_tensor(stv, stv, wcs[:, 4 * hg:4 * hg + 4, q:q + 1].to_broadcast([64, 4, 64]), ALU.mult),
                  [STf.name, wcs.name], [STf.name])
                P.add("act", lambda e, stv=stv, hg=hg: e.copy(STb[:, 4 * hg:4 * hg + 4, :], stv), reads=[STf.name], writes=[STb.name])
                if C.cfg.get("rc_stage", 9) <= 5:
                    continue
                ysb = sbr("ysb", [64, 256], F32)
                ysq = sbr("ysq", [64, 256], F32)
                st = sbr("st", [64, 16], F32)
                yn = sbr("yn", [64, 256], BF16)
                P.add("act", lambda e, ysb=ysb, py=py: e.copy(ysb[:], py[0:64, 0:256]), reads=[py.name], writes=[ysb.name])
                y3 = ysb[:].rearrange("p (h i) -> p h i", i=64)
                A(P, ysq[:], ysb[:], AF.Square, [ysb.name], [ysq.name])
                V(P, lambda e, st=st, y3=y3: e.tensor_reduce(st[:, 0:4], y3, AX.X, ALU.add), [ysb.name], [st.name])
                V(P, lambda e, st=st, ysq=ysq: e.tensor_reduce(st[:, 4:8], ysq[:].rearrange("p (h i) -> p h i", i=64), AX.X, ALU.add), [ysq.name], [st.name])
                V(P, lambda e, st=st: e.tensor_scalar(st[:, 0:4], st[:, 0:4], 1.0 / 64, None, ALU.mult), [st.name], [st.name])
                V(P, lambda e, st=st: e.tensor_tensor(st[:, 8:12], st[:, 0:4], st[:, 0:4], ALU.mult), [st.name], [st.name])
                V(P, lambda e, st=st: e.scalar_tensor_tensor(st[:, 4:8], st[:, 4:8], 1.0 / 64, st[:, 8:12], ALU.mult, ALU.subtract), [st.name], [st.name])
                A(P, st[:, 4:8], st[:, 4:8], AF.Sqrt, [st.name, "const"], [st.name], bias=gneps, scale=1.0)
                V(P, lambda e, st=st: e.reciprocal(st[:, 4:8], st[:, 4:8]), [st.name], [st.name])
                V(P, lambda e, st=st, y3=y3: e.tensor_tensor(y3, y3, st[:, 0:4].unsqueeze(2).to_broadcast([64, 4, 64]), ALU.subtract), [st.name, ysb.name], [ysb.name])
                V(P, lambda e, st=st, y3=y3, yn=yn: e.tensor_tensor(yn[:].rearrange("p (h i) -> p h i", i=64), y3,
                                                                   st[:, 4:8].unsqueeze(2).to_broadcast([64, 4, 64]), ALU.mult), [st.name, ysb.name], [yn.name])
                if C.cfg.get("rc_stage", 9) <= 6:
                    continue
                pyt = nB()
                for pp in range(2):
                    P.add("pe", lambda e, pyt=pyt, pp=pp, yn=yn: e.transpose(pyt[:, pp * 64:(pp + 1) * 64], yn[:, pp * 128:(pp + 1) * 128], id64),
                          reads=[yn.name, ident_bf.name], writes=[pyt.name])
                yf = sbr("yf", [128, 128], F32)
                for pp in range(2):
                    c = 2 * hg + pp
                    tw = slice(ql * 64, ql * 64 + 64)
                    V(P, lambda e, yf=yf, pyt=pyt, pp=pp, c=c: e.tensor_scalar(yf[:, pp * 64:(pp + 1) * 64], pyt[:, pp * 64:(pp + 1) * 64],
                                                                               gnw[:, c:c + 1], gnb[:, c:c + 1], ALU.mult, ALU.add),
                      [pyt.name, "params"], [yf.name])
                    V(P, lambda e, yf=yf, pp=pp, c=c, tw=tw: e.tensor_tensor(yf[:, pp * 64:(pp + 1) * 64], yf[:, pp * 64:(pp + 1) * 64], Bw[:, c, tw], ALU.add),
                      [yf.name, Bw.name], [yf.name])
                    V(P, lambda e, yf=yf, pp=pp, c=c, tw=tw: e.tensor_tensor(Yw[:, c, tw], yf[:, pp * 64:(pp + 1) * 64], Gw[:, c, tw], ALU.mult),
                      [yf.name, Gw.name], [Yw.name])
        P.add("sp", lambda e, w=w: e.dma_start(out=yd_v[:, :, w * 512:(w + 1) * 512], in_=Yw[:]), reads=[Yw.name], writes=[("ydT", w)], dma=True)
    P.release(m)


def load_hT(P, C, tag):
    hT = P.sb(tag + "_hT", [128, NCH, T], BF16)
    P.add("sp", lambda e: e.dma_start(out=hT[:], in_=C.hT_d.rearrange("(c p) t -> p c t", p=128)), writes=[hT.name], dma=True)
    return hT


def phase_gates(P, C, li):
    m = P.mark()
    hT = load_hT(P, C, "gt")
    proj_from_hT(P, C, hT, C.W[li]["w_merge_gate"], 4 * D, C.gatesT, "gt", None, func=AF.Sigmoid)
    P.release(m)


def phase_merge(P, C, li):
    m = P.mark()
    W = C.W[li]
    br = (("w_branch_a", C.yaT, 8), ("w_branch_b", C.ybT, 4), ("w_branch_c", C.ycT, 8), ("w_branch_d", C.ydT, 8))
    ysb = [P.sb("mg_y%d" % k, [128, kc, 512], BF16) for k, (_, _, kc) in enumerate(br)]
    wsb = [[P.sb("mg_w%d_%d" % (k, i), [128, kc, 256], BF16) for k, (_, _, kc) in enumerate(br)] for i in range(2)]
    gts = [P.sb("mg_g%d" % i, [128, 512], F32) for i in range(4)]
    tmps = [P.sb("mg_t%d" % i, [128, 512], F32) for i in range(3)]
    macc = [P.sb("mg_acc%d" % i, [128, 512], F32) for i in range(2)]
    outs = [P.sb("mg_o%d" % i, [128, 512], BF16) for i in range(2)]
    pss = [P.ps("mg_ps%d" % i, [128, 512], F32) for i in range(6)]
    kp = 0
    kg = 0
    kt = 0
    for g in range(4):
        tsl = slice(g * 512, g * 512 + 512)
        for k, (_, ysrc, kc) in enumerate(br):
            P.add("sp", lambda e, k=k, ysrc=ysrc, tsl=tsl: e.dma_start(out=ysb[k][:], in_=ysrc.rearrange("(c p) t -> p c t", p=128)[:, :, tsl]),
                  writes=[ysb[k].name], dma=True)
        for nb in range(8):
            wset = wsb[nb % 2]
            for k, (wn, _, kc) in enumerate(br):
                P.add("pool", lambda e, k=k, wn=wn, wset=wset, nb=nb: e.dma_start(out=wset[k][:], in_=W[wn].rearrange("(c p) n -> p c n", p=128)[:, :, nb * 256:(nb + 1) * 256]),
                      writes=[wset[k].name], dma=True)
            for j in range(2):
                n = nb * 2 + j
                acc = macc[n % 2]
                for k, (_, _, kc) in enumerate(br):
                    pt = pss[kp % 6]
                    kp += 1
                    for c in range(kc):
                        P.add("pe", lambda e, pt=pt, k=k, c=c, j=j, wset=wset, kc=kc: e.matmul(pt[:], wset[k][:, c, j * 128:(j + 1) * 128], ysb[k][:, c, :],
                                                                                         start=(c == 0), stop=(c == kc - 1)),
                              reads=[wset[k].name, ysb[k].name], writes=[pt.name])
                    gt = gts[kg % 4]
                    kg += 1
                    P.add("sp", lambda e, gt=gt, k=k, n=n, tsl=tsl: e.dma_start(out=gt[:], in_=C.gatesT[k * D + n * 128:k * D + n * 128 + 128, tsl]),
                          writes=[gt.name], dma=True)
                    if k == 0:
                        V(P, lambda e, acc=acc, pt=pt, gt=gt: e.tensor_tensor(acc[:], pt[:], gt[:], ALU.mult), [pt.name, gt.name], [acc.name])
                    else:
                        tm = tmps[kt % 3]
                        kt += 1
                        V(P, lambda e, tm=tm, pt=pt, gt=gt: e.tensor_tensor(tm[:], pt[:], gt[:], ALU.mult), [pt.name, gt.name], [tm.name])
                        P.add("pool", lambda e, acc=acc, tm=tm: e.tensor_tensor(acc[:], acc[:], tm[:], ALU.add), reads=[acc.name, tm.name], writes=[acc.name])
                ob = outs[n % 2]
                A(P, ob[:], acc[:], AF.Copy, [acc.name], [ob.name])
                P.add("sp", lambda e, ob=ob, n=n, tsl=tsl: e.dma_start(out=C.mergedT[n * 128:(n + 1) * 128, tsl], in_=ob[:]), reads=[ob.name],
                      writes=[("mergedT", n, g)], dma=True)
    P.release(m)


def proj_norm_residual(P, C, aT, KC, w, gcol, x_in, x_out, tag, gate=None, krows=None):
    m = P.mark()
    a_sb = P.sb(tag + "_a", [128, KC, 512], BF16)
    wsb = [P.sb("%s_w%d" % (tag, i), [128, KC, 256], BF16) for i in range(2)]
    mo = P.sb(tag + "_mo", [128, NCH, 512], F32)
    sqs = [P.sb("%s_sq%d" % (tag, i), [128, 512], F32) for i in range(2)]
    xs = [P.sb("%s_x%d" % (tag, i), [128, 512], F32) for i in range(3)]
    gts = [P.sb("%s_g%d" % (tag, i), [128, 512], F32) for i in range(2)] if gate is not None else None
    rstd = P.sb(tag + "_rstd", [128, 512], F32)
    pss = [P.ps("%s_ps%d" % (tag, i), [128, 512], F32) for i in range(4)]
    ss = P.ps(tag + "_ss", [128, 512], F32)
    av = aT.rearrange("(c p) t -> p c t", p=128)
    wv = w.rearrange("(c p) n -> p c n", p=128)
    kp = 0
    for g in range(4):
        tsl = slice(g * 512, g * 512 + 512)
        P.add("pool", lambda e, tsl=tsl: e.dma_start(out=a_sb[:], in_=av[:, :, tsl]), writes=[a_sb.name], dma=True)
        for nb in range(8):
            wt = wsb[nb % 2]
            P.add("pool", lambda e, wt=wt, nb=nb: e.dma_start(out=wt[:], in_=wv[:, :, nb * 256:(nb + 1) * 256]), writes=[wt.name], dma=True)
            for j in range(2):
                n = nb * 2 + j
                pt = pss[kp % 4]
                kp += 1
                for c in range(KC):
                    P.add("pe", lambda e, pt=pt, wt=wt, c=c, j=j: e.matmul(pt[:], wt[:, c, j * 128:(j + 1) * 128], a_sb[:, c, :], start=(c == 0), stop=(c == KC - 1)),
                          reads=[wt.name, a_sb.name], writes=[pt.name])
                if gate is not None:
                    gt = gts[n % 2]
                    P.add("sp", lambda e, gt=gt, n=n, tsl=tsl: e.dma_start(out=gt[:], in_=gate[n * 128:(n + 1) * 128, tsl]), writes=[gt.name], dma=True)
                    V(P, lambda e, pt=pt, gt=gt, n=n: e.tensor_tensor(mo[:, n, :], pt[:], gt[:], ALU.mult), [pt.name, gt.name], [mo.name])
                elif n % 2 == 0:
                    V(P, lambda e, pt=pt, n=n: e.tensor_copy(mo[:, n, :], pt[:]), [pt.name], [mo.name])
                else:
                    P.add("act", lambda e, pt=pt, n=n: e.copy(mo[:, n, :], pt[:]), reads=[pt.name], writes=[mo.name])
                sq = sqs[n % 2]
                A(P, sq[:], mo[:, n, :], AF.Square, [mo.name], [sq.name])
                P.add("pe", lambda e, sq=sq, n=n: e.matmul(ss[:], C.ones_invD, sq[:], start=(n == 0), stop=(n == NCH - 1)),
                      reads=[sq.name, "const"], writes=[ss.name])
        A(P, rstd[:], ss[:], AF.Sqrt, [ss.name, "const"], [rstd.name], bias=C.eps_col, scale=1.0)
        V(P, lambda e: e.reciprocal(rstd[:], rstd[:]), [rstd.name], [rstd.name])
        for n in range(NCH):
            xt = xs[n % 3]
            P.add("sp", lambda e, xt=xt, n=n, tsl=tsl: e.dma_start(out=xt[:], in_=x_in[n * 128:(n + 1) * 128, tsl]), writes=[xt.name], dma=True)
            V(P, lambda e, n=n: e.scalar_tensor_tensor(mo[:, n, :], mo[:, n, :], gcol[:, n:n + 1], rstd[:], ALU.mult, ALU.mult),
              [mo.name, rstd.name, "params"], [mo.name])
            P.add("pool", lambda e, xt=xt, n=n: e.tensor_tensor(xt[:], xt[:], mo[:, n, :], ALU.add), reads=[xt.name, mo.name], writes=[xt.name])
            P.add("sp", lambda e, xt=xt, n=n, tsl=tsl: e.dma_start(out=x_out[n * 128:(n + 1) * 128, tsl], in_=xt[:]), reads=[xt.name],
                  writes=[(tag, "xo", n, g)], dma=True)
    P.release(m)


def phase_mixout(P, C, li):
    proj_norm_residual(P, C, C.mergedT, NCH, C.W[li]["w_out"], C.pcol("norm_mix_post"), C.x0, C.x1, "mo")


def phase_ffn_up(P, C, li):
    m = P.mark()
    hT = P.sb("fu_hT", [128, NCH, T], BF16)
    rmsnorm_hT(P, C, C.x1, C.pcol("norm_ffn_pre"), hT, "n2")
    wv = C.W[li]["w_ffn_up"].rearrange("(c p) n -> p c n", p=128)
    wsb = [[P.sb("fu_w%d_%d" % (i, s_), [128, NCH, 256], BF16) for s_ in range(2)] for i in range(2)]
    raw = [[P.sb("fu_raw%d_%d" % (i, s_), [128, T], F32) for s_ in range(2)] for i in range(2)]
    cv = [P.sb("fu_cv%d" % s_, [128, T], F32) for s_ in range(2)]
    t1 = P.sb("fu_t1", [128, T], F32)
    t2 = P.sb("fu_t2", [128, T], F32)
    ao = [P.sb("fu_ao%d" % i, [128, T], BF16) for i in range(2)]
    pss = [P.ps("fu_ps%d" % i, [128, 512], F32) for i in range(6)]
    cw = [C.pcol("ffn_conv_w%d" % k) for k in range(3)]
    cb = C.pcol("ffn_conv_b")
    kp = 0
    NP = D_FF // 128
    for nb in range(NP // 2):
        wt = wsb[nb % 2]
        for s_ in range(2):
            col0 = s_ * D_FF + nb * 256
            P.add("pool", lambda e, wt=wt, s_=s_, col0=col0: e.dma_start(out=wt[s_][:], in_=wv[:, :, col0:col0 + 256]), writes=[wt[s_].name], dma=True)
        for j in range(2):
            n = nb * 2 + j
            rw = raw[n % 2]
            for s_ in range(2):
                for g in range(4):
                    tsl = slice(g * 512, g * 512 + 512)
                    pt = pss[kp % 6]
                    kp += 1
                    for c in range(NCH):
                        P.add("pe", lambda e, pt=pt, wt=wt, s_=s_, c=c, j=j, tsl=tsl: e.matmul(pt[:], wt[s_][:, c, j * 128:(j + 1) * 128], hT[:, c, tsl],
                                                                                          start=(c == 0), stop=(c == NCH - 1)),
                              reads=[wt[s_].name, hT.name], writes=[pt.name])
                    if kp % 2 == 0:
                        V(P, lambda e, pt=pt, rw=rw, s_=s_, tsl=tsl: e.tensor_copy(rw[s_][:, tsl], pt[:]), [pt.name], [rw[s_].name])
                    else:
                        P.add("act", lambda e, pt=pt, rw=rw, s_=s_, tsl=tsl: e.copy(rw[s_][:, tsl], pt[:]), reads=[pt.name], writes=[rw[s_].name])
                ch = s_ * NP + n
                x_ = rw[s_]
                o_ = cv[s_]
                eng = "dve"
                P.add(eng, lambda e, x_=x_, o_=o_, ch=ch: e.tensor_scalar(o_[:], x_[:], cw[2][:, ch:ch + 1], cb[:, ch:ch + 1], ALU.mult, ALU.add),
                      reads=[x_.name, "params"], writes=[o_.name])
                for kk_, sh in ((1, 1), (0, 2)):
                    P.add(eng, lambda e, x_=x_, o_=o_, ch=ch, kk_=kk_, sh=sh: e.scalar_tensor_tensor(o_[:, sh:], x_[:, :T - sh], cw[kk_][:, ch:ch + 1], o_[:, sh:],
                                                                                                   ALU.mult, ALU.add),
                          reads=[x_.name, o_.name, "params"], writes=[o_.name])
            ob = ao[n % 2]
            gelu_mul(P, C, cv[0][:], cv[1][:], ob[:], t1[:], t2[:], dict(g=cv[0].name, t1=t1.name, t2=t2.name, o=cv[1].name, out=ob.name))
            P.add("sp", lambda e, ob=ob, n=n: e.dma_start(out=C.actT[n * 128:(n + 1) * 128, :], in_=ob[:]), reads=[ob.name], writes=[("actT", n)], dma=True)
    P.release(m)


def phase_ffn_down(P, C, li):
    proj_norm_residual(P, C, C.actT, D_FF // 128, C.W[li]["w_ffn_down"], C.pcol("norm_ffn_post"), C.x1, C.x2, "fd")


def phase_ple(P, C, li):
    m = P.mark()
    hT = P.sb("pl_hT", [128, NCH, T], BF16)
    rmsnorm_hT(P, C, C.x2, C.pcol("norm_ple_pre"), hT, "n3")
    proj_from_hT(P, C, hT, C.W[li]["w_ple_gate"], D, C.gatesT[0:D, :], "pg", None, func=AF.Sigmoid)
    P.release(m)
    proj_norm_residual(P, C, C.pT_in[li], 2, C.W[li]["w_ple"], C.pcol("norm_ple_post"), C.x2, C.x3, "pl", gate=C.gatesT[0:D, :])


SCRATCH = {
    "projA": ([2048, T], F32),
    "projB": ([3072, T], F32),
    "projC": ([2048, T], F32),
    "projD": ([RW_IN, T], F32),
    "vtokB": ([T, 1536], BF16),
    "vtokC": ([T, 1024], BF16),
    "hT_d": ([D, T], BF16),
    "yaT": ([LRU_W, T], BF16),
    "ybT": ([512, T], BF16),
    "ycT": ([SB_W, T], BF16),
    "ydT": ([RW_W, T], BF16),
    "vfirstT": ([RW_W, T], F32),
    "gatesT": ([4 * D, T], F32),
    "mergedT": ([D, T], BF16),
    "rw_ar": ([RW_W, 32 * 128], BF16),
    "rw_kb": ([RW_W, 32 * 128], BF16),
    "rw_v": ([RW_W, T], BF16),
    "rw_wc": ([RW_W, 32], F32),
    "rw_g": ([RW_W, T], F32),
    "rw_bonus": ([RW_W, T], F32),
    "xT_a": ([D, T], F32),
    "xT_b": ([D, T], F32),
    "actT": ([D_FF, T], BF16),
}

WEIGHTS = {
    "w_in": [D, N_IN], "w_merge_gate": [D, 4 * D], "lru_w_r": [LRU_W, 128], "lru_w_i": [LRU_W, 128],
    "rwkv_w_up": [64, RW_W], "rwkv_a_up": [64, RW_W], "rwkv_g_up": [160, RW_W],
    "rwkv_v_down": [RW_W, 32], "rwkv_v_up": [32, RW_W],
    "w_branch_a": [LRU_W, D], "w_branch_b": [512, D], "w_branch_c": [SB_W, D], "w_branch_d": [RW_W, D],
    "w_out": [D, D], "w_ffn_up": [D, 2 * D_FF], "w_ffn_down": [D_FF, D], "w_ple": [PLE, D], "w_ple_gate": [D, D],
}


def build_program(cfg):
    nc = bass.Bass("TRN2", target_bir_lowering=False)
    P = Prog(nc)
    C = Ctx()
    C.cfg = cfg
    dbg = cfg.get("debug_out", ())
    ext_in = cfg.get("ext_in", ())
    layers = cfg["layers"]
    phases = cfg.get("phases", None)

    def din(name, shape, dt=F32):
        return nc.dram_tensor(name, list(shape), dt, kind="ExternalInput").ap()

    NBc = cfg.get("nb", 1)
    sfx = lambda s_: "" if NBc == 1 else "_b%d" % s_
    xT_ins = [din("xT_in" + sfx(s_), [D, T]) for s_ in range(NBc)]
    pT_ins = [[din("pT%d%s" % (L, sfx(s_)), [PLE, T]) for L in layers] for s_ in range(NBc)]
    out_Ts = [nc.dram_tensor("out_T" + sfx(s_), [D, T], F32, kind="ExternalOutput").ap() for s_ in range(NBc)]
    C.xT_in, C.pT_in, C.out_T = xT_ins[0], pT_ins[0], out_Ts[0]
    need_w = cfg.get("weights", None)
    C.W = []
    for L in layers:
        d = {}
        for nm, shp in WEIGHTS.items():
            if need_w is None or nm in need_w:
                d[nm] = din("%s_%d" % (nm, L), shp)
        C.W.append(d)
    C.npar = cfg["npar"]
    C.params_d = [din("params%d" % L, [128, C.npar]) for L in layers]
    C.consts_d = din("consts", [128, NCONST])
    for nm, (shp, dt) in SCRATCH.items():
        kind = "ExternalOutput" if nm in dbg else ("ExternalInput" if nm in ext_in else "Internal")
        setattr(C, nm, nc.dram_tensor(nm, list(shp), dt, kind=kind).ap())

    C.params = P.sb("params", [128, C.npar], F32)
    C.consts = P.sb("consts_sb", [128, NCONST], F32)
    C.ones_invD = C.consts[:, 0:128]
    C.ident = C.consts[:, 128:256]
    C.eps_col = C.consts[:, 256:257]
    C.one_col = C.consts[:, 257:258]
    C.rmat = C.consts[:, 512:640]
    C.dilmask = C.consts[:, 640:896]
    C.sbmask = C.consts[:, 896:1024]
    C.ones1 = C.consts[:, 1024:1152]
    C.blk64 = C.consts[:, 1152:1280]
    C.rope_d = din("rope", [128, 3, T])
    C.poff = cfg["poff"]
    C.pcol = lambda nm: C.params[:, C.poff[nm][0]:C.poff[nm][0] + C.poff[nm][1]]
    P.add("sp", lambda e: e.dma_start(out=C.consts[:], in_=C.consts_d), writes=["const"], dma=True)

    bufs = [C.xT_a, C.xT_b]
    for s_ in range(NBc):
      C.xT_in, C.pT_in, C.out_T = xT_ins[s_], pT_ins[s_], out_Ts[s_]
      xcur = C.xT_in
      for li, L in enumerate(layers):
          P.add("sp", lambda e, li=li: e.dma_start(out=C.params[:], in_=C.params_d[li]), writes=["params"], dma=True)
          last = (li == len(layers) - 1)
          C.xT = xcur
          C.x0 = xcur
          C.x1 = bufs[li % 2]
          C.x2 = bufs[(li + 1) % 2]
          C.x3 = C.out_T if last else bufs[li % 2]
          if "x_override" in cfg:
              for k_, v_ in cfg["x_override"].items():
                  setattr(C, k_, getattr(C, v_))
          C.L = L
          C.li = li
          for ph in PHASES:
              if phases is None or ph.__name__ in phases:
                  ph(P, C, li)
          P.barrier()
          xcur = C.x3
    P.finish([])
    P.emit()
    return nc, P


NCONST = 2048


def layer_weights(inp, L):
    f = lambda a: np.ascontiguousarray(np.asarray(a, np.float32))
    d = {}
    for nm in ("w_in", "w_merge_gate", "rwkv_w_up", "rwkv_a_up", "rwkv_g_up", "w_branch_a", "w_branch_b", "w_branch_c",
               "w_branch_d", "w_out", "w_ffn_up", "w_ffn_down", "w_ple", "w_ple_gate"):
        d[nm] = f(inp[nm][L])
    d["lru_w_r"] = f(inp["lru_w_r"][L].reshape(LRU_W, 128))
    d["lru_w_i"] = f(inp["lru_w_i"][L].reshape(LRU_W, 128))
    if L > 0:
        d["rwkv_v_down"] = f(inp["rwkv_v_down"][L - 1])
        d["rwkv_v_up"] = f(inp["rwkv_v_up"][L - 1])
    else:
        d["rwkv_v_down"] = np.zeros((RW_W, 32), np.float32)
        d["rwkv_v_up"] = np.zeros((32, RW_W), np.float32)
    return d


def make_consts():
    c = np.zeros((128, NCONST), np.float32)
    c[:, 0:128] = 1.0 / D
    c[:, 128:256] = np.eye(128, dtype=np.float32)
    c[:, 256] = EPS
    c[:, 257] = 1.0
    j = np.arange(128)
    rm = np.zeros((128, 128), np.float32)
    rm[j[:64] + 64, j[:64]] = -1.0
    rm[j[64:] - 64, j[64:]] = 1.0
    c[:, 512:640] = rm
    c[:, 640:768] = (j[:, None] >= j[None, :])
    c[:, 768:896] = (j[None, :] >= j[:, None])
    c[:, 896:1024] = (j[None, :] < j[:, None])
    c[:, 1024:1152] = 1.0
    c[:, 258] = GN_EPS
    c[0:64, 1152:1216] = 1.0
    c[64:128, 1216:1280] = 1.0
    j64 = np.arange(64)
    strict = (j64[:, None] < j64[None, :]).astype(np.float32)
    incl = (j64[:, None] <= j64[None, :]).astype(np.float32)
    for k in range(4):
        c[0:64, 1280 + k * 128:1280 + k * 128 + 64] = strict
        c[0:64, 1280 + k * 128 + 64:1280 + k * 128 + 128] = incl
        c[0:64, 1792 + k * 64:1792 + (k + 1) * 64] = strict.T
    return c


def make_rope():
    half = 64
    inv = (10000.0 ** (-np.arange(half, dtype=np.float32) / half)).astype(np.float32)
    ang = np.arange(T, dtype=np.float32)[None, :] * inv[:, None]
    r = np.zeros((128, 3, T), np.float32)
    r[:, 2, :] = 1.0
    r[:, 2, 0::64] = 0.0
    r[:64, 0] = np.cos(ang); r[64:, 0] = np.cos(ang)
    r[:64, 1] = np.sin(ang); r[64:, 1] = np.sin(ang)
    return r


PHASES = [phase_inproj, phase_rglru, phase_dilated, phase_stickbreak, phase_rwkv_prep, phase_rwkv_chunks,
          phase_gates, phase_merge, phase_mixout, phase_ffn_up, phase_ffn_down, phase_ple]


NCORES = 4
NB = 2


def kernel(**inputs):
    inp = {k: np.asarray(v) for k, v in inputs.items()}
    B = inp["x"].shape[0]
    assert B == NCORES * NB
    pks = [pack_params(inp, L) for L in range(DEPTH)]
    cfg = dict(layers=list(range(DEPTH)), npar=pks[0].n, poff=pks[0].off, nb=NB)
    nc, _ = build_program(cfg)
    consts = make_consts()
    rope = make_rope()
    lws = [layer_weights(inp, L) for L in range(DEPTH)]
    sfx = lambda s_: "" if NB == 1 else "_b%d" % s_
    in_maps = []
    for c in range(NCORES):
        im = {"consts": consts, "rope": rope}
        for s_ in range(NB):
            b = c * NB + s_
            im["xT_in" + sfx(s_)] = np.ascontiguousarray(inp["x"][b].T)
            for L in range(DEPTH):
                im["pT%d%s" % (L, sfx(s_))] = np.ascontiguousarray(inp["p"][L, b].T)
        for L in range(DEPTH):
            im["params%d" % L] = pks[L].array()
            for nm, a in lws[L].items():
                im["%s_%d" % (nm, L)] = a
        in_maps.append(im)
    res = run_bass_kernel_spmd(nc, in_maps, core_ids=list(range(NCORES)))
    outs = []
    for c in range(NCORES):
        for s_ in range(NB):
            outs.append(np.ascontiguousarray(np.asarray(res.results[c]["out_T" + sfx(s_)]).T))
    return np.stack(outs, axis=0).astype(np.float32)
```
